# Optimizing a Trainium2 kernel written in Bass

```python
import math
import jax, jax.numpy as jnp
from jax import lax
import numpy as np

D_MODEL = 1024
BATCH = 4
SEQ = 8192
DEPTH = 4

D_FF = 2816
NORM_EPS = 1e-6
CHUNK = 64
LOG_FLOOR = 1e-20

A_HEADS = 4
A_DK = 128
A_DV = 64
B_HEADS = 4
B_DK = 128
B_DV = 128
CONV_K = 4
C_HEADS = 4
C_DK = 64
C_DV = 64
ROPE_BASE = 10000.0

MIX_WIDTH = A_HEADS * A_DV + B_HEADS * B_DV + C_HEADS * C_DV
IN_SPLITS = (
    A_HEADS * A_DK, A_HEADS * A_DK, A_HEADS * A_DV, A_HEADS * A_DV,
    B_HEADS * B_DK, B_HEADS * B_DK, B_HEADS * B_DV, B_HEADS * B_DV,
    B_HEADS, B_HEADS,
    C_HEADS * C_DK, C_HEADS * C_DK, C_HEADS * C_DV, C_HEADS * C_DV,
)
D_IN = int(sum(IN_SPLITS))
SPLIT_IDX = tuple(int(v) for v in np.cumsum(IN_SPLITS)[:-1])
GDN_CONV_CH = B_HEADS * (2 * B_DK + B_DV)

kernel_name = "hybrid_hgrn2_gdn_retention_macaron"


def rms_norm(x, w):
    xf = x.astype(jnp.float32)
    y = xf * lax.rsqrt(jnp.mean(xf * xf, axis=-1, keepdims=True) + NORM_EPS)
    return (y * w.astype(jnp.float32)).astype(x.dtype)


def head_rms(o):
    return o * lax.rsqrt(jnp.mean(o * o, axis=-1, keepdims=True) + NORM_EPS)


def masked_exp(d, mask):
    return jnp.where(mask, jnp.exp(jnp.where(mask, d, 0.0)), 0.0)


def swiglu_ffn(h, wg, wu, wd):
    return (jax.nn.silu(h @ wg) * (h @ wu)) @ wd


def split_heads(t, n_heads):
    return t.reshape(t.shape[:-1] + (n_heads, t.shape[-1] // n_heads))


def to_chunks(t):
    b, s, h, d = t.shape
    return t.reshape(b, s // CHUNK, CHUNK, h, d).transpose(0, 3, 1, 2, 4)


def from_chunks(t):
    b, h, n, c, d = t.shape
    return t.transpose(0, 2, 3, 1, 4).reshape(b, n * c, h, d)


def hgrn2_lower_bounds(p):
    s = jax.nn.softmax(p.astype(jnp.float32), axis=0)
    return jnp.cumsum(s, axis=0) - s[0:1]


def hgrn2_mixer(q, f_logit, i_in, gate, lb, norm_w):
    f32 = jnp.float32
    q = jax.nn.silu(split_heads(q, A_HEADS).astype(f32))
    z = split_heads(f_logit, A_HEADS).astype(f32)
    v = split_heads(i_in, A_HEADS).astype(f32)
    lb = lb.reshape(A_HEADS, A_DK)
    f = lb + (1.0 - lb) * jax.nn.sigmoid(z)
    log_f = jnp.log(jnp.maximum(f, LOG_FLOOR))
    k = (1.0 - lb) * jax.nn.sigmoid(-z)
    qc, kc, vc, lfc = to_chunks(q), to_chunks(k), to_chunks(v), to_chunks(log_f)
    b = jnp.cumsum(lfc, axis=3)
    b_last = b[:, :, :, -1:, :]
    q_in = qc * jnp.exp(b)
    k_st = kc * jnp.exp(b_last - b)
    decay_st = jnp.exp(b_last[:, :, :, 0, :])
    causal = jnp.tril(jnp.ones((CHUNK, CHUNK), dtype=bool))[:, :, None]

    def step(S, xs):
        q_c, k_c, v_c, b_c, qi, ks, dl = xs
        diff = b_c[:, :, :, None, :] - b_c[:, :, None, :, :]
        dec = masked_exp(diff, causal)
        attn = jnp.einsum('bhik,bhijk,bhjk->bhij', q_c, dec, k_c)
        o = jnp.einsum('bhik,bhkv->bhiv', qi, S) + jnp.einsum('bhij,bhjv->bhiv', attn, v_c)
        S = dl[..., None] * S + jnp.einsum('bhjk,bhjv->bhkv', ks, v_c)
        return S, o

    xs = tuple(jnp.moveaxis(t, 2, 0) for t in (qc, kc, vc, b, q_in, k_st, decay_st))
    S0 = jnp.zeros((q.shape[0], A_HEADS, A_DK, A_DV), f32)
    _, o = lax.scan(step, S0, xs)
    o = from_chunks(jnp.moveaxis(o, 0, 2))
    g = split_heads(gate, A_HEADS).astype(f32)
    o = head_rms(o) * norm_w.astype(f32) * jax.nn.silu(g)
    return o.reshape(o.shape[0], o.shape[1], A_HEADS * A_DV)


def causal_depthwise_conv(x, w):
    ch = x.shape[-1]
    return lax.conv_general_dilated(
        x, w[:, None, :].astype(x.dtype), window_strides=(1,), padding=[(CONV_K - 1, 0)],
        dimension_numbers=('NWC', 'WIO', 'NWC'), feature_group_count=ch)


def gated_deltanet_mixer(q, k, v, z, beta_logit, a_logit, conv_w, a_log, dt_bias, norm_w):
    f32 = jnp.float32
    qkv = jax.nn.silu(causal_depthwise_conv(jnp.concatenate([q, k, v], axis=-1), conv_w)).astype(f32)
    q, k, v = jnp.split(qkv, [B_HEADS * B_DK, 2 * B_HEADS * B_DK], axis=-1)
    q, k, v = split_heads(q, B_HEADS), split_heads(k, B_HEADS), split_heads(v, B_HEADS)
    q = q * lax.rsqrt(jnp.sum(q * q, -1, keepdims=True) + NORM_EPS) * (B_DK ** -0.5)
    k = k * lax.rsqrt(jnp.sum(k * k, -1, keepdims=True) + NORM_EPS)
    beta = jax.nn.sigmoid(beta_logit.astype(f32))
    g = -jnp.exp(a_log.astype(f32)) * jax.nn.softplus(a_logit.astype(f32) + dt_bias.astype(f32))
    qc, kc, vc = to_chunks(q), to_chunks(k), to_chunks(v)
    bc = to_chunks(beta[..., None])[..., 0]
    gc = jnp.cumsum(to_chunks(g[..., None])[..., 0], axis=-1)
    idx = jnp.arange(CHUNK)
    incl = idx[:, None] >= idx[None, :]
    strict = idx[:, None] > idx[None, :]
    diff = gc[..., :, None] - gc[..., None, :]
    m = bc[..., :, None] * jnp.einsum('bhnik,bhnjk->bhnij', kc, kc) * masked_exp(diff, strict)
    a_mat = m + jnp.eye(CHUNK, dtype=f32)
    rhs = jnp.concatenate([vc * bc[..., None], kc * (bc * jnp.exp(gc))[..., None]], axis=-1)
    sol = lax.linalg.triangular_solve(a_mat, rhs, left_side=True, lower=True, unit_diagonal=True)
    u, w = sol[..., :B_DV], sol[..., B_DV:]
    attn = jnp.einsum('bhnik,bhnjk->bhnij', qc, kc) * masked_exp(diff, incl)
    q_in = qc * jnp.exp(gc)[..., None]
    k_st = kc * jnp.exp(gc[..., -1:] - gc)[..., None]
    decay_st = jnp.exp(gc[..., -1])

    def step(S, xs):
        u_c, w_c, attn_c, qi, ks, dl = xs
        v_new = u_c - jnp.einsum('bhck,bhkv->bhcv', w_c, S)
        o = jnp.einsum('bhik,bhkv->bhiv', qi, S) + jnp.einsum('bhij,bhjv->bhiv', attn_c, v_new)
        S = dl[..., None, None] * S + jnp.einsum('bhjk,bhjv->bhkv', ks, v_new)
        return S, o

    xs = tuple(jnp.moveaxis(t, 2, 0) for t in (u, w, attn, q_in, k_st, decay_st))
    S0 = jnp.zeros((q.shape[0], B_HEADS, B_DK, B_DV), f32)
    _, o = lax.scan(step, S0, xs)
    o = from_chunks(jnp.moveaxis(o, 0, 2))
    zg = split_heads(z, B_HEADS).astype(f32)
    o = head_rms(o) * norm_w.astype(f32) * jax.nn.silu(zg)
    return o.reshape(o.shape[0], o.shape[1], B_HEADS * B_DV)


def rope(x, positions):
    half = x.shape[-1] // 2
    inv = ROPE_BASE ** (-jnp.arange(half, dtype=jnp.float32) / half)
    ang = positions.astype(jnp.float32)[..., None] * inv
    cos, sin = jnp.cos(ang)[:, :, None, :], jnp.sin(ang)[:, :, None, :]
    x1, x2 = x[..., :half], x[..., half:]
    return jnp.concatenate([x1 * cos - x2 * sin, x2 * cos + x1 * sin], axis=-1)


def retention_mixer(q, k, v, gate, positions):
    f32 = jnp.float32
    q = rope(split_heads(q, C_HEADS).astype(f32), positions)
    k = rope(split_heads(k, C_HEADS).astype(f32), positions) * (C_DK ** -0.5)
    v = split_heads(v, C_HEADS).astype(f32)
    log_gamma = jnp.log1p(-jnp.exp2(-5.0 - jnp.arange(C_HEADS, dtype=f32)))
    qc, kc, vc = to_chunks(q), to_chunks(k), to_chunks(v)
    idx = jnp.arange(CHUNK, dtype=f32)
    rel = idx[:, None] - idx[None, :]
    dmat = masked_exp(log_gamma[:, None, None] * rel, rel >= 0)
    scores = jnp.einsum('bhnik,bhnjk->bhnij', qc, kc) * dmat[:, None]
    o_intra = jnp.einsum('bhnij,bhnjv->bhniv', scores, vc)
    k_st = kc * jnp.exp(log_gamma[:, None] * (CHUNK - 1 - idx))[None, :, None, :, None]
    chunk_states = jnp.einsum('bhnjk,bhnjv->bhnkv', k_st, vc)
    decay_chunk = jnp.exp(log_gamma * CHUNK)[None, :, None, None]

    def step(R, s):
        return decay_chunk * R + s, R

    R0 = jnp.zeros((q.shape[0], C_HEADS, C_DK, C_DV), f32)
    _, r_prev = lax.scan(step, R0, jnp.moveaxis(chunk_states, 2, 0))
    r_prev = jnp.moveaxis(r_prev, 0, 2)
    q_in = qc * jnp.exp(log_gamma[:, None] * (idx + 1.0))[None, :, None, :, None]
    o = o_intra + jnp.einsum('bhnik,bhnkv->bhniv', q_in, r_prev)
    o = from_chunks(o)
    g = split_heads(gate, C_HEADS).astype(f32)
    o = head_rms(o) * jax.nn.silu(g)
    return o.reshape(o.shape[0], o.shape[1], C_HEADS * C_DV)


def setup_inputs(seed: int = 0) -> dict:
    key = jax.random.key(seed)
    ks = jax.random.split(key, 24)
    f32 = jnp.float32

    def nrm(k, shape, fan_in):
        return jax.random.normal(k, shape, f32) * (fan_in ** -0.5)

    def gain(k, shape):
        return 1.0 + 0.01 * jax.random.normal(k, shape, f32)

    x = jax.random.normal(ks[0], (BATCH, SEQ, D_MODEL), f32)
    offset = jax.random.randint(ks[1], (BATCH, 1), 0, 4096, dtype=jnp.int32)
    positions = jnp.arange(SEQ, dtype=jnp.int32)[None, :] + offset
    dt = jnp.exp(jax.random.uniform(ks[2], (DEPTH, B_HEADS), f32) * (math.log(0.1) - math.log(0.001)) + math.log(0.001))
    return {
        "x": x,
        "positions": positions,
        "ffn1_norm": gain(ks[3], (DEPTH, D_MODEL)),
        "ffn1_w_gate": nrm(ks[4], (DEPTH, D_MODEL, D_FF), D_MODEL),
        "ffn1_w_up": nrm(ks[5], (DEPTH, D_MODEL, D_FF), D_MODEL),
        "ffn1_w_down": nrm(ks[6], (DEPTH, D_FF, D_MODEL), D_FF),
        "mix_norm": gain(ks[7], (DEPTH, D_MODEL)),
        "w_in": nrm(ks[8], (DEPTH, D_MODEL, D_IN), D_MODEL),
        "hgrn_lower_bounds": 0.1 * jax.random.normal(ks[9], (DEPTH, A_HEADS * A_DK), f32),
        "hgrn_norm": gain(ks[10], (DEPTH, A_DV)),
        "gdn_conv": nrm(ks[11], (DEPTH, CONV_K, GDN_CONV_CH), CONV_K),
        "gdn_a_log": jnp.log(jax.random.uniform(ks[12], (DEPTH, B_HEADS), f32, 1.0, 16.0)),
        "gdn_dt_bias": dt + jnp.log(-jnp.expm1(-dt)),
        "gdn_norm": gain(ks[13], (DEPTH, B_DV)),
        "w_out": nrm(ks[14], (DEPTH, MIX_WIDTH, D_MODEL), MIX_WIDTH),
        "ffn2_norm": gain(ks[15], (DEPTH, D_MODEL)),
        "ffn2_w_gate": nrm(ks[16], (DEPTH, D_MODEL, D_FF), D_MODEL),
        "ffn2_w_up": nrm(ks[17], (DEPTH, D_MODEL, D_FF), D_MODEL),
        "ffn2_w_down": nrm(ks[18], (DEPTH, D_FF, D_MODEL), D_FF),
        "final_norm": gain(ks[19], (D_MODEL,)),
    }


def reference(x, positions, ffn1_norm, ffn1_w_gate, ffn1_w_up, ffn1_w_down, mix_norm, w_in,
              hgrn_lower_bounds, hgrn_norm, gdn_conv, gdn_a_log, gdn_dt_bias, gdn_norm, w_out,
              ffn2_norm, ffn2_w_gate, ffn2_w_up, ffn2_w_down, final_norm):
    lbs = hgrn2_lower_bounds(hgrn_lower_bounds)
    for l in range(DEPTH):
        h = rms_norm(x, ffn1_norm[l])
        x = x + 0.5 * swiglu_ffn(h, ffn1_w_gate[l], ffn1_w_up[l], ffn1_w_down[l])
        h = rms_norm(x, mix_norm[l])
        (a_q, a_f, a_i, a_g, b_q, b_k, b_v, b_z, b_beta, b_a,
         c_q, c_k, c_v, c_g) = jnp.split(h @ w_in[l], SPLIT_IDX, axis=-1)
        o_a = hgrn2_mixer(a_q, a_f, a_i, a_g, lbs[l], hgrn_norm[l])
        o_b = gated_deltanet_mixer(b_q, b_k, b_v, b_z, b_beta, b_a, gdn_conv[l], gdn_a_log[l], gdn_dt_bias[l], gdn_norm[l])
        o_c = retention_mixer(c_q, c_k, c_v, c_g, positions)
        mixed = jnp.concatenate([o_a, o_b, o_c], axis=-1).astype(x.dtype)
        x = x + mixed @ w_out[l]
        h = rms_norm(x, ffn2_norm[l])
        x = x + 0.5 * swiglu_ffn(h, ffn2_w_gate[l], ffn2_w_up[l], ffn2_w_down[l])
    return rms_norm(x, final_norm)
```

```python
import math
import numpy as np
import concourse.bass as bass
import concourse.mybir as mybir
from concourse.bass_utils import run_bass_kernel_spmd

F32 = mybir.dt.float32
BF16 = mybir.dt.bfloat16
I32 = mybir.dt.int32
AF = mybir.ActivationFunctionType
ALU = mybir.AluOpType
AX = mybir.AxisListType

D = 1024
DFF = 2816
DIN = 4616
T = 512
C = 64
NCH = T // C
EPS = 1e-6
NKC = D // 128
NFC = DFF // 128


class Op:
    __slots__ = ("eng", "fn", "deps", "pos", "signal", "sigval", "dma", "dsem", "dval", "waits", "gi")

    def __init__(self, eng, fn, dma):
        self.eng = eng
        self.fn = fn
        self.dma = dma
        self.deps = []
        self.signal = False
        self.sigval = 0
        self.dsem = None
        self.dval = 0
        self.waits = []


ENGS = ("pe", "act", "dve", "pool", "sp")
NDMASEM = 12


class Sched:
    def __init__(self):
        self.ops = []
        self.lastw = {}
        self.readers = {}
        self.per_eng = {e: [] for e in ENGS}
        self.dma_hist = {e: [] for e in ENGS}

    def add(self, eng, fn, reads=(), writes=(), dma=False, extra=()):
        op = Op(eng, fn, dma)
        deps = set(extra)
        for k in reads:
            w = self.lastw.get(k)
            if w is not None:
                deps.add(w)
        for k in writes:
            w = self.lastw.get(k)
            if w is not None:
                deps.add(w)
            for r in self.readers.get(k, ()):
                deps.add(r)
        for k in reads:
            self.readers.setdefault(k, []).append(op)
        for k in writes:
            self.lastw[k] = op
            self.readers[k] = []
        if dma:
            h = self.dma_hist[eng]
            n = len(h)
            op.dsem = (eng, n % NDMASEM)
            op.dval = 16 * (n // NDMASEM + 1)
            if n >= NDMASEM:
                deps.add(h[n - NDMASEM])
            h.append(op)
        deps.discard(op)
        op.deps = list(deps)
        op.pos = len(self.per_eng[eng])
        op.gi = len(self.ops)
        self.per_eng[eng].append(op)
        self.ops.append(op)
        return op

    def plan(self):
        maxpos = {e: {e2: -1 for e2 in ENGS} for e in ENGS}
        maxd = {e: {} for e in ENGS}
        for op in self.ops:
            e = op.eng
            for p in sorted(op.deps, key=lambda o: o.gi):
                if p.dma:
                    cur = maxd[e].get(p.dsem, 0)
                    if p.dval > cur:
                        maxd[e][p.dsem] = p.dval
                        op.waits.append(("d", p))
                else:
                    if p.eng == "pe" and e == "pe":
                        continue
                    if p.pos > maxpos[e][p.eng]:
                        maxpos[e][p.eng] = p.pos
                        p.signal = True
                        op.waits.append(("c", p))
        for e in ENGS:
            n = 0
            for op in self.per_eng[e]:
                if op.signal and not op.dma:
                    n += 1
                    op.sigval = n

    def emit(self, nc, engsem, dmasem, block):
        sched = self

        def run(engname, eng):
            for op in sched.per_eng[engname]:
                for kind, p in op.waits:
                    if kind == "d":
                        eng.wait_ge(dmasem[p.dsem], p.dval)
                    else:
                        eng.wait_ge(engsem[p.eng], p.sigval)
                if op.fn is None:
                    continue
                ins = op.fn(eng)
                if op.dma:
                    ins.then_inc(dmasem[op.dsem], 16)
                elif op.signal:
                    ins.then_inc(engsem[engname], 1)

        @block.tensor
        def _(pe):
            run("pe", pe)

        @block.scalar
        def _(act):
            run("act", act)

        @block.vector
        def _(dve):
            run("dve", dve)

        @block.gpsimd
        def _(pool):
            run("pool", pool)

        @block.sync
        def _(sp):
            run("sp", sp)


WNAMES = ("ffn1_w_gate", "ffn1_w_up", "ffn1_w_down", "w_in", "w_out",
          "ffn2_w_gate", "ffn2_w_up", "ffn2_w_down")
WSHAPES = {"ffn1_w_gate": (D, DFF), "ffn1_w_up": (D, DFF), "ffn1_w_down": (DFF, D),
           "w_in": (D, DIN), "w_out": (D, D),
           "ffn2_w_gate": (D, DFF), "ffn2_w_up": (D, DFF), "ffn2_w_down": (DFF, D)}


AH, ADK, ADV = 4, 128, 64
BH, BDK, BDV = 4, 128, 128
CH_, CDK, CDV = 4, 64, 64
O_TRI, O_SU, O_STRICT, O_INCL, O_DMAT, O_GDEC, O_GQ, O_PERM, O_INVF, O_ONES = 0, 64, 128, 384, 640, 896, 1152, 1408, 1472, 1473
C64W = O_ONES + 128
C1_RR = 6.28125
C2_RR = 2.0 * math.pi - 6.28125


def build_program(NT, DEPTH, enable=("A", "B", "C")):
    import contextlib
    S = NT * T
    nc = bass.Bass("TRN2", target_bir_lowering=False)
    dram = {}

    def din(name, shape, dt=F32):
        dram[name] = nc.dram_tensor(name, list(shape), dt, kind="ExternalInput").ap()
        return dram[name]

    x_d = din("x", (S, D))
    pos_d = din("pos", (1, S), I32)
    for wn in WNAMES:
        din(wn, (DEPTH,) + WSHAPES[wn])
    NNW = DEPTH * 3 * NKC + NKC
    din("normw", (128, NNW))
    din("c128", (128, 640))
    din("c64", (64, C64W))
    din("lbp", (128, 16))
    din("cw", (128, DEPTH * 48))
    din("hnorm", (1, DEPTH * 64))
    din("gnorm", (1, DEPTH * 128))
    din("alog", (1, DEPTH * 4))
    din("dtb", (1, DEPTH * 4))
    y_d = nc.dram_tensor("y", [S, D], F32, kind="ExternalOutput").ap()
    wbf = {wn: nc.dram_tensor("bf_" + wn, [DEPTH] + list(WSHAPES[wn]), BF16, kind="Internal").ap()
           for wn in WNAMES}

    sc = Sched()
    stack = contextlib.ExitStack()

    def salloc(name, shape, dt):
        return stack.enter_context(nc.sbuf_tensor("sb_" + name, list(shape), dt))

    g64 = [float(v) for v in np.exp(np.log1p(-np.exp2(-5.0 - np.arange(4, dtype=np.float32))).astype(np.float32) * np.float32(C)).astype(np.float32)]

    with stack:
        NCELL = 44
        arena = salloc("arena", [128, NCELL * 512], F32)
        xT = salloc("xT", [128, NKC, T], F32)
        hT = salloc("hT", [128, NKC, T], BF16)
        rstd = salloc("rstd", [128, T], F32)
        rtmp = salloc("rtmp", [128, T], F32)
        normw = salloc("normw", [128, NNW], F32)
        c128 = salloc("c128", [128, 640], F32)
        c64 = salloc("c64", [64, C64W], F32)
        identb = salloc("identb", [128, 128], BF16)
        onesb = salloc("onesb", [128, 128], BF16)
        cst = salloc("cst", [128, 4], F32)
        lbp = salloc("lbp", [128, 16], F32)
        lbs = salloc("lbs", [128, 16], F32)
        lbw = salloc("lbw", [128, 16], F32)
        oml = salloc("oml", [128, 16], F32)
        noml = salloc("noml", [128, 16], F32)
        lbm = salloc("lbm", [128, 8], F32)
        cw = salloc("cw", [128, DEPTH * 48], F32)
        hnorm = salloc("hnorm", [64, DEPTH * 64], F32)
        gnorm = salloc("gnorm", [64, DEPTH * 128], F32)
        negA = salloc("negA", [64, DEPTH * 4], F32)
        dtb = salloc("dtb", [64, DEPTH * 4], F32)
        dlA = salloc("dlA", [128, 32], F32)
        dlB = salloc("dlB", [128, 32], F32)
        sm = {n: salloc("sm_" + n, [64, 32], F32) for n in
              ("beta", "xa", "ea", "sp", "g", "gc", "egc", "bg", "tmpd", "est")}
        rs4 = salloc("rs4", [64, 16], F32)
        SA = [salloc("SA%d" % l, [128, 256], F32) for l in range(DEPTH)]
        SB = [salloc("SB%d" % l, [128, 512], F32) for l in range(DEPTH)]
        RC = [salloc("RC%d" % l, [64, 256], F32) for l in range(DEPTH)]
        SAb = [salloc("SAb%d" % l, [128, 256], BF16) for l in range(DEPTH)]
        SBb = [salloc("SBb%d" % l, [128, 512], BF16) for l in range(DEPTH)]
        RCb = [salloc("RCb%d" % l, [64, 256], BF16) for l in range(DEPTH)]
        halo = [salloc("halo%d" % l, [128, 36], F32) for l in range(DEPTH)]
        NWS = 8
        wslot = [salloc("wslot%d" % i, [128, 2048], BF16) for i in range(NWS)]
        ps = [stack.enter_context(nc.psum_tensor("ps%d" % i, [128, 512], F32)) for i in range(8)]
        engsem = {e: stack.enter_context(nc.semaphore("s_" + e)) for e in ENGS}
        dmasem = {}
        for e in ("sp", "pool"):
            for i in range(NDMASEM):
                dmasem[(e, i)] = stack.enter_context(nc.semaphore("d_%s%d" % (e, i)))
        block = stack.enter_context(nc.Block())

        ident = c128[:, 0:128]
        ident64 = c128[0:64, 0:64]
        scanmask = c128[:, 128:640]
        tri = c64[:, O_TRI:O_TRI + 64]
        su = c64[:, O_SU:O_SU + 64]
        onesf = c64[:, O_ONES:O_ONES + 128]

        def c64v(off):
            return c64[:, off:off + 256].rearrange("p (h n) -> p h n", h=4)

        def A(eng, method, *args, r=(), w=(), **kw):
            return sc.add(eng, lambda e: getattr(e, method)(*args, **kw), reads=r, writes=w)

        def cells(c0, n):
            return [("a", c) for c in range(c0, c0 + n)]

        def av(c0, n, dt=F32, parts=128):
            ap = arena[0:parts, c0 * 512:(c0 + n) * 512]
            if dt == BF16:
                ap = ap.bitcast(BF16)
            return ap

        def dma(q, out, in_, reads=(), writes=()):
            return sc.add(q, lambda eng: eng.dma_start(out=out, in_=in_), reads, writes, dma=True)

        wctr = [0]

        def load_w(wn, l, k0, nk, c0, ncol):
            i = wctr[0] % NWS
            wctr[0] += 1
            view = wslot[i][:, 0:nk * ncol].rearrange("p (k n) -> p k n", k=nk)
            src = wbf[wn][l, k0 * 128:(k0 + nk) * 128, c0:c0 + ncol].rearrange("(k p) n -> p k n", p=128)
            rk = [("bf", wn, l, r0) for r0 in range((k0 * 128) // 256 * 256, (k0 + nk) * 128, 256)]
            dma("sp", view, src, reads=rk, writes=("wslot%d" % i,))
            return view, "wslot%d" % i

        psctr = [0]

        def psbank(lo=0, hi=8):
            i = lo + psctr[0] % (hi - lo)
            psctr[0] += 1
            return ps[i], "ps%d" % i

        def mm(out, lhsT, rhs, start, stop, r, w):
            return A("pe", "matmul", out, lhsT=lhsT, rhs=rhs, start=start, stop=stop, r=r, w=w)

        def tp(out, in_, idn, r, w):
            return A("pe", "transpose", out, in_, idn, r=r, w=w)

        def act(out, in_, func, r, w, **kw):
            return A("act", "activation", out=out, in_=in_, func=func, r=r, w=w, **kw)

        hid = av(0, 11, BF16).rearrange("p (f t) -> p f t", f=NFC)
        sqv = av(11, 4, BF16).rearrange("p (c t) -> p c t", c=NKC)
        sgv = [av(15, 1), av(16, 1)]
        xin = [av(17, 2), av(19, 2)]
        yout = [av(21, 2), av(23, 2)]

        def fm4(c0, parts=128):
            return av(c0, 2, BF16, parts).rearrange("p (h t) -> p h t", h=4)
        qinA, ktA, qB, kB, vsb, qinB = fm4(0), fm4(2), fm4(4), fm4(6), fm4(8), fm4(10)
        qrb, krb, qinC = fm4(12, 64), fm4(14, 64), fm4(16, 64)
        cosT, sinT = av(18, 1, F32, 64), av(19, 1, F32, 64)
        mixedT = av(20, 4, BF16).rearrange("p (c t) -> p c t", c=NKC)
        K_MIXT = cells(20, 4)

        def tmp(i, parts=128):
            return av(24 + i, 1, F32, parts), ("a", 24 + i)

        dma("sp", normw[:], dram["normw"][:, :], writes=("normw",))
        dma("sp", c128[:], dram["c128"][:, :], writes=("c128",))
        dma("sp", c64[:], dram["c64"][:, :], writes=("c64",))
        dma("sp", lbp[:], dram["lbp"][:, :], writes=("lbp",))
        dma("sp", cw[:], dram["cw"][:, :], writes=("cw",))
        dma("sp", hnorm[:], dram["hnorm"][0:1, :].partition_broadcast(64), writes=("hnorm",))
        dma("sp", gnorm[:], dram["gnorm"][0:1, :].partition_broadcast(64), writes=("gnorm",))
        dma("sp", negA[:], dram["alog"][0:1, :].partition_broadcast(64), writes=("negA",))
        dma("sp", dtb[:], dram["dtb"][0:1, :].partition_broadcast(64), writes=("dtb",))
        A("dve", "tensor_copy", identb[:], ident, r=("c128",), w=("identb",))
        A("pool", "memset", onesb[:], 1.0, w=("onesb",))
        A("pool", "memset", cst[:, 0:1], EPS, w=("cst",))
        A("pool", "memset", cst[:, 1:2], 1.0, w=("cst",))
        A("pool", "memset", cst[:, 2:3], math.pi / 2, w=("cst",))
        A("pool", "memset", cst[:, 3:4], 0.0, w=("cst",))
        for l in range(DEPTH):
            A("pool", "memset", SA[l][:], 0.0, w=(("SA", l),))
            A("pool", "memset", SB[l][:], 0.0, w=(("SB", l),))
            A("pool", "memset", RC[l][:], 0.0, w=(("RC", l),))
            A("pool", "memset", SAb[l][:], 0.0, w=(("SAb", l),))
            A("pool", "memset", SBb[l][:], 0.0, w=(("SBb", l),))
            A("pool", "memset", RCb[l][:], 0.0, w=(("RCb", l),))
            A("pool", "memset", halo[l][:], 0.0, w=(("halo", l),))
        act(negA[:], negA[:], AF.Exp, r=("negA",), w=("negA",))
        A("dve", "tensor_scalar", negA[:], negA[:], -1.0, None, op0=ALU.mult, r=("negA",), w=("negA",))
        lb3 = lbp[:].rearrange("p (h l) -> p h l", h=4)
        A("dve", "tensor_reduce", lbm[:, 0:4], lb3, axis=AX.X, op=ALU.max, r=("lbp",), w=("lbm",))
        A("dve", "tensor_tensor", lbw[:].rearrange("p (h l) -> p h l", h=4), lb3,
          lbm[:, 0:4].unsqueeze(2).to_broadcast([128, 4, 4]), op=ALU.subtract, r=("lbp", "lbm"), w=("lbw",))
        act(lbw[:], lbw[:], AF.Exp, r=("lbw",), w=("lbw",))
        A("dve", "tensor_reduce", lbm[:, 4:8], lbw[:].rearrange("p (h l) -> p h l", h=4), axis=AX.X, op=ALU.add,
          r=("lbw",), w=("lbm",))
        A("dve", "reciprocal", lbm[:, 4:8], lbm[:, 4:8], r=("lbm",), w=("lbm",))
        A("dve", "tensor_tensor", lbw[:].rearrange("p (h l) -> p h l", h=4), lbw[:].rearrange("p (h l) -> p h l", h=4),
          lbm[:, 4:8].unsqueeze(2).to_broadcast([128, 4, 4]), op=ALU.mult, r=("lbw", "lbm"), w=("lbw",))
        lbs3 = lbs[:].rearrange("p (h l) -> p h l", h=4)
        lbw3 = lbw[:].rearrange("p (h l) -> p h l", h=4)
        A("pool", "memset", lbs[:], 0.0, w=("lbs",))
        for l in range(1, 4):
            A("dve", "tensor_tensor", lbs3[:, :, l], lbs3[:, :, l - 1], lbw3[:, :, l], op=ALU.add,
              r=("lbs", "lbw"), w=("lbs",))
        A("dve", "tensor_scalar", oml[:], lbs[:], -1.0, 1.0, op0=ALU.mult, op1=ALU.add, r=("lbs",), w=("oml",))
        A("dve", "tensor_scalar", noml[:], lbs[:], -1.0, None, op0=ALU.add, r=("lbs",), w=("noml",))
        for l in range(DEPTH):
            for wn in WNAMES:
                K_, N_ = WSHAPES[wn]
                for r0 in range(0, K_, 256):
                    r1 = min(K_, r0 + 256)
                    dma("pool", wbf[wn][l, r0:r1, :], dram[wn][l, r0:r1, :], writes=(("bf", wn, l, r0),))

        def rmsnorm():
            act(sqv.rearrange("p c t -> p (c t)"), xT[:].rearrange("p c t -> p (c t)"), AF.Square,
                r=("xT",), w=cells(11, 4))
            pb, pk = psbank()
            for c in range(NKC):
                mm(pb[:], onesb[:], sqv[:, c, :], c == 0, c == NKC - 1, r=cells(11, 4) + ["onesb"], w=(pk,))
            act(rtmp[:], pb[:], AF.Ln, r=(pk, "cst"), w=("rtmp",), bias=cst[:, 0:1], scale=1.0 / D)
            act(rstd[:], rtmp[:], AF.Exp, r=("rtmp",), w=("rstd",), scale=-0.5)

        def norm_apply(widx, out_tile, out_keys):
            for c in range(NKC):
                A("dve", "scalar_tensor_tensor", out=out_tile[:, c, :], in0=xT[:, c, :],
                  scalar=normw[:, widx + c:widx + c + 1], in1=rstd[:], op0=ALU.mult, op1=ALU.mult,
                  r=("xT", "rstd", "normw"), w=out_keys)

        def ffn(l, which):
            pre = "ffn%d_w_" % which
            widx = (l * 3 + (0 if which == 1 else 2)) * NKC
            rmsnorm()
            norm_apply(widx, hT, ("hT",))
            for blk in range(NFC // 2):
                wg, kg = load_w(pre + "gate", l, 0, NKC, blk * 256, 256)
                wu, ku = load_w(pre + "up", l, 0, NKC, blk * 256, 256)
                for sub in range(2):
                    f = blk * 2 + sub
                    pg, kpg = psbank(0, 4)
                    pu, kpu = psbank(0, 4)
                    for c in range(NKC):
                        mm(pg[:], wg[:, c, sub * 128:(sub + 1) * 128], hT[:, c, :], c == 0, c == NKC - 1,
                           r=("hT", kg), w=(kpg,))
                    for c in range(NKC):
                        mm(pu[:], wu[:, c, sub * 128:(sub + 1) * 128], hT[:, c, :], c == 0, c == NKC - 1,
                           r=("hT", ku), w=(kpu,))
                    sgt = sgv[f % 2]
                    ksg = ("a", 15 + f % 2)
                    act(sgt, pg[:], AF.Silu, r=(kpg,), w=(ksg,))
                    A("dve", "tensor_tensor", hid[:, f, :], pu[:], sgt, op=ALU.mult, r=(kpu, ksg),
                      w=(("a", f // 2),))
            for half in range(2):
                banks = [(ps[4 + i], "ps%d" % (4 + i)) for i in range(4)]
                for f0 in range(0, NFC, 4):
                    nf = min(4, NFC - f0)
                    wd, kd = load_w(pre + "down", l, f0, nf, half * 512, 512)
                    for fi in range(nf):
                        f = f0 + fi
                        for dci in range(4):
                            pb, pk = banks[dci]
                            mm(pb[:], wd[:, fi, dci * 128:(dci + 1) * 128], hid[:, f, :], f == 0, f == NFC - 1,
                               r=(("a", f // 2), kd), w=(pk,))
                for dci in range(4):
                    dc = half * 4 + dci
                    pb, pk = banks[dci]
                    A("dve", "scalar_tensor_tensor", out=xT[:, dc, :], in0=pb[:], scalar=0.5, in1=xT[:, dc, :],
                      op0=ALU.mult, op1=ALU.add, r=(pk, "xT"), w=("xT",))

        def load_tile(t):
            for b in range(T // 128):
                xi = xin[b % 2]
                xk = cells(17 + 2 * (b % 2), 2)
                r0 = t * T + b * 128
                dma("sp", xi, x_d[r0:r0 + 128, :], writes=xk)
                for g in range(2):
                    pb, pk = psbank()
                    for j in range(4):
                        c = g * 4 + j
                        tp(pb[:, j * 128:(j + 1) * 128], xi[:, c * 128:(c + 1) * 128], ident, r=xk + ["c128"], w=(pk,))
                    act(xT[:, g * 4:(g + 1) * 4, b * 128:(b + 1) * 128], pb[:].rearrange("p (j n) -> p j n", j=4),
                        AF.Copy, r=(pk,), w=("xT",))
            posi, kpi = tmp(0, 64)
            posf, kpf = tmp(1, 64)
            kq, kkq = tmp(2, 64)
            ang, kang = tmp(3, 64)
            s1, ks1 = tmp(4, 64)
            c1, kc1 = tmp(5, 64)
            invf = c64[:, O_INVF:O_INVF + 1]
            dma("sp", posi.bitcast(I32), pos_d[0:1, t * T:(t + 1) * T].partition_broadcast(64), writes=(kpi,))
            A("dve", "tensor_copy", posf, posi.bitcast(I32), r=(kpi,), w=(kpf,))
            A("dve", "tensor_scalar", kq, posf, invf, 1.0 / (2 * math.pi), op0=ALU.mult, op1=ALU.mult,
              r=(kpf, "c64"), w=(kkq,))
            A("dve", "tensor_copy", posi.bitcast(I32), kq, r=(kkq,), w=(kpi,))
            A("dve", "tensor_copy", kq, posi.bitcast(I32), r=(kpi,), w=(kkq,))
            A("dve", "tensor_scalar", ang, posf, invf, None, op0=ALU.mult, r=(kpf, "c64"), w=(kang,))
            A("dve", "scalar_tensor_tensor", out=ang, in0=kq, scalar=-C1_RR, in1=ang, op0=ALU.mult, op1=ALU.add,
              r=(kkq, kang), w=(kang,))
            A("dve", "scalar_tensor_tensor", out=ang, in0=kq, scalar=-C2_RR, in1=ang, op0=ALU.mult, op1=ALU.add,
              r=(kkq, kang), w=(kang,))
            A("dve", "tensor_scalar", ang, ang, 0.25, None, op0=ALU.mult, r=(kang,), w=(kang,))
            act(s1, ang, AF.Sin, r=(kang,), w=(ks1,))
            act(c1, ang, AF.Sin, r=(kang, "cst"), w=(kc1,), bias=cst[0:64, 2:3])
            s2, c2 = posf, kq
            A("dve", "scalar_tensor_tensor", out=s2, in0=s1, scalar=2.0, in1=c1, op0=ALU.mult, op1=ALU.mult,
              r=(ks1, kc1), w=(kpf,))
            A("dve", "tensor_tensor", c2, s1, s1, op=ALU.mult, r=(ks1,), w=(kkq,))
            A("dve", "tensor_scalar", c2, c2, -2.0, 1.0, op0=ALU.mult, op1=ALU.add, r=(kkq,), w=(kkq,))
            A("dve", "scalar_tensor_tensor", out=sinT, in0=s2, scalar=2.0, in1=c2, op0=ALU.mult, op1=ALU.mult,
              r=(kpf, kkq), w=(("a", 19),))
            A("dve", "tensor_tensor", cosT, s2, s2, op=ALU.mult, r=(kpf,), w=(("a", 18),))
            A("dve", "tensor_scalar", cosT, cosT, -2.0, 1.0, op0=ALU.mult, op1=ALU.add, r=(("a", 18),), w=(("a", 18),))

        outs = []

        def store_tile(t):
            widx = DEPTH * 3 * NKC
            rmsnorm()
            norm_apply(widx, xT, ("xT",))
            for b in range(T // 128):
                yo = yout[b % 2]
                yk = cells(21 + 2 * (b % 2), 2)
                for g in range(2):
                    pb, pk = psbank()
                    for j in range(4):
                        c = g * 4 + j
                        tp(pb[:, j * 128:(j + 1) * 128], xT[:, c, b * 128:(b + 1) * 128], ident, r=("xT", "c128"), w=(pk,))
                    act(yo[:, g * 512:(g + 1) * 512], pb[:], AF.Copy, r=(pk,), w=yk)
                r0 = t * T + b * 128
                outs.append(dma("sp", y_d[r0:r0 + 128, :], yo, reads=yk))

        def head_out(po, pk, gw, kgw, H, dv, c0mix, ch, extra_r=()):
            W = H * dv
            osq, kosq = tmp(16, 64)
            omix, komix = tmp(17, 64)
            omb = omix.bitcast(BF16)[:, 0:W]
            act(osq[:, 0:W], po[0:64, 0:W], AF.Square, r=(pk,), w=(kosq,))
            A("dve", "tensor_reduce", rs4[:, 0:4], osq[:, 0:W].rearrange("p (h v) -> p h v", h=H), axis=AX.X,
              op=ALU.add, r=(kosq,), w=("rs4",))
            act(rs4[:, 4:8], rs4[:, 0:4], AF.Ln, r=("rs4", "cst"), w=("rs4",), bias=cst[0:64, 0:1], scale=1.0 / dv)
            act(rs4[:, 8:12], rs4[:, 4:8], AF.Exp, r=("rs4",), w=("rs4",), scale=-0.5)
            for h in range(H):
                A("dve", "scalar_tensor_tensor", out=omb[:, h * dv:(h + 1) * dv], in0=po[0:64, h * dv:(h + 1) * dv],
                  scalar=rs4[:, 8 + h:9 + h], in1=gw[:, h * dv:(h + 1) * dv], op0=ALU.mult, op1=ALU.mult,
                  r=(pk, "rs4", kgw), w=(komix,))
            pb, pkb = psbank()
            pbb = pb[:].bitcast(BF16)
            n = W // 128
            for j in range(n):
                tp(pbb[:, j * 64:(j + 1) * 64], omb[:, j * 128:(j + 1) * 128], identb[0:64, 0:64],
                   r=(komix, "identb"), w=(pkb,))
            act(mixedT[:, c0mix:c0mix + n, ch * C:(ch + 1) * C],
                pbb[:, 0:n * 64].rearrange("p (j n) -> p j n", j=n), AF.Copy, r=(pkb,), w=K_MIXT)

        def mixer(l):
            widx = (l * 3 + 1) * NKC
            rmsnorm()
            norm_apply(widx, hT, ("hT",))
            if len(enable) < 3:
                A("pool", "memset", mixedT.rearrange("p c t -> p (c t)"), 0.0, w=K_MIXT)

            if "A" in enable:
                for h in range(4):
                    wq, kwq = load_w("w_in", l, 0, NKC, h * 128, 128)
                    wf, kwf = load_w("w_in", l, 0, NKC, 512 + h * 128, 128)
                    pq, kpq = psbank()
                    pf, kpf_ = psbank()
                    for c in range(NKC):
                        mm(pq[:], wq[:, c, :], hT[:, c, :], c == 0, c == NKC - 1, r=("hT", kwq), w=(kpq,))
                    for c in range(NKC):
                        mm(pf[:], wf[:, c, :], hT[:, c, :], c == 0, c == NKC - 1, r=("hT", kwf), w=(kpf_,))
                    b0 = (h % 2) * 6
                    (tq, ktq), (ts, kts), (tk, ktk), (tf, ktf), (teb, kteb), (tenb, ktenb) = [tmp(b0 + i) for i in range(6)]
                    li = h * 4 + l
                    act(tq, pq[:], AF.Silu, r=(kpq,), w=(ktq,))
                    act(ts, pf[:], AF.Sigmoid, r=(kpf_,), w=(kts,))
                    A("dve", "tensor_scalar", tf, ts, oml[:, li:li + 1], lbs[:, li:li + 1], op0=ALU.mult, op1=ALU.add,
                      r=(kts, "oml", "lbs"), w=(ktf,))
                    A("dve", "tensor_scalar", tf, tf, 1e-20, None, op0=ALU.max, r=(ktf,), w=(ktf,))
                    act(tf, tf, AF.Ln, r=(ktf,), w=(ktf,))
                    A("dve", "tensor_scalar", tk, ts, noml[:, li:li + 1], oml[:, li:li + 1], op0=ALU.mult, op1=ALU.add,
                      r=(kts, "oml", "noml"), w=(ktk,))
                    A("dve", "tensor_tensor_scan", ts, scanmask, tf, 0.0, op0=ALU.mult, op1=ALU.add,
                      r=(ktf, "c128"), w=(kts,))
                    act(teb, ts, AF.Exp, r=(kts,), w=(kteb,))
                    act(tenb, ts, AF.Exp, r=(kts,), w=(ktenb,), scale=-1.0)
                    A("dve", "tensor_tensor", qinA[:, h, :], tq, teb, op=ALU.mult, r=(ktq, kteb), w=cells(0, 2))
                    A("dve", "tensor_tensor", ktA[:, h, :], tk, tenb, op=ALU.mult, r=(ktk, ktenb), w=cells(2, 2))
                    A("dve", "tensor_copy", dlA[:, h * 8:(h + 1) * 8],
                      teb.rearrange("p (c j) -> p c j", j=C)[:, :, C - 1], r=(kteb,), w=("dlA",))

            if "B" in enable:
                for cidx in range(12):
                    wv_, kwv = load_w("w_in", l, 0, NKC, 1536 + cidx * 128, 128)
                    pc, kpc = psbank()
                    for c in range(NKC):
                        mm(pc[:], wv_[:, c, :], hT[:, c, :], c == 0, c == NKC - 1, r=("hT", kwv), w=(kpc,))
                    b0 = 12 + (cidx % 2) * 4
                    cb = av(24 + b0, 2)
                    kcb = cells(24 + b0, 2)
                    acc, kacc = tmp(b0 + 2)
                    tm_, ktm = tmp(b0 + 3)
                    hk = ("halo", l, cidx)
                    A("pool", "tensor_copy", cb[:, 0:3], halo[l][:, cidx * 3:cidx * 3 + 3], r=(("halo", l), hk), w=kcb)
                    act(cb[:, 3:3 + T], pc[:], AF.Copy, r=(kpc,), w=kcb)
                    A("pool", "tensor_copy", halo[l][:, cidx * 3:cidx * 3 + 3], cb[:, T:T + 3], r=kcb, w=(hk,))
                    cwb = l * 48 + cidx * 4
                    A("pool", "tensor_scalar", acc, cb[:, 0:T], cw[:, cwb:cwb + 1], None, op0=ALU.mult,
                      r=kcb + ["cw"], w=(kacc,))
                    for w_ in range(1, 4):
                        A("dve", "scalar_tensor_tensor", out=acc, in0=cb[:, w_:w_ + T], scalar=cw[:, cwb + w_:cwb + w_ + 1],
                          in1=acc, op0=ALU.mult, op1=ALU.add, r=kcb + ["cw", kacc], w=(kacc,))
                    act(tm_, acc, AF.Silu, r=(kacc,), w=(ktm,))
                    h = cidx % 4
                    if cidx < 8:
                        sqb = acc.bitcast(BF16)[:, 0:T]
                        act(sqb, tm_, AF.Square, r=(ktm,), w=(kacc,))
                        pn, kpn = psbank()
                        mm(pn[:], onesb[:], sqb, True, True, r=(kacc, "onesb"), w=(kpn,))
                        r1 = cb[:, 0:T]
                        r2 = cb[:, T:2 * T]
                        act(r1, pn[:], AF.Ln, r=(kpn, "cst"), w=kcb, bias=cst[:, 0:1])
                        act(r2, r1, AF.Exp, r=kcb, w=kcb, scale=-0.5)
                        dst, kd_ = (qB, cells(4, 2)) if cidx < 4 else (kB, cells(6, 2))
                        A("dve", "scalar_tensor_tensor", out=dst[:, h, :], in0=tm_,
                          scalar=(BDK ** -0.5 if cidx < 4 else 1.0), in1=r2, op0=ALU.mult, op1=ALU.mult,
                          r=[ktm] + kcb, w=kd_)
                    else:
                        A("dve", "tensor_copy", vsb[:, h, :], tm_, r=(ktm,), w=cells(8, 2))
                wt, kwt = load_w("w_in", l, 0, NKC, 3584, 8)
                pba, kpba = psbank()
                for ch in range(NCH):
                    for c in range(NKC):
                        mm(pba[0:64, ch * 8:(ch + 1) * 8], hT[:, c, ch * C:(ch + 1) * C], wt[:, c, :], c == 0, c == NKC - 1,
                           r=("hT", kwt), w=(kpba,))
                pba3 = pba[0:64, 0:64].rearrange("p (c n) -> p c n", n=8)

                def s3(n):
                    return sm[n][:].rearrange("p (c h) -> p c h", h=4)
                act(s3("beta"), pba3[:, :, 0:4], AF.Sigmoid, r=(kpba,), w=("sm_beta",))
                A("dve", "tensor_tensor", s3("xa"), pba3[:, :, 4:8],
                  dtb[:, l * 4:(l + 1) * 4].unsqueeze(1).to_broadcast([64, NCH, 4]), op=ALU.add,
                  r=(kpba, "dtb"), w=("sm_xa",))
                act(sm["ea"][:], sm["xa"][:], AF.Exp, r=("sm_xa",), w=("sm_ea",))
                act(sm["sp"][:], sm["ea"][:], AF.Ln, r=("sm_ea", "cst"), w=("sm_sp",), bias=cst[0:64, 1:2])
                A("dve", "tensor_tensor", s3("g"), s3("sp"),
                  negA[:, l * 4:(l + 1) * 4].unsqueeze(1).to_broadcast([64, NCH, 4]), op=ALU.mult,
                  r=("sm_sp", "negA"), w=("sm_g",))
                pg1, kpg1 = psbank()
                mm(pg1[0:64, 0:32], tri, sm["g"][:], True, True, r=("c64", "sm_g"), w=(kpg1,))
                mm(pg1[0:64, 32:64], onesf[:, 0:64], sm["g"][:], True, True, r=("c64", "sm_g"), w=(kpg1,))
                pg2, kpg2 = psbank()
                mm(pg2[:, 0:32], onesf, sm["g"][:], True, True, r=("c64", "sm_g"), w=(kpg2,))
                act(sm["gc"][:], pg1[0:64, 0:32], AF.Copy, r=(kpg1,), w=("sm_gc",))
                act(sm["egc"][:], pg1[0:64, 0:32], AF.Exp, r=(kpg1,), w=("sm_egc",))
                A("dve", "tensor_tensor", sm["bg"][:], sm["beta"][:], sm["egc"][:], op=ALU.mult,
                  r=("sm_beta", "sm_egc"), w=("sm_bg",))
                A("dve", "tensor_tensor", sm["tmpd"][:], pg1[0:64, 32:64], sm["gc"][:], op=ALU.subtract,
                  r=(kpg1, "sm_gc"), w=("sm_tmpd",))
                act(sm["est"][:], sm["tmpd"][:], AF.Exp, r=("sm_tmpd",), w=("sm_est",))
                act(dlB[:], pg2[:, 0:32], AF.Exp, r=(kpg2,), w=("dlB",))

            if "C" in enable:
                perm = c64[:, O_PERM:O_PERM + 64]
                for qk in range(2):
                    for h in range(4):
                        wc_, kwc = load_w("w_in", l, 0, NKC, 3592 + qk * 256 + h * 64, 64)
                        px, kpx = psbank()
                        for c in range(NKC):
                            mm(px[0:64, :], wc_[:, c, :], hT[:, c, :], c == 0, c == NKC - 1, r=("hT", kwc), w=(kpx,))
                        b0 = 6 + ((qk * 4 + h) % 2) * 3
                        (xf, kxf), (t1, kt1), (t2, kt2) = [tmp(b0 + i, 64) for i in range(3)]
                        act(xf, px[0:64, :], AF.Copy, r=(kpx,), w=(kxf,), scale=(1.0 if qk == 0 else CDK ** -0.5))
                        pr, kpr = psbank()
                        mm(pr[0:64, :], perm, xf, True, True, r=("c64", kxf), w=(kpr,))
                        A("dve", "tensor_tensor", t1, xf, cosT, op=ALU.mult, r=(kxf, ("a", 18)), w=(kt1,))
                        A("dve", "tensor_tensor", t2, pr[0:64, :], sinT, op=ALU.mult, r=(kpr, ("a", 19)), w=(kt2,))
                        A("dve", "tensor_tensor", t1, t1, t2, op=ALU.add, r=(kt1, kt2), w=(kt1,))
                        if qk == 0:
                            act(qrb[:, h, :], t1, AF.Copy, r=(kt1,), w=cells(12, 2))
                            A("dve", "tensor_tensor", qinC[:, h, :].rearrange("p (c i) -> p c i", i=C),
                              t1.rearrange("p (c i) -> p c i", i=C),
                              c64[:, O_GQ + h * 64:O_GQ + (h + 1) * 64].unsqueeze(1).to_broadcast([64, NCH, C]),
                              op=ALU.mult, r=(kt1, "c64"), w=cells(16, 2))
                        else:
                            act(krb[:, h, :], t1, AF.Copy, r=(kt1,), w=cells(14, 2))

            wTM = {}
            if "A" in enable:
                wTM["A"] = [load_w("w_in", l, 0, NKC, 1024 + i * 256, 256) for i in range(2)]
            if "B" in enable:
                wTM["B"] = [load_w("w_in", l, 0, NKC, 3072 + i * 256, 256) for i in range(2)]
            if "C" in enable:
                wTM["C"] = [load_w("w_in", l, 0, NKC, 4104 + i * 256, 256) for i in range(2)]

            def tm_proj(which, ch):
                pb, pk = psbank()
                for i in range(2):
                    wv_, kw_ = wTM[which][i]
                    for c in range(NKC):
                        mm(pb[0:64, i * 256:(i + 1) * 256], hT[:, c, ch * C:(ch + 1) * C], wv_[:, c, :],
                           c == 0, c == NKC - 1, r=("hT", kw_), w=(pk,))
                return pb, pk

            for ch in range(NCH):
                csl = slice(ch * C, (ch + 1) * C)
                if "A" in enable:
                    pat, kpat = tm_proj("A", ch)
                    vA_, kvA = tmp(0, 64)
                    vA = vA_.bitcast(BF16)[:, 0:256]
                    gwA, kgwA = tmp(1, 64)
                    act(vA, pat[0:64, 0:256], AF.Copy, r=(kpat,), w=(kvA,))
                    act(gwA[:, 0:256], pat[0:64, 256:512], AF.Silu, r=(kpat,), w=(kgwA,))
                    A("dve", "tensor_tensor", gwA[:, 0:256].rearrange("p (h v) -> p h v", h=4),
                      gwA[:, 0:256].rearrange("p (h v) -> p h v", h=4),
                      hnorm[:, l * 64:(l + 1) * 64].unsqueeze(1).to_broadcast([64, 4, 64]), op=ALU.mult,
                      r=(kgwA, "hnorm"), w=(kgwA,))
                    pkt, kpkt = psbank()
                    pktb = pkt[0:64, :].bitcast(BF16)
                    for h in range(4):
                        tp(pktb[:, h * 128:(h + 1) * 128], ktA[:, h, csl], identb[:], r=cells(2, 2) + ["identb"], w=(kpkt,))
                    kt_, kkt = tmp(2, 64)
                    ktTM = kt_.bitcast(BF16)[:, 0:512]
                    act(ktTM, pktb[:, 0:512], AF.Copy, r=(kpkt,), w=(kkt,))
                    pa, kpa = psbank()
                    for h in range(4):
                        mm(pa[0:64, h * 64:(h + 1) * 64], ktA[:, h, csl], qinA[:, h, csl], True, True,
                           r=cells(0, 4), w=(kpa,))
                    pt_, kpt = tmp(3, 64)
                    PT = pt_.bitcast(BF16)[:, 0:256]
                    A("dve", "tensor_tensor", PT.rearrange("p (h n) -> p h n", h=4),
                      pa[0:64, 0:256].rearrange("p (h n) -> p h n", h=4),
                      tri.unsqueeze(1).to_broadcast([64, 4, 64]), op=ALU.mult, r=(kpa, "c64"), w=(kpt,))
                    pu_, kpu_ = psbank()
                    for h in range(4):
                        mm(pu_[:, h * 64:(h + 1) * 64], ktTM[:, h * 128:(h + 1) * 128], vA[:, h * 64:(h + 1) * 64],
                           True, True, r=(kkt, kvA), w=(kpu_,))
                    po, kpo = psbank()
                    for h in range(4):
                        mm(po[0:64, h * 64:(h + 1) * 64], qinA[:, h, csl], SAb[l][:, h * 64:(h + 1) * 64], True, False,
                           r=cells(0, 2) + [("SAb", l)], w=(kpo,))
                        mm(po[0:64, h * 64:(h + 1) * 64], PT[:, h * 64:(h + 1) * 64], vA[:, h * 64:(h + 1) * 64], False, True,
                           r=(kpt, kvA), w=(kpo,))
                    tS, ktS = tmp(4)
                    A("dve", "tensor_tensor", tS[:, 0:256], pu_[:, 0:256], SA[l][:], op=ALU.add,
                      r=(kpu_, ("SA", l)), w=(ktS,))
                    A("dve", "tensor_tensor", SA[l][:].rearrange("p (h v) -> p h v", h=4),
                      tS[:, 0:256].rearrange("p (h v) -> p h v", h=4),
                      dlA[:].rearrange("p (h c) -> p h c", h=4)[:, :, ch:ch + 1].to_broadcast([128, 4, 64]),
                      op=ALU.mult, r=(ktS, "dlA"), w=(("SA", l),))
                    act(SAb[l][:], SA[l][:], AF.Copy, r=(("SA", l),), w=(("SAb", l),))
                    head_out(po, kpo, gwA, kgwA, 4, 64, 0, ch)

                if "C" in enable:
                    pct, kpct = tm_proj("C", ch)
                    vC_, kvC = tmp(0, 64)
                    vC = vC_.bitcast(BF16)[:, 0:256]
                    gwC, kgwC = tmp(1, 64)
                    act(vC, pct[0:64, 0:256], AF.Copy, r=(kpct,), w=(kvC,))
                    act(gwC[:, 0:256], pct[0:64, 256:512], AF.Silu, r=(kpct,), w=(kgwC,))
                    pkt, kpkt = psbank()
                    pktb = pkt[0:64, :].bitcast(BF16)
                    for h in range(4):
                        tp(pktb[:, h * 64:(h + 1) * 64], krb[:, h, csl], identb[0:64, 0:64], r=cells(14, 2) + ["identb"], w=(kpkt,))
                    kt_, kkt = tmp(2, 64)
                    kst = kt_.bitcast(BF16)[:, 0:256]
                    A("dve", "tensor_tensor", kst, pktb[:, 0:256], c64[:, O_GDEC:O_GDEC + 256], op=ALU.mult,
                      r=(kpkt, "c64"), w=(kkt,))
                    pa, kpa = psbank()
                    for h in range(4):
                        mm(pa[0:64, h * 64:(h + 1) * 64], krb[:, h, csl], qrb[:, h, csl], True, True,
                           r=cells(12, 4), w=(kpa,))
                    pt_, kpt = tmp(3, 64)
                    PT = pt_.bitcast(BF16)[:, 0:256]
                    A("dve", "tensor_tensor", PT, pa[0:64, 0:256], c64[:, O_DMAT:O_DMAT + 256], op=ALU.mult,
                      r=(kpa, "c64"), w=(kpt,))
                    pu_, kpu_ = psbank()
                    for h in range(4):
                        mm(pu_[0:64, h * 64:(h + 1) * 64], kst[:, h * 64:(h + 1) * 64], vC[:, h * 64:(h + 1) * 64],
                           True, True, r=(kkt, kvC), w=(kpu_,))
                    po, kpo = psbank()
                    for h in range(4):
                        mm(po[0:64, h * 64:(h + 1) * 64], qinC[:, h, csl], RCb[l][:, h * 64:(h + 1) * 64], True, False,
                           r=cells(16, 2) + [("RCb", l)], w=(kpo,))
                        mm(po[0:64, h * 64:(h + 1) * 64], PT[:, h * 64:(h + 1) * 64], vC[:, h * 64:(h + 1) * 64], False, True,
                           r=(kpt, kvC), w=(kpo,))
                    for h in range(4):
                        A("dve", "scalar_tensor_tensor", out=RC[l][:, h * 64:(h + 1) * 64], in0=RC[l][:, h * 64:(h + 1) * 64],
                          scalar=g64[h], in1=pu_[0:64, h * 64:(h + 1) * 64], op0=ALU.mult, op1=ALU.add,
                          r=(kpu_, ("RC", l)), w=(("RC", l),))
                    act(RCb[l][:], RC[l][:], AF.Copy, r=(("RC", l),), w=(("RCb", l),))
                    head_out(po, kpo, gwC, kgwC, 4, 64, 6, ch)

                if "B" in enable:
                    pbt, kpbt = tm_proj("B", ch)
                    gwB, kgwB = tmp(1, 64)
                    act(gwB, pbt[0:64, :], AF.Silu, r=(kpbt,), w=(kgwB,))
                    A("dve", "tensor_tensor", gwB.rearrange("p (h v) -> p h v", h=4),
                      gwB.rearrange("p (h v) -> p h v", h=4),
                      gnorm[:, l * 128:(l + 1) * 128].unsqueeze(1).to_broadcast([64, 4, 128]), op=ALU.mult,
                      r=(kgwB, "gnorm"), w=(kgwB,))

                    def s3c(n):
                        return sm[n][:].rearrange("p (c h) -> p c h", h=4)[:, ch, :]

                    def bc128(n):
                        return s3c(n).unsqueeze(2).to_broadcast([64, 4, 128])
                    pkt, kpkt = psbank()
                    pktb = pkt[0:64, :].bitcast(BF16)
                    for h in range(4):
                        tp(pktb[:, h * 128:(h + 1) * 128], kB[:, h, csl], identb[:], r=cells(6, 2) + ["identb"], w=(kpkt,))
                    kbg_, kkbg = tmp(0, 64)
                    kbg = kbg_.bitcast(BF16)[:, 0:512]
                    kst_, kkst = tmp(2, 64)
                    kst = kst_.bitcast(BF16)[:, 0:512]
                    A("dve", "tensor_tensor", kbg.rearrange("p (h n) -> p h n", h=4),
                      pktb[:, 0:512].rearrange("p (h n) -> p h n", h=4), bc128("bg"), op=ALU.mult,
                      r=(kpkt, "sm_bg"), w=(kkbg,))
                    A("dve", "tensor_tensor", kst.rearrange("p (h n) -> p h n", h=4),
                      pktb[:, 0:512].rearrange("p (h n) -> p h n", h=4), bc128("est"), op=ALU.mult,
                      r=(kpkt, "sm_est"), w=(kkst,))
                    pvt, kpvt = psbank()
                    pvtb = pvt[0:64, :].bitcast(BF16)
                    for h in range(4):
                        tp(pvtb[:, h * 128:(h + 1) * 128], vsb[:, h, csl], identb[:], r=cells(8, 2) + ["identb"], w=(kpvt,))
                    bv_, kbv = tmp(3, 64)
                    bv = bv_.bitcast(BF16)[:, 0:512]
                    A("dve", "tensor_tensor", bv.rearrange("p (h n) -> p h n", h=4),
                      pvtb[:, 0:512].rearrange("p (h n) -> p h n", h=4), bc128("beta"), op=ALU.mult,
                      r=(kpvt, "sm_beta"), w=(kbv,))
                    pkk, kpkk = psbank()
                    for h in range(4):
                        mm(pkk[0:64, h * 64:(h + 1) * 64], kB[:, h, csl], kB[:, h, csl], True, True, r=cells(6, 2), w=(kpkk,))
                    for h in range(4):
                        mm(pkk[0:64, 256 + h * 64:256 + (h + 1) * 64], qB[:, h, csl], kB[:, h, csl], True, True,
                           r=cells(4, 4), w=(kpkk,))
                    gt, kgt = tmp(4, 64)
                    A("dve", "tensor_tensor", gt[:, 0:256].rearrange("p (h n) -> p h n", h=4),
                      tri.unsqueeze(1).to_broadcast([64, 4, 64]),
                      s3c("g").unsqueeze(2).to_broadcast([64, 4, 64]), op=ALU.mult, r=("c64", "sm_g"), w=(kgt,))
                    pdf, kpdf = psbank()
                    for h in range(4):
                        mm(pdf[0:64, h * 64:(h + 1) * 64], gt[:, h * 64:(h + 1) * 64], su, True, True,
                           r=(kgt, "c64"), w=(kpdf,))
                    E_, kE = tmp(5, 64)
                    act(E_[:, 0:256], pdf[0:64, 0:256], AF.Exp, r=(kpdf,), w=(kE,))
                    A("dve", "tensor_tensor", E_[:, 256:512], E_[:, 0:256], c64[:, O_INCL:O_INCL + 256], op=ALU.mult,
                      r=(kE, "c64"), w=(kE,))
                    A("dve", "tensor_tensor", E_[:, 0:256], E_[:, 0:256], c64[:, O_STRICT:O_STRICT + 256], op=ALU.mult,
                      r=(kE, "c64"), w=(kE,))
                    MQ = [tmp(6, 64), tmp(7, 64)]
                    At_, kAt = tmp(8, 64)
                    Q0 = MQ[0][0][:, 256:512]
                    for h in range(4):
                        A("dve", "scalar_tensor_tensor", out=Q0[:, h * 64:(h + 1) * 64], in0=pkk[0:64, h * 64:(h + 1) * 64],
                          scalar=s3c("beta")[:, h:h + 1], in1=E_[:, h * 64:(h + 1) * 64], op0=ALU.mult, op1=ALU.mult,
                          r=(kpkk, "sm_beta", kE), w=(MQ[0][1],))
                    A("dve", "tensor_tensor", At_[:, 0:256], pkk[0:64, 256:512], E_[:, 256:512], op=ALU.mult,
                      r=(kpkk, kE), w=(kAt,))
                    ptr, kptr = psbank()
                    for h in range(4):
                        tp(ptr[0:64, h * 64:(h + 1) * 64], Q0[:, h * 64:(h + 1) * 64], ident64, r=(MQ[0][1], "c128"), w=(kptr,))
                    for h in range(4):
                        tp(ptr[0:64, 256 + h * 64:256 + (h + 1) * 64], At_[:, h * 64:(h + 1) * 64], ident64,
                           r=(kAt, "c128"), w=(kptr,))
                    act(MQ[0][0][:, 0:256], ptr[0:64, 0:256], AF.Copy, r=(kptr,), w=(MQ[0][1],))
                    aT_, kaT = tmp(9, 64)
                    attnT = aT_.bitcast(BF16)[:, 0:256]
                    act(attnT, ptr[0:64, 256:512], AF.Copy, r=(kptr,), w=(kaT,))
                    dg, kdg = tmp(10, 64)
                    A("dve", "tensor_tensor", dg[:, 0:256].rearrange("p (h n) -> p h n", h=4),
                      ident64.unsqueeze(1).to_broadcast([64, 4, 64]),
                      s3c("egc").unsqueeze(2).to_broadcast([64, 4, 64]), op=ALU.mult, r=("c128", "sm_egc"), w=(kdg,))
                    pqe, kpqe = psbank()
                    mm(pqe[:, 0:256], onesf, dg[:, 0:256], True, True, r=("c64", kdg), w=(kpqe,))
                    A("dve", "tensor_tensor", qinB[:, :, csl], qB[:, :, csl],
                      pqe[:, 0:256].rearrange("p (h n) -> p h n", h=4), op=ALU.mult,
                      r=cells(4, 2) + [kpqe], w=cells(10, 2))
                    Rb = [tmp(11, 64), tmp(12, 64)]
                    A("dve", "tensor_tensor", Rb[0][0][:, 0:256].rearrange("p (h n) -> p h n", h=4),
                      ident64.unsqueeze(1).to_broadcast([64, 4, 64]),
                      MQ[0][0][:, 0:256].rearrange("p (h n) -> p h n", h=4), op=ALU.subtract,
                      r=("c128", MQ[0][1]), w=(Rb[0][1],))
                    Tm_, kTm = tmp(13, 64)
                    TmT = Tm_.bitcast(BF16)[:, 0:256]
                    for k in range(5):
                        cur, kcur = MQ[k % 2]
                        nxt, knxt = MQ[(k + 1) % 2]
                        psq, kpsq = psbank()
                        if k < 4:
                            for h in range(4):
                                hs = slice(h * 64, (h + 1) * 64)
                                hq = slice(256 + h * 64, 256 + (h + 1) * 64)
                                mm(psq[0:64, hs], cur[:, hq], cur[:, hs], True, True, r=(kcur,), w=(kpsq,))
                        for h in range(4):
                            hs = slice(h * 64, (h + 1) * 64)
                            hq = slice(256 + h * 64, 256 + (h + 1) * 64)
                            mm(psq[0:64, hq], cur[:, hs], cur[:, hq], True, True, r=(kcur,), w=(kpsq,))
                        if k < 4:
                            act(nxt, psq[0:64, :], AF.Copy, r=(kpsq,), w=(knxt,))
                        else:
                            act(nxt[:, 256:512], psq[0:64, 256:512], AF.Copy, r=(kpsq,), w=(knxt,))
                        rc, krc = Rb[k % 2]
                        rn, krn = Rb[(k + 1) % 2]
                        pru, kpru = psbank()
                        for h in range(4):
                            hs = slice(h * 64, (h + 1) * 64)
                            hq = slice(256 + h * 64, 256 + (h + 1) * 64)
                            mm(pru[0:64, hs], nxt[:, hq], rc[:, hs], True, True, r=(knxt, krc), w=(kpru,))
                        if k < 4:
                            A("dve", "tensor_tensor", rn[:, 0:256], rc[:, 0:256], pru[0:64, 0:256], op=ALU.add,
                              r=(krc, kpru), w=(krn,))
                        else:
                            A("dve", "tensor_tensor", TmT, rc[:, 0:256], pru[0:64, 0:256], op=ALU.add,
                              r=(krc, kpru), w=(kTm,))
                    pu_, kpu_ = psbank()
                    for h in range(4):
                        mm(pu_[0:64, h * 128:(h + 1) * 128], TmT[:, h * 64:(h + 1) * 64], bv[:, h * 128:(h + 1) * 128],
                           True, True, r=(kTm, kbv), w=(kpu_,))
                    uB, kuB = tmp(14, 64)
                    act(uB, pu_[0:64, :], AF.Copy, r=(kpu_,), w=(kuB,))
                    pw, kpw = psbank()
                    for h in range(4):
                        mm(pw[:, h * 64:(h + 1) * 64], kbg[:, h * 128:(h + 1) * 128], TmT[:, h * 64:(h + 1) * 64],
                           True, True, r=(kTm, kkbg), w=(kpw,))
                    wT_, kwT = tmp(15)
                    wTb = wT_.bitcast(BF16)[:, 0:256]
                    act(wTb, pw[:, 0:256], AF.Copy, r=(kpw,), w=(kwT,))
                    pws, kpws = psbank()
                    for h in range(4):
                        mm(pws[0:64, h * 128:(h + 1) * 128], wTb[:, h * 64:(h + 1) * 64], SBb[l][:, h * 128:(h + 1) * 128],
                           True, True, r=(kwT, ("SBb", l)), w=(kpws,))
                    vn_, kvn = tmp(18, 64)
                    vnew = vn_.bitcast(BF16)[:, 0:512]
                    A("dve", "tensor_tensor", vnew, uB, pws[0:64, :], op=ALU.subtract, r=(kuB, kpws), w=(kvn,))
                    po, kpo = psbank()
                    for h in range(4):
                        mm(po[0:64, h * 128:(h + 1) * 128], qinB[:, h, csl], SBb[l][:, h * 128:(h + 1) * 128], True, False,
                           r=cells(10, 2) + [("SBb", l)], w=(kpo,))
                        mm(po[0:64, h * 128:(h + 1) * 128], attnT[:, h * 64:(h + 1) * 64], vnew[:, h * 128:(h + 1) * 128],
                           False, True, r=(kaT, kvn), w=(kpo,))
                    psu, kpsu = psbank()
                    for h in range(4):
                        mm(psu[:, h * 128:(h + 1) * 128], kst[:, h * 128:(h + 1) * 128], vnew[:, h * 128:(h + 1) * 128],
                           True, True, r=(kkst, kvn), w=(kpsu,))
                    for h in range(4):
                        A("dve", "scalar_tensor_tensor", out=SB[l][:, h * 128:(h + 1) * 128],
                          in0=SB[l][:, h * 128:(h + 1) * 128], scalar=dlB[:, ch * 4 + h:ch * 4 + h + 1],
                          in1=psu[:, h * 128:(h + 1) * 128], op0=ALU.mult, op1=ALU.add,
                          r=(kpsu, ("SB", l), "dlB"), w=(("SB", l),))
                    act(SBb[l][:], SB[l][:], AF.Copy, r=(("SB", l),), w=(("SBb", l),))
                    head_out(po, kpo, gwB, kgwB, 4, 128, 2, ch)

            for blk in range(4):
                wo, kwo = load_w("w_out", l, 0, NKC, blk * 256, 256)
                for sub in range(2):
                    dc = blk * 2 + sub
                    pb, pk = psbank()
                    for c in range(NKC):
                        mm(pb[:], wo[:, c, sub * 128:(sub + 1) * 128], mixedT[:, c, :], c == 0, c == NKC - 1,
                           r=K_MIXT + [kwo], w=(pk,))
                    A("dve", "tensor_tensor", xT[:, dc, :], pb[:], xT[:, dc, :], op=ALU.add, r=(pk, "xT"), w=("xT",))

        for t in range(NT):
            load_tile(t)
            for l in range(DEPTH):
                ffn(l, 1)
                if enable:
                    mixer(l)
                ffn(l, 2)
            store_tile(t)
        sc.add("sp", None, extra=outs)
        sc.plan()
        sc.emit(nc, engsem, dmasem, block)
    return nc


def host_consts(inputs, DEPTH):
    f32 = np.float32
    nw = np.zeros((128, DEPTH * 3 * NKC + NKC), f32)
    for l in range(DEPTH):
        for wi, nm in enumerate(("ffn1_norm", "mix_norm", "ffn2_norm")):
            nw[:, (l * 3 + wi) * NKC:(l * 3 + wi + 1) * NKC] = np.asarray(inputs[nm][l]).reshape(NKC, 128).T
    nw[:, DEPTH * 3 * NKC:] = np.asarray(inputs["final_norm"]).reshape(NKC, 128).T
    c128 = np.zeros((128, 640), f32)
    c128[:, 0:128] = np.eye(128, dtype=f32)
    sm_ = np.ones(T, f32)
    sm_[::C] = 0.0
    c128[:, 128:640] = sm_[None, :]
    i = np.arange(C)
    c64 = np.zeros((64, C64W), f32)
    tri = (i[:, None] <= i[None, :]).astype(f32)
    c64[:, O_TRI:O_TRI + 64] = tri
    c64[:, O_SU:O_SU + 64] = (i[:, None] > i[None, :]).astype(f32)
    strict = (i[:, None] > i[None, :]).astype(f32)
    incl = (i[:, None] >= i[None, :]).astype(f32)
    c64[:, O_STRICT:O_STRICT + 256] = np.tile(strict, (1, 4))
    c64[:, O_INCL:O_INCL + 256] = np.tile(incl, (1, 4))
    lg = np.log1p(-np.exp2(-5.0 - np.arange(4, dtype=f32))).astype(f32)
    for h in range(4):
        rel = (i[None, :] - i[:, None]).astype(f32)
        dm = np.where(rel >= 0, np.exp(lg[h] * np.where(rel >= 0, rel, 0.0)), 0.0).astype(f32)
        c64[:, O_DMAT + h * 64:O_DMAT + (h + 1) * 64] = dm
        c64[:, O_GDEC + h * 64:O_GDEC + (h + 1) * 64] = np.exp(lg[h] * (C - 1 - i)).astype(f32)[:, None]
        c64[:, O_GQ + h * 64:O_GQ + (h + 1) * 64] = np.exp(lg[h] * (i + 1.0)).astype(f32)[None, :]
    pm = np.zeros((64, 64), f32)
    for d in range(32):
        pm[d + 32, d] = -1.0
        pm[d, d + 32] = 1.0
    c64[:, O_PERM:O_PERM + 64] = pm
    inv = (np.float32(10000.0) ** (-np.arange(32, dtype=f32) / np.float32(32))).astype(f32)
    c64[:, O_INVF] = np.concatenate([inv, inv])
    c64[:, O_ONES:O_ONES + 128] = 1.0
    lbp = np.asarray(inputs["hgrn_lower_bounds"]).reshape(4, 4, 128).transpose(2, 1, 0).reshape(128, 16)
    cwh = np.asarray(inputs["gdn_conv"])[:DEPTH].reshape(DEPTH, 4, 12, 128).transpose(3, 0, 2, 1).reshape(128, DEPTH * 48)
    return {"normw": nw, "c128": c128, "c64": c64,
            "lbp": np.ascontiguousarray(lbp, dtype=f32), "cw": np.ascontiguousarray(cwh, dtype=f32),
            "hnorm": np.ascontiguousarray(np.asarray(inputs["hgrn_norm"])[:DEPTH].reshape(1, -1), dtype=f32),
            "gnorm": np.ascontiguousarray(np.asarray(inputs["gdn_norm"])[:DEPTH].reshape(1, -1), dtype=f32),
            "alog": np.ascontiguousarray(np.asarray(inputs["gdn_a_log"])[:DEPTH].reshape(1, -1), dtype=f32),
            "dtb": np.ascontiguousarray(np.asarray(inputs["gdn_dt_bias"])[:DEPTH].reshape(1, -1), dtype=f32)}


def run(inputs, NT, DEPTH, seqs, ncores, enable=("A", "B", "C")):
    nc = build_program(NT, DEPTH, tuple(enable))
    consts = host_consts(inputs, DEPTH)
    in_maps = []
    S = NT * T
    for ci in range(ncores):
        b = seqs[ci]
        m = {"x": np.ascontiguousarray(inputs["x"][b, :S]),
             "pos": np.ascontiguousarray(inputs["positions"][b, :S].reshape(1, S)).astype(np.int32)}
        for wn in WNAMES:
            m[wn] = np.ascontiguousarray(inputs[wn][:DEPTH])
        m.update(consts)
        in_maps.append(m)
    res = run_bass_kernel_spmd(nc, in_maps, core_ids=list(range(ncores)))
    return [r["y"] for r in res.results]


def kernel(**inputs):
    inputs = {k: np.asarray(v) for k, v in inputs.items()}
    outs = run(inputs, NT=16, DEPTH=4, seqs=[0, 1, 2, 3, 0, 1, 2, 3], ncores=8)
    return np.stack(outs[:4], axis=0).astype(np.float32)
```

```python
import math
import numpy as np
import concourse.bass as bass
import concourse.mybir as mybir
from concourse.bass_utils import run_bass_kernel_spmd

F32 = mybir.dt.float32
BF16 = mybir.dt.bfloat16
I32 = mybir.dt.int32
AF = mybir.ActivationFunctionType
ALU = mybir.AluOpType
AX = mybir.AxisListType

D = 1024
DFF = 2816
DIN = 4616
T = 512
C = 64
NCH = T // C
EPS = 1e-6
NKC = D // 128
NFC = DFF // 128


class Op:
    __slots__ = ("eng", "fn", "deps", "pos", "signal", "sigval", "dma", "dsem", "dval", "waits", "gi")

    def __init__(self, eng, fn, dma):
        self.eng = eng
        self.fn = fn
        self.dma = dma
        self.deps = []
        self.signal = False
        self.sigval = 0
        self.dsem = None
        self.dval = 0
        self.waits = []


ENGS = ("pe", "act", "dve", "pool", "sp")
NDMASEM = 12


class Sched:
    def __init__(self):
        self.ops = []
        self.lastw = {}
        self.readers = {}
        self.per_eng = {e: [] for e in ENGS}
        self.dma_hist = {e: [] for e in ENGS}

    def add(self, eng, fn, reads=(), writes=(), dma=False, extra=()):
        op = Op(eng, fn, dma)
        deps = set(extra)
        for k in reads:
            w = self.lastw.get(k)
            if w is not None:
                deps.add(w)
        for k in writes:
            w = self.lastw.get(k)
            if w is not None:
                deps.add(w)
            for r in self.readers.get(k, ()):
                deps.add(r)
        for k in reads:
            self.readers.setdefault(k, []).append(op)
        for k in writes:
            self.lastw[k] = op
            self.readers[k] = []
        if dma:
            h = self.dma_hist[eng]
            n = len(h)
            op.dsem = (eng, n % NDMASEM)
            op.dval = 16 * (n // NDMASEM + 1)
            if n >= NDMASEM:
                deps.add(h[n - NDMASEM])
            h.append(op)
        deps.discard(op)
        op.deps = list(deps)
        op.pos = len(self.per_eng[eng])
        op.gi = len(self.ops)
        self.per_eng[eng].append(op)
        self.ops.append(op)
        return op

    def plan(self):
        maxpos = {e: {e2: -1 for e2 in ENGS} for e in ENGS}
        maxd = {e: {} for e in ENGS}
        for op in self.ops:
            e = op.eng
            for p in sorted(op.deps, key=lambda o: o.gi):
                if p.dma:
                    cur = maxd[e].get(p.dsem, 0)
                    if p.dval > cur:
                        maxd[e][p.dsem] = p.dval
                        op.waits.append(("d", p))
                else:
                    if p.eng == "pe" and e == "pe":
                        continue
                    if p.pos > maxpos[e][p.eng]:
                        maxpos[e][p.eng] = p.pos
                        p.signal = True
                        op.waits.append(("c", p))
        for e in ENGS:
            n = 0
            for op in self.per_eng[e]:
                if op.signal and not op.dma:
                    n += 1
                    op.sigval = n

    def emit(self, nc, engsem, dmasem, block):
        sched = self

        def run(engname, eng):
            for op in sched.per_eng[engname]:
                for kind, p in op.waits:
                    if kind == "d":
                        eng.wait_ge(dmasem[p.dsem], p.dval)
                    else:
                        eng.wait_ge(engsem[p.eng], p.sigval)
                if op.fn is None:
                    continue
                ins = op.fn(eng)
                if op.dma:
                    ins.then_inc(dmasem[op.dsem], 16)
                elif op.signal:
                    ins.then_inc(engsem[engname], 1)

        @block.tensor
        def _(pe):
            run("pe", pe)

        @block.scalar
        def _(act):
            run("act", act)

        @block.vector
        def _(dve):
            run("dve", dve)

        @block.gpsimd
        def _(pool):
            run("pool", pool)

        @block.sync
        def _(sp):
            run("sp", sp)


WNAMES = ("ffn1_w_gate", "ffn1_w_up", "ffn1_w_down", "w_in", "w_out",
          "ffn2_w_gate", "ffn2_w_up", "ffn2_w_down")
WSHAPES = {"ffn1_w_gate": (D, DFF), "ffn1_w_up": (D, DFF), "ffn1_w_down": (DFF, D),
           "w_in": (D, DIN), "w_out": (D, D),
           "ffn2_w_gate": (D, DFF), "ffn2_w_up": (D, DFF), "ffn2_w_down": (DFF, D)}


AH, ADK, ADV = 4, 128, 64
BH, BDK, BDV = 4, 128, 128
CH_, CDK, CDV = 4, 64, 64
O_TRI, O_SU, O_STRICT, O_INCL, O_DMAT, O_GDEC, O_GQ, O_PERM, O_INVF, O_ONES = 0, 64, 128, 384, 640, 896, 1152, 1408, 1472, 1473
C64W = O_ONES + 128
C1_RR = 6.28125
C2_RR = 2.0 * math.pi - 6.28125


SBUF_LEFT = [0]


def build_program(NT, DEPTH, enable=("A", "B", "C")):
    import contextlib
    S = NT * T
    nc = bass.Bass("TRN2", target_bir_lowering=False)
    dram = {}

    def din(name, shape, dt=F32):
        dram[name] = nc.dram_tensor(name, list(shape), dt, kind="ExternalInput").ap()
        return dram[name]

    x_d = din("x", (S, D))
    pos_d = din("pos", (1, S), I32)
    for wn in WNAMES:
        din(wn, (DEPTH,) + WSHAPES[wn])
    NNW = DEPTH * 3 * NKC + NKC
    din("normw", (128, NNW))
    din("c128", (128, 640))
    din("c64", (64, C64W))
    din("lbp", (128, 16))
    din("cw", (128, DEPTH * 48))
    din("hnorm", (1, DEPTH * 64))
    din("gnorm", (1, DEPTH * 128))
    din("alog", (1, DEPTH * 4))
    din("dtb", (1, DEPTH * 4))
    y_d = nc.dram_tensor("y", [S, D], F32, kind="ExternalOutput").ap()
    wbf = {wn: nc.dram_tensor("bf_" + wn, [DEPTH] + list(WSHAPES[wn]), BF16, kind="Internal").ap()
           for wn in WNAMES}

    sc = Sched()
    stack = contextlib.ExitStack()

    def salloc(name, shape, dt):
        return stack.enter_context(nc.sbuf_tensor("sb_" + name, list(shape), dt))

    g64 = [float(v) for v in np.exp(np.log1p(-np.exp2(-5.0 - np.arange(4, dtype=np.float32))).astype(np.float32) * np.float32(C)).astype(np.float32)]

    with stack:
        NCELL = 48
        arena = salloc("arena", [128, NCELL * 512], F32)
        xT = salloc("xT", [128, NKC, T], F32)
        hT = salloc("hT", [128, NKC, T], BF16)
        rstd = salloc("rstd", [128, T], F32)
        rtmp = salloc("rtmp", [128, T], F32)
        normw = salloc("normw", [128, NNW], F32)
        c128 = salloc("c128", [128, 640], F32)
        c64 = salloc("c64", [64, C64W], F32)
        identb = salloc("identb", [128, 128], BF16)
        onesb = salloc("onesb", [128, 128], BF16)
        cst = salloc("cst", [128, 4], F32)
        lbp = salloc("lbp", [128, 16], F32)
        lbs = salloc("lbs", [128, 16], F32)
        lbw = salloc("lbw", [128, 16], F32)
        oml = salloc("oml", [128, 16], F32)
        noml = salloc("noml", [128, 16], F32)
        lbm = salloc("lbm", [128, 8], F32)
        cw = salloc("cw", [128, DEPTH * 48], F32)
        hnorm = salloc("hnorm", [64, DEPTH * 64], F32)
        gnorm = salloc("gnorm", [64, DEPTH * 128], F32)
        negA = salloc("negA", [64, DEPTH * 4], F32)
        dtb = salloc("dtb", [64, DEPTH * 4], F32)
        dlA = salloc("dlA", [128, 32], F32)
        dlB = salloc("dlB", [128, 32], F32)
        sm = {n: salloc("sm_" + n, [64, 32], F32) for n in
              ("beta", "xa", "ea", "sp", "g", "gc", "egc", "bg", "tmpd", "est")}
        rs4 = salloc("rs4", [64, 64], F32)
        SA = [salloc("SA%d" % l, [128, 256], F32) for l in range(DEPTH)]
        SB = [salloc("SB%d" % l, [128, 512], F32) for l in range(DEPTH)]
        RC = [salloc("RC%d" % l, [64, 256], F32) for l in range(DEPTH)]
        SAb = [salloc("SAb%d" % l, [128, 256], BF16) for l in range(DEPTH)]
        SBb = [salloc("SBb%d" % l, [128, 512], BF16) for l in range(DEPTH)]
        RCb = [salloc("RCb%d" % l, [64, 256], BF16) for l in range(DEPTH)]
        halo = [salloc("halo%d" % l, [128, 36], F32) for l in range(DEPTH)]
        NWS = 8
        wslot = [salloc("wslot%d" % i, [128, 2048], BF16) for i in range(NWS)]
        ps = [stack.enter_context(nc.psum_tensor("ps%d" % i, [128, 512], F32)) for i in range(8)]
        engsem = {e: stack.enter_context(nc.semaphore("s_" + e)) for e in ENGS}
        dmasem = {}
        for e in ("sp", "pool"):
            for i in range(NDMASEM):
                dmasem[(e, i)] = stack.enter_context(nc.semaphore("d_%s%d" % (e, i)))
        block = stack.enter_context(nc.Block())

        ident = c128[:, 0:128]
        ident64 = c128[0:64, 0:64]
        scanmask = c128[:, 128:640]
        tri = c64[:, O_TRI:O_TRI + 64]
        su = c64[:, O_SU:O_SU + 64]
        onesf = c64[:, O_ONES:O_ONES + 128]

        def c64v(off):
            return c64[:, off:off + 256].rearrange("p (h n) -> p h n", h=4)

        def A(eng, method, *args, r=(), w=(), **kw):
            return sc.add(eng, lambda e: getattr(e, method)(*args, **kw), reads=r, writes=w)

        def cells(c0, n):
            return [("a", c) for c in range(c0, c0 + n)]

        def av(c0, n, dt=F32, parts=128):
            ap = arena[0:parts, c0 * 512:(c0 + n) * 512]
            if dt == BF16:
                ap = ap.bitcast(BF16)
            return ap

        def dma(q, out, in_, reads=(), writes=()):
            return sc.add(q, lambda eng: eng.dma_start(out=out, in_=in_), reads, writes, dma=True)

        wctr = [0]

        def load_w(wn, l, k0, nk, c0, ncol):
            i = wctr[0] % NWS
            wctr[0] += 1
            view = wslot[i][:, 0:nk * ncol].rearrange("p (k n) -> p k n", k=nk)
            src = wbf[wn][l, k0 * 128:(k0 + nk) * 128, c0:c0 + ncol].rearrange("(k p) n -> p k n", p=128)
            rk = [("bf", wn, l, r0) for r0 in range((k0 * 128) // 256 * 256, (k0 + nk) * 128, 256)]
            dma("sp", view, src, reads=rk, writes=("wslot%d" % i,))
            return view, "wslot%d" % i

        psctr = [0]

        def psbank(lo=0, hi=8):
            i = lo + psctr[0] % (hi - lo)
            psctr[0] += 1
            return ps[i], "ps%d" % i

        def mm(out, lhsT, rhs, start, stop, r, w):
            return A("pe", "matmul", out, lhsT=lhsT, rhs=rhs, start=start, stop=stop, r=r, w=w)

        def tp(out, in_, idn, r, w):
            return A("pe", "transpose", out, in_, idn, r=r, w=w)

        def act(out, in_, func, r, w, **kw):
            return A("act", "activation", out=out, in_=in_, func=func, r=r, w=w, **kw)

        hid = av(0, 11, BF16).rearrange("p (f t) -> p f t", f=NFC)
        sqv = av(11, 4, BF16).rearrange("p (c t) -> p c t", c=NKC)
        sgv = [av(15, 1), av(16, 1)]
        xin = [av(17, 2), av(19, 2)]
        yout = [av(21, 2), av(23, 2)]

        def fm4(c0, parts=128):
            return av(c0, 2, BF16, parts).rearrange("p (h t) -> p h t", h=4)
        qinA, ktA, qB, kB, vsb, qinB = fm4(0), fm4(2), fm4(4), fm4(6), fm4(8), fm4(10)
        qrb, krb, qinC = fm4(12, 64), fm4(14, 64), fm4(16, 64)
        cosT, sinT = av(18, 1, F32, 64), av(19, 1, F32, 64)
        mixedT = av(20, 4, BF16).rearrange("p (c t) -> p c t", c=NKC)
        K_MIXT = cells(20, 4)

        def tmp(i, parts=128):
            return av(24 + i, 1, F32, parts), ("a", 24 + i)

        dma("sp", normw[:], dram["normw"][:, :], writes=("normw",))
        dma("sp", c128[:], dram["c128"][:, :], writes=("c128",))
        dma("sp", c64[:], dram["c64"][:, :], writes=("c64",))
        dma("sp", lbp[:], dram["lbp"][:, :], writes=("lbp",))
        dma("sp", cw[:], dram["cw"][:, :], writes=("cw",))
        dma("sp", hnorm[:], dram["hnorm"][0:1, :].partition_broadcast(64), writes=("hnorm",))
        dma("sp", gnorm[:], dram["gnorm"][0:1, :].partition_broadcast(64), writes=("gnorm",))
        dma("sp", negA[:], dram["alog"][0:1, :].partition_broadcast(64), writes=("negA",))
        dma("sp", dtb[:], dram["dtb"][0:1, :].partition_broadcast(64), writes=("dtb",))
        A("dve", "tensor_copy", identb[:], ident, r=("c128",), w=("identb",))
        A("pool", "memset", onesb[:], 1.0, w=("onesb",))
        A("pool", "memset", cst[:, 0:1], EPS, w=("cst",))
        A("pool", "memset", cst[:, 1:2], 1.0, w=("cst",))
        A("pool", "memset", cst[:, 2:3], math.pi / 2, w=("cst",))
        A("pool", "memset", cst[:, 3:4], 0.0, w=("cst",))
        for l in range(DEPTH):
            A("pool", "memset", SA[l][:], 0.0, w=(("SA", l),))
            A("pool", "memset", SB[l][:], 0.0, w=(("SB", l),))
            A("pool", "memset", RC[l][:], 0.0, w=(("RC", l),))
            A("pool", "memset", SAb[l][:], 0.0, w=(("SAb", l),))
            A("pool", "memset", SBb[l][:], 0.0, w=(("SBb", l),))
            A("pool", "memset", RCb[l][:], 0.0, w=(("RCb", l),))
            A("pool", "memset", halo[l][:], 0.0, w=(("halo", l),))
        act(negA[:], negA[:], AF.Exp, r=("negA",), w=("negA",))
        A("dve", "tensor_scalar", negA[:], negA[:], -1.0, None, op0=ALU.mult, r=("negA",), w=("negA",))
        lb3 = lbp[:].rearrange("p (h l) -> p h l", h=4)
        A("dve", "tensor_reduce", lbm[:, 0:4], lb3, axis=AX.X, op=ALU.max, r=("lbp",), w=("lbm",))
        A("dve", "tensor_tensor", lbw[:].rearrange("p (h l) -> p h l", h=4), lb3,
          lbm[:, 0:4].unsqueeze(2).to_broadcast([128, 4, 4]), op=ALU.subtract, r=("lbp", "lbm"), w=("lbw",))
        act(lbw[:], lbw[:], AF.Exp, r=("lbw",), w=("lbw",))
        A("dve", "tensor_reduce", lbm[:, 4:8], lbw[:].rearrange("p (h l) -> p h l", h=4), axis=AX.X, op=ALU.add,
          r=("lbw",), w=("lbm",))
        A("dve", "reciprocal", lbm[:, 4:8], lbm[:, 4:8], r=("lbm",), w=("lbm",))
        A("dve", "tensor_tensor", lbw[:].rearrange("p (h l) -> p h l", h=4), lbw[:].rearrange("p (h l) -> p h l", h=4),
          lbm[:, 4:8].unsqueeze(2).to_broadcast([128, 4, 4]), op=ALU.mult, r=("lbw", "lbm"), w=("lbw",))
        lbs3 = lbs[:].rearrange("p (h l) -> p h l", h=4)
        lbw3 = lbw[:].rearrange("p (h l) -> p h l", h=4)
        A("pool", "memset", lbs[:], 0.0, w=("lbs",))
        for l in range(1, 4):
            A("dve", "tensor_tensor", lbs3[:, :, l], lbs3[:, :, l - 1], lbw3[:, :, l], op=ALU.add,
              r=("lbs", "lbw"), w=("lbs",))
        A("dve", "tensor_scalar", oml[:], lbs[:], -1.0, 1.0, op0=ALU.mult, op1=ALU.add, r=("lbs",), w=("oml",))
        A("dve", "tensor_scalar", noml[:], lbs[:], -1.0, None, op0=ALU.add, r=("lbs",), w=("noml",))
        for l in range(DEPTH):
            for wn in WNAMES:
                K_, N_ = WSHAPES[wn]
                for r0 in range(0, K_, 256):
                    r1 = min(K_, r0 + 256)
                    dma("pool", wbf[wn][l, r0:r1, :], dram[wn][l, r0:r1, :], writes=(("bf", wn, l, r0),))

        def rmsnorm():
            act(sqv.rearrange("p c t -> p (c t)"), xT[:].rearrange("p c t -> p (c t)"), AF.Square,
                r=("xT",), w=cells(11, 4))
            pb, pk = psbank()
            for c in range(NKC):
                mm(pb[:], onesb[:], sqv[:, c, :], c == 0, c == NKC - 1, r=cells(11, 4) + ["onesb"], w=(pk,))
            act(rtmp[:], pb[:], AF.Ln, r=(pk, "cst"), w=("rtmp",), bias=cst[:, 0:1], scale=1.0 / D)
            act(rstd[:], rtmp[:], AF.Exp, r=("rtmp",), w=("rstd",), scale=-0.5)

        def norm_apply(widx, out_tile, out_keys):
            for c in range(NKC):
                A("dve", "scalar_tensor_tensor", out=out_tile[:, c, :], in0=xT[:, c, :],
                  scalar=normw[:, widx + c:widx + c + 1], in1=rstd[:], op0=ALU.mult, op1=ALU.mult,
                  r=("xT", "rstd", "normw"), w=out_keys)

        def ffn(l, which):
            pre = "ffn%d_w_" % which
            widx = (l * 3 + (0 if which == 1 else 2)) * NKC
            rmsnorm()
            norm_apply(widx, hT, ("hT",))
            for blk in range(NFC // 2):
                wg, kg = load_w(pre + "gate", l, 0, NKC, blk * 256, 256)
                wu, ku = load_w(pre + "up", l, 0, NKC, blk * 256, 256)
                for sub in range(2):
                    f = blk * 2 + sub
                    pg, kpg = psbank(0, 4)
                    pu, kpu = psbank(0, 4)
                    for c in range(NKC):
                        mm(pg[:], wg[:, c, sub * 128:(sub + 1) * 128], hT[:, c, :], c == 0, c == NKC - 1,
                           r=("hT", kg), w=(kpg,))
                    for c in range(NKC):
                        mm(pu[:], wu[:, c, sub * 128:(sub + 1) * 128], hT[:, c, :], c == 0, c == NKC - 1,
                           r=("hT", ku), w=(kpu,))
                    sgt = sgv[f % 2]
                    ksg = ("a", 15 + f % 2)
                    act(sgt, pg[:], AF.Silu, r=(kpg,), w=(ksg,))
                    A("dve", "tensor_tensor", hid[:, f, :], pu[:], sgt, op=ALU.mult, r=(kpu, ksg),
                      w=(("a", f // 2),))
            for half in range(2):
                banks = [(ps[4 + i], "ps%d" % (4 + i)) for i in range(4)]
                for f0 in range(0, NFC, 4):
                    nf = min(4, NFC - f0)
                    wd, kd = load_w(pre + "down", l, f0, nf, half * 512, 512)
                    for fi in range(nf):
                        f = f0 + fi
                        for dci in range(4):
                            pb, pk = banks[dci]
                            mm(pb[:], wd[:, fi, dci * 128:(dci + 1) * 128], hid[:, f, :], f == 0, f == NFC - 1,
                               r=(("a", f // 2), kd), w=(pk,))
                for dci in range(4):
                    dc = half * 4 + dci
                    pb, pk = banks[dci]
                    A("dve", "scalar_tensor_tensor", out=xT[:, dc, :], in0=pb[:], scalar=0.5, in1=xT[:, dc, :],
                      op0=ALU.mult, op1=ALU.add, r=(pk, "xT"), w=("xT",))

        def load_tile(t):
            for b in range(T // 128):
                xi = xin[b % 2]
                xk = cells(17 + 2 * (b % 2), 2)
                r0 = t * T + b * 128
                dma("sp", xi, x_d[r0:r0 + 128, :], writes=xk)
                for g in range(2):
                    pb, pk = psbank()
                    for j in range(4):
                        c = g * 4 + j
                        tp(pb[:, j * 128:(j + 1) * 128], xi[:, c * 128:(c + 1) * 128], ident, r=xk + ["c128"], w=(pk,))
                    act(xT[:, g * 4:(g + 1) * 4, b * 128:(b + 1) * 128], pb[:].rearrange("p (j n) -> p j n", j=4),
                        AF.Copy, r=(pk,), w=("xT",))
            posi, kpi = tmp(0, 64)
            posf, kpf = tmp(1, 64)
            kq, kkq = tmp(2, 64)
            ang, kang = tmp(3, 64)
            s1, ks1 = tmp(4, 64)
            c1, kc1 = tmp(5, 64)
            invf = c64[:, O_INVF:O_INVF + 1]
            dma("sp", posi.bitcast(I32), pos_d[0:1, t * T:(t + 1) * T].partition_broadcast(64), writes=(kpi,))
            A("dve", "tensor_copy", posf, posi.bitcast(I32), r=(kpi,), w=(kpf,))
            A("dve", "tensor_scalar", kq, posf, invf, 1.0 / (2 * math.pi), op0=ALU.mult, op1=ALU.mult,
              r=(kpf, "c64"), w=(kkq,))
            A("dve", "tensor_copy", posi.bitcast(I32), kq, r=(kkq,), w=(kpi,))
            A("dve", "tensor_copy", kq, posi.bitcast(I32), r=(kpi,), w=(kkq,))
            A("dve", "tensor_scalar", ang, posf, invf, None, op0=ALU.mult, r=(kpf, "c64"), w=(kang,))
            A("dve", "scalar_tensor_tensor", out=ang, in0=kq, scalar=-C1_RR, in1=ang, op0=ALU.mult, op1=ALU.add,
              r=(kkq, kang), w=(kang,))
            A("dve", "scalar_tensor_tensor", out=ang, in0=kq, scalar=-C2_RR, in1=ang, op0=ALU.mult, op1=ALU.add,
              r=(kkq, kang), w=(kang,))
            A("dve", "tensor_scalar", ang, ang, 0.25, None, op0=ALU.mult, r=(kang,), w=(kang,))
            act(s1, ang, AF.Sin, r=(kang,), w=(ks1,))
            act(c1, ang, AF.Sin, r=(kang, "cst"), w=(kc1,), bias=cst[0:64, 2:3])
            s2, c2 = posf, kq
            A("dve", "scalar_tensor_tensor", out=s2, in0=s1, scalar=2.0, in1=c1, op0=ALU.mult, op1=ALU.mult,
              r=(ks1, kc1), w=(kpf,))
            A("dve", "tensor_tensor", c2, s1, s1, op=ALU.mult, r=(ks1,), w=(kkq,))
            A("dve", "tensor_scalar", c2, c2, -2.0, 1.0, op0=ALU.mult, op1=ALU.add, r=(kkq,), w=(kkq,))
            A("dve", "scalar_tensor_tensor", out=sinT, in0=s2, scalar=2.0, in1=c2, op0=ALU.mult, op1=ALU.mult,
              r=(kpf, kkq), w=(("a", 19),))
            A("dve", "tensor_tensor", cosT, s2, s2, op=ALU.mult, r=(kpf,), w=(("a", 18),))
            A("dve", "tensor_scalar", cosT, cosT, -2.0, 1.0, op0=ALU.mult, op1=ALU.add, r=(("a", 18),), w=(("a", 18),))

        outs = []

        def store_tile(t):
            widx = DEPTH * 3 * NKC
            rmsnorm()
            norm_apply(widx, xT, ("xT",))
            for b in range(T // 128):
                yo = yout[b % 2]
                yk = cells(21 + 2 * (b % 2), 2)
                for g in range(2):
                    pb, pk = psbank()
                    for j in range(4):
                        c = g * 4 + j
                        tp(pb[:, j * 128:(j + 1) * 128], xT[:, c, b * 128:(b + 1) * 128], ident, r=("xT", "c128"), w=(pk,))
                    act(yo[:, g * 512:(g + 1) * 512], pb[:], AF.Copy, r=(pk,), w=yk)
                r0 = t * T + b * 128
                outs.append(dma("sp", y_d[r0:r0 + 128, :], yo, reads=yk))

        def subt(cell, off, n, dt=F32, parts=64, tag=0):
            ap = arena[0:parts, (24 + cell) * 512 + off:(24 + cell) * 512 + off + n]
            if dt == BF16:
                ap = ap.bitcast(BF16)
            return ap, ("t", cell, tag)

        class Unit:
            def __init__(self, name, gen, pre=()):
                self.name, self.gen, self.pre = name, gen, set(pre)

        def run_units(units, width):
            done = set()
            pending = list(units)
            active = []
            while pending or active:
                i = 0
                while len(active) < width and i < len(pending):
                    if pending[i].pre <= done:
                        active.append(pending.pop(i))
                    else:
                        i += 1
                assert active, "unit prerequisites can never be met"
                for u in list(active):
                    try:
                        next(u.gen)
                    except StopIteration:
                        active.remove(u)
                        done.add(u.name)

        class PsPool:
            def __init__(self, lo, hi):
                self.lo, self.hi, self.n = lo, hi, 0

            def get(self):
                i = self.lo + self.n % (self.hi - self.lo)
                self.n += 1
                return ps[i], "ps%d" % i

        bridge_t = salloc("bridge", [128, 2], F32)

        def bridge(keys):
            A("pool", "memset", bridge_t[:, 0:1], 0.0, r=(), w=list(keys) + ["bridge_t"])

        def head_out(po, pk, gw, kgw, H, dv, c0mix, ch, osq, kosq, omb, komix, pool):
            W = H * dv
            act(osq[:, 0:W], po[0:64, 0:W], AF.Square, r=(pk,), w=(kosq,))
            rs = rs4[:, c0mix * 8:c0mix * 8 + 16]
            krs = ("rs4", c0mix)
            A("dve", "tensor_reduce", rs[:, 0:4], osq[:, 0:W].rearrange("p (h v) -> p h v", h=H), axis=AX.X,
              op=ALU.add, r=(kosq,), w=(krs,))
            act(rs[:, 4:8], rs[:, 0:4], AF.Ln, r=(krs, "cst"), w=(krs,), bias=cst[0:64, 0:1], scale=1.0 / dv)
            act(rs[:, 8:12], rs[:, 4:8], AF.Exp, r=(krs,), w=(krs,), scale=-0.5)
            yield
            for h in range(H):
                A("dve", "scalar_tensor_tensor", out=omb[:, h * dv:(h + 1) * dv], in0=po[0:64, h * dv:(h + 1) * dv],
                  scalar=rs[:, 8 + h:9 + h], in1=gw[:, h * dv:(h + 1) * dv], op0=ALU.mult, op1=ALU.mult,
                  r=(pk, krs, kgw), w=(komix,))
            pb, pkb = pool.get()
            pbb = pb[:].bitcast(BF16)
            n = W // 128
            for j in range(n):
                tp(pbb[:, j * 64:(j + 1) * 64], omb[:, j * 128:(j + 1) * 128], identb[0:64, 0:64],
                   r=(komix, "identb"), w=(pkb,))
            act(mixedT[:, c0mix:c0mix + n, ch * C:(ch + 1) * C],
                pbb[:, 0:n * 64].rearrange("p (j n) -> p j n", j=n), AF.Copy, r=(pkb,), w=[("mixT", c0mix, ch)])

        def mixer(l):
            widx = (l * 3 + 1) * NKC
            rmsnorm()
            norm_apply(widx, hT, ("hT",))
            if len(enable) < 3:
                A("pool", "memset", mixedT.rearrange("p c t -> p (c t)"), 0.0, w=K_MIXT)

            def fmA(h, slot):
                wq, kwq = load_w("w_in", l, 0, NKC, h * 128, 128)
                wf, kwf = load_w("w_in", l, 0, NKC, 512 + h * 128, 128)
                pq, kpq = psbank()
                pf, kpf_ = psbank()
                for c in range(NKC):
                    mm(pq[:], wq[:, c, :], hT[:, c, :], c == 0, c == NKC - 1, r=("hT", kwq), w=(kpq,))
                for c in range(NKC):
                    mm(pf[:], wf[:, c, :], hT[:, c, :], c == 0, c == NKC - 1, r=("hT", kwf), w=(kpf_,))
                b0 = slot * 6
                (tq, ktq), (ts, kts), (tk, ktk), (tf, ktf), (teb, kteb), (tenb, ktenb) = [tmp(b0 + i) for i in range(6)]
                li = h * 4 + l
                act(tq, pq[:], AF.Silu, r=(kpq,), w=(ktq,))
                act(ts, pf[:], AF.Sigmoid, r=(kpf_,), w=(kts,))
                yield
                A("dve", "tensor_scalar", tf, ts, oml[:, li:li + 1], lbs[:, li:li + 1], op0=ALU.mult, op1=ALU.add,
                  r=(kts, "oml", "lbs"), w=(ktf,))
                A("dve", "tensor_scalar", tf, tf, 1e-20, None, op0=ALU.max, r=(ktf,), w=(ktf,))
                act(tf, tf, AF.Ln, r=(ktf,), w=(ktf,))
                A("dve", "tensor_scalar", tk, ts, noml[:, li:li + 1], oml[:, li:li + 1], op0=ALU.mult, op1=ALU.add,
                  r=(kts, "oml", "noml"), w=(ktk,))
                yield
                A("dve", "tensor_tensor_scan", ts, scanmask, tf, 0.0, op0=ALU.mult, op1=ALU.add,
                  r=(ktf, "c128"), w=(kts,))
                act(teb, ts, AF.Exp, r=(kts,), w=(kteb,))
                act(tenb, ts, AF.Exp, r=(kts,), w=(ktenb,), scale=-1.0)
                yield
                A("dve", "tensor_tensor", qinA[:, h, :], tq, teb, op=ALU.mult, r=(ktq, kteb), w=[("qinA", h)])
                A("dve", "tensor_tensor", ktA[:, h, :], tk, tenb, op=ALU.mult, r=(ktk, ktenb), w=[("ktA", h)])
                A("dve", "tensor_copy", dlA[:, h * 8:(h + 1) * 8],
                  teb.rearrange("p (c j) -> p c j", j=C)[:, :, C - 1], r=(kteb,), w=[("dlA", h)])

            def fmB(cidx, slot):
                wv_, kwv = load_w("w_in", l, 0, NKC, 1536 + cidx * 128, 128)
                pc, kpc = psbank()
                for c in range(NKC):
                    mm(pc[:], wv_[:, c, :], hT[:, c, :], c == 0, c == NKC - 1, r=("hT", kwv), w=(kpc,))
                b0 = 12 + slot * 4
                cb = av(24 + b0, 2)
                kcb = cells(24 + b0, 2)
                acc, kacc = tmp(b0 + 2)
                tm_, ktm = tmp(b0 + 3)
                hk = ("halo", l, cidx)
                A("pool", "tensor_copy", cb[:, 0:3], halo[l][:, cidx * 3:cidx * 3 + 3], r=(("halo", l), hk), w=kcb)
                act(cb[:, 3:3 + T], pc[:], AF.Copy, r=(kpc,), w=kcb)
                A("pool", "tensor_copy", halo[l][:, cidx * 3:cidx * 3 + 3], cb[:, T:T + 3], r=kcb, w=(hk,))
                yield
                cwb = l * 48 + cidx * 4
                A("dve", "tensor_scalar", acc, cb[:, 0:T], cw[:, cwb:cwb + 1], None, op0=ALU.mult,
                  r=kcb + ["cw"], w=(kacc,))
                for w_ in range(1, 4):
                    A("dve", "scalar_tensor_tensor", out=acc, in0=cb[:, w_:w_ + T], scalar=cw[:, cwb + w_:cwb + w_ + 1],
                      in1=acc, op0=ALU.mult, op1=ALU.add, r=kcb + ["cw", kacc], w=(kacc,))
                act(tm_, acc, AF.Silu, r=(kacc,), w=(ktm,))
                yield
                h = cidx % 4
                if cidx < 8:
                    sqb = acc.bitcast(BF16)[:, 0:T]
                    act(sqb, tm_, AF.Square, r=(ktm,), w=(kacc,))
                    pn, kpn = psbank()
                    mm(pn[:], onesb[:], sqb, True, True, r=(kacc, "onesb"), w=(kpn,))
                    yield
                    r1 = cb[:, 0:T]
                    r2 = cb[:, T:2 * T]
                    act(r1, pn[:], AF.Ln, r=(kpn, "cst"), w=kcb, bias=cst[:, 0:1])
                    act(r2, r1, AF.Exp, r=kcb, w=kcb, scale=-0.5)
                    yield
                    dst, kd_ = (qB, ("qB", h)) if cidx < 4 else (kB, ("kB", h))
                    A("dve", "scalar_tensor_tensor", out=dst[:, h, :], in0=tm_,
                      scalar=(BDK ** -0.5 if cidx < 4 else 1.0), in1=r2, op0=ALU.mult, op1=ALU.mult,
                      r=[ktm] + kcb, w=[kd_])
                else:
                    A("dve", "tensor_copy", vsb[:, h, :], tm_, r=(ktm,), w=[("vsb", h)])

            def s3(n):
                return sm[n][:].rearrange("p (c h) -> p c h", h=4)

            def fmBsmall():
                wt, kwt = load_w("w_in", l, 0, NKC, 3584, 8)
                pba, kpba = psbank()
                for ch in range(NCH):
                    for c in range(NKC):
                        mm(pba[0:64, ch * 8:(ch + 1) * 8], hT[:, c, ch * C:(ch + 1) * C], wt[:, c, :], c == 0, c == NKC - 1,
                           r=("hT", kwt), w=(kpba,))
                pba3 = pba[0:64, 0:64].rearrange("p (c n) -> p c n", n=8)
                act(s3("beta"), pba3[:, :, 0:4], AF.Sigmoid, r=(kpba,), w=("sm_beta",))
                A("dve", "tensor_tensor", s3("xa"), pba3[:, :, 4:8],
                  dtb[:, l * 4:(l + 1) * 4].unsqueeze(1).to_broadcast([64, NCH, 4]), op=ALU.add,
                  r=(kpba, "dtb"), w=("sm_xa",))
                act(sm["ea"][:], sm["xa"][:], AF.Exp, r=("sm_xa",), w=("sm_ea",))
                act(sm["sp"][:], sm["ea"][:], AF.Ln, r=("sm_ea", "cst"), w=("sm_sp",), bias=cst[0:64, 1:2])
                A("dve", "tensor_tensor", s3("g"), s3("sp"),
                  negA[:, l * 4:(l + 1) * 4].unsqueeze(1).to_broadcast([64, NCH, 4]), op=ALU.mult,
                  r=("sm_sp", "negA"), w=("sm_g",))
                yield
                pg1, kpg1 = psbank()
                mm(pg1[0:64, 0:32], tri, sm["g"][:], True, True, r=("c64", "sm_g"), w=(kpg1,))
                mm(pg1[0:64, 32:64], onesf[:, 0:64], sm["g"][:], True, True, r=("c64", "sm_g"), w=(kpg1,))
                pg2, kpg2 = psbank()
                mm(pg2[:, 0:32], onesf, sm["g"][:], True, True, r=("c64", "sm_g"), w=(kpg2,))
                yield
                act(sm["gc"][:], pg1[0:64, 0:32], AF.Copy, r=(kpg1,), w=("sm_gc",))
                act(sm["egc"][:], pg1[0:64, 0:32], AF.Exp, r=(kpg1,), w=("sm_egc",))
                A("dve", "tensor_tensor", sm["bg"][:], sm["beta"][:], sm["egc"][:], op=ALU.mult,
                  r=("sm_beta", "sm_egc"), w=("sm_bg",))
                A("dve", "tensor_tensor", sm["tmpd"][:], pg1[0:64, 32:64], sm["gc"][:], op=ALU.subtract,
                  r=(kpg1, "sm_gc"), w=("sm_tmpd",))
                act(sm["est"][:], sm["tmpd"][:], AF.Exp, r=("sm_tmpd",), w=("sm_est",))
                act(dlB[:], pg2[:, 0:32], AF.Exp, r=(kpg2,), w=("dlB",))

            def fmC(qk, h, slot):
                perm = c64[:, O_PERM:O_PERM + 64]
                wc_, kwc = load_w("w_in", l, 0, NKC, 3592 + qk * 256 + h * 64, 64)
                px, kpx = psbank()
                for c in range(NKC):
                    mm(px[0:64, :], wc_[:, c, :], hT[:, c, :], c == 0, c == NKC - 1, r=("hT", kwc), w=(kpx,))
                b0 = 6 + slot * 3
                (xf, kxf), (t1, kt1), (t2, kt2) = [tmp(b0 + i, 64) for i in range(3)]
                act(xf, px[0:64, :], AF.Copy, r=(kpx,), w=(kxf,), scale=(1.0 if qk == 0 else CDK ** -0.5))
                yield
                pr, kpr = psbank()
                mm(pr[0:64, :], perm, xf, True, True, r=("c64", kxf), w=(kpr,))
                A("dve", "tensor_tensor", t1, xf, cosT, op=ALU.mult, r=(kxf, ("a", 18)), w=(kt1,))
                yield
                A("dve", "tensor_tensor", t2, pr[0:64, :], sinT, op=ALU.mult, r=(kpr, ("a", 19)), w=(kt2,))
                A("dve", "tensor_tensor", t1, t1, t2, op=ALU.add, r=(kt1, kt2), w=(kt1,))
                yield
                if qk == 0:
                    act(qrb[:, h, :], t1, AF.Copy, r=(kt1,), w=[("qrb", h)])
                    A("dve", "tensor_tensor", qinC[:, h, :].rearrange("p (c i) -> p c i", i=C),
                      t1.rearrange("p (c i) -> p c i", i=C),
                      c64[:, O_GQ + h * 64:O_GQ + (h + 1) * 64].unsqueeze(1).to_broadcast([64, NCH, C]),
                      op=ALU.mult, r=(kt1, "c64"), w=[("qinC", h)])
                else:
                    act(krb[:, h, :], t1, AF.Copy, r=(kt1,), w=[("krb", h)])

            ALLFINE = ([(n_, h) for n_ in ("qinA", "ktA", "qB", "kB", "vsb", "qrb", "krb", "qinC", "dlA") for h in range(4)] +
                       [("qinB", ch) for ch in range(NCH)] + [("mixT", c0, ch) for c0 in (0, 2, 6) for ch in range(NCH)] +
                       [("t", c_, tg) for c_ in range(NCELL - 24) for tg in range(3)])
            ALLKEYS = cells(0, NCELL) + ALLFINE
            bridge(ALLKEYS)

            units = []
            if "A" in enable:
                units += [Unit("A%d" % h, fmA(h, h % 2), ["A%d" % (h - 2)] if h >= 2 else []) for h in range(4)]
            if "B" in enable:
                units += [Unit("B%d" % ci, fmB(ci, ci % 2), ["B%d" % (ci - 2)] if ci >= 2 else []) for ci in range(12)]
                units.append(Unit("Bs", fmBsmall()))
            if "C" in enable:
                allA = ["A%d" % h for h in range(4)] if "A" in enable else []
                for i in range(8):
                    units.append(Unit("C%d" % i, fmC(i // 4, i % 4, i % 2), allA + (["C%d" % (i - 2)] if i >= 2 else [])))
            run_units(units, 3)
            bridge(ALLKEYS)

            wTM = {}
            for which, c0 in (("A", 1024), ("B", 3072), ("C", 4104)):
                if which in enable:
                    wTM[which] = [load_w("w_in", l, 0, NKC, c0 + i * 256, 256) for i in range(2)]

            poolBp, poolBs, poolAC = PsPool(0, 3), PsPool(3, 6), PsPool(6, 8)

            def tm_proj(which, ch, pool):
                pb, pk = pool.get()
                for i in range(2):
                    wv_, kw_ = wTM[which][i]
                    for c in range(NKC):
                        mm(pb[0:64, i * 256:(i + 1) * 256], hT[:, c, ch * C:(ch + 1) * C], wv_[:, c, :],
                           c == 0, c == NKC - 1, r=("hT", kw_), w=(pk,))
                return pb, pk

            K_QINA = [("qinA", h) for h in range(4)]
            K_KTA = [("ktA", h) for h in range(4)]
            K_QB = [("qB", h) for h in range(4)]
            K_KB = [("kB", h) for h in range(4)]
            K_VSB = [("vsb", h) for h in range(4)]
            K_QRB = [("qrb", h) for h in range(4)]
            K_KRB = [("krb", h) for h in range(4)]
            K_QINC = [("qinC", h) for h in range(4)]
            K_DLA = [("dlA", h) for h in range(4)]

            def chA(ch):
                csl = slice(ch * C, (ch + 1) * C)
                pat, kpat = tm_proj("A", ch, poolAC)
                gwA, kgwA = subt(0, 0, 256, tag=0)
                vA, kvA = subt(0, 256, 128, BF16, tag=1)
                ktTM, kkt = subt(1, 0, 256, BF16, tag=0)
                PT, kpt = subt(1, 256, 128, BF16, tag=1)
                tS, ktS = subt(2, 0, 256, parts=128)
                osq, kosq = subt(3, 0, 256, tag=0)
                omb, komix = subt(3, 256, 128, BF16, tag=1)
                act(vA, pat[0:64, 0:256], AF.Copy, r=(kpat,), w=(kvA,))
                act(gwA, pat[0:64, 256:512], AF.Silu, r=(kpat,), w=(kgwA,))
                A("dve", "tensor_tensor", gwA.rearrange("p (h v) -> p h v", h=4),
                  gwA.rearrange("p (h v) -> p h v", h=4),
                  hnorm[:, l * 64:(l + 1) * 64].unsqueeze(1).to_broadcast([64, 4, 64]), op=ALU.mult,
                  r=(kgwA, "hnorm"), w=(kgwA,))
                yield
                pkt, kpkt = poolAC.get()
                pktb = pkt[0:64, :].bitcast(BF16)
                for h in range(4):
                    tp(pktb[:, h * 128:(h + 1) * 128], ktA[:, h, csl], identb[:], r=K_KTA + ["identb"], w=(kpkt,))
                act(ktTM, pktb[:, 0:512], AF.Copy, r=(kpkt,), w=(kkt,))
                pa, kpa = poolAC.get()
                for h in range(4):
                    mm(pa[0:64, h * 64:(h + 1) * 64], ktA[:, h, csl], qinA[:, h, csl], True, True,
                       r=K_KTA + K_QINA, w=(kpa,))
                A("dve", "tensor_tensor", PT.rearrange("p (h n) -> p h n", h=4),
                  pa[0:64, 0:256].rearrange("p (h n) -> p h n", h=4),
                  tri.unsqueeze(1).to_broadcast([64, 4, 64]), op=ALU.mult, r=(kpa, "c64"), w=(kpt,))
                yield
                pu_, kpu_ = poolAC.get()
                for h in range(4):
                    mm(pu_[:, h * 64:(h + 1) * 64], ktTM[:, h * 128:(h + 1) * 128], vA[:, h * 64:(h + 1) * 64],
                       True, True, r=(kkt, kvA), w=(kpu_,))
                po, kpo = poolAC.get()
                for h in range(4):
                    mm(po[0:64, h * 64:(h + 1) * 64], qinA[:, h, csl], SAb[l][:, h * 64:(h + 1) * 64], True, False,
                       r=K_QINA + [("SAb", l)], w=(kpo,))
                    mm(po[0:64, h * 64:(h + 1) * 64], PT[:, h * 64:(h + 1) * 64], vA[:, h * 64:(h + 1) * 64], False, True,
                       r=(kpt, kvA), w=(kpo,))
                yield
                A("dve", "tensor_tensor", tS, pu_[:, 0:256], SA[l][:], op=ALU.add, r=(kpu_, ("SA", l)), w=(ktS,))
                A("dve", "tensor_tensor", SA[l][:].rearrange("p (h v) -> p h v", h=4),
                  tS.rearrange("p (h v) -> p h v", h=4),
                  dlA[:].rearrange("p (h c) -> p h c", h=4)[:, :, ch:ch + 1].to_broadcast([128, 4, 64]),
                  op=ALU.mult, r=[ktS] + K_DLA, w=(("SA", l),))
                act(SAb[l][:], SA[l][:], AF.Copy, r=(("SA", l),), w=(("SAb", l),))
                yield
                yield from head_out(po, kpo, gwA, kgwA, 4, 64, 0, ch, osq, kosq, omb, komix, poolAC)

            def chC(ch):
                csl = slice(ch * C, (ch + 1) * C)
                pct, kpct = tm_proj("C", ch, poolAC)
                gwC, kgwC = subt(4, 0, 256, tag=0)
                vC, kvC = subt(4, 256, 128, BF16, tag=1)
                kst, kkt = subt(5, 0, 128, BF16, tag=0)
                PT, kpt = subt(5, 128, 128, BF16, tag=1)
                osq, kosq = subt(6, 0, 256, tag=0)
                omb, komix = subt(6, 256, 128, BF16, tag=1)
                act(vC, pct[0:64, 0:256], AF.Copy, r=(kpct,), w=(kvC,))
                act(gwC, pct[0:64, 256:512], AF.Silu, r=(kpct,), w=(kgwC,))
                yield
                pkt, kpkt = poolAC.get()
                pktb = pkt[0:64, :].bitcast(BF16)
                for h in range(4):
                    tp(pktb[:, h * 64:(h + 1) * 64], krb[:, h, csl], identb[0:64, 0:64], r=K_KRB + ["identb"], w=(kpkt,))
                A("dve", "tensor_tensor", kst, pktb[:, 0:256], c64[:, O_GDEC:O_GDEC + 256], op=ALU.mult,
                  r=(kpkt, "c64"), w=(kkt,))
                pa, kpa = poolAC.get()
                for h in range(4):
                    mm(pa[0:64, h * 64:(h + 1) * 64], krb[:, h, csl], qrb[:, h, csl], True, True,
                       r=K_KRB + K_QRB, w=(kpa,))
                A("dve", "tensor_tensor", PT, pa[0:64, 0:256], c64[:, O_DMAT:O_DMAT + 256], op=ALU.mult,
                  r=(kpa, "c64"), w=(kpt,))
                yield
                pu_, kpu_ = poolAC.get()
                for h in range(4):
                    mm(pu_[0:64, h * 64:(h + 1) * 64], kst[:, h * 64:(h + 1) * 64], vC[:, h * 64:(h + 1) * 64],
                       True, True, r=(kkt, kvC), w=(kpu_,))
                po, kpo = poolAC.get()
                for h in range(4):
                    mm(po[0:64, h * 64:(h + 1) * 64], qinC[:, h, csl], RCb[l][:, h * 64:(h + 1) * 64], True, False,
                       r=K_QINC + [("RCb", l)], w=(kpo,))
                    mm(po[0:64, h * 64:(h + 1) * 64], PT[:, h * 64:(h + 1) * 64], vC[:, h * 64:(h + 1) * 64], False, True,
                       r=(kpt, kvC), w=(kpo,))
                yield
                for h in range(4):
                    A("dve", "scalar_tensor_tensor", out=RC[l][:, h * 64:(h + 1) * 64], in0=RC[l][:, h * 64:(h + 1) * 64],
                      scalar=g64[h], in1=pu_[0:64, h * 64:(h + 1) * 64], op0=ALU.mult, op1=ALU.add,
                      r=(kpu_, ("RC", l)), w=(("RC", l),))
                act(RCb[l][:], RC[l][:], AF.Copy, r=(("RC", l),), w=(("RCb", l),))
                yield
                yield from head_out(po, kpo, gwC, kgwC, 4, 64, 6, ch, osq, kosq, omb, komix, poolAC)

            def hand(ch):
                p = ch % 2
                gwB, kgwB = subt(7 + p * 3, 0, 512, tag=0)
                uB, kuB = subt(8 + p * 3, 0, 512, tag=0)
                kst, kkst = subt(9 + p * 3, 0, 256, BF16, tag=0)
                attnT, kaT = subt(9 + p * 3, 256, 128, BF16, tag=1)
                wTb, kwT = subt(9 + p * 3, 384, 128, BF16, parts=128, tag=2)
                return gwB, kgwB, uB, kuB, kst, kkst, attnT, kaT, wTb, kwT

            def s3c(n, ch):
                return sm[n][:].rearrange("p (c h) -> p c h", h=4)[:, ch, :]

            def chBpre(ch):
                csl = slice(ch * C, (ch + 1) * C)
                gwB, kgwB, uB, kuB, kst, kkst, attnT, kaT, wTb, kwT = hand(ch)
                kbg, kkbg = subt(13, 0, 256, BF16, tag=0)
                bv, kbv = subt(13, 256, 256, BF16, tag=1)
                gt, kgt = subt(14, 0, 256, tag=0)
                At_, kAt = subt(14, 256, 256, tag=1)
                E_, kE = subt(15, 0, 512)
                MQ = [subt(16, 0, 512), subt(17, 0, 512)]
                dg, kdg = subt(18, 0, 256, tag=0)
                TmT, kTm = subt(18, 256, 128, BF16, tag=1)
                Rb = [subt(19, 0, 256, tag=0), subt(19, 256, 256, tag=1)]

                def bc128(n):
                    return s3c(n, ch).unsqueeze(2).to_broadcast([64, 4, 128])
                pbt, kpbt = tm_proj("B", ch, poolBp)
                act(gwB, pbt[0:64, :], AF.Silu, r=(kpbt,), w=(kgwB,))
                A("dve", "tensor_tensor", gwB.rearrange("p (h v) -> p h v", h=4),
                  gwB.rearrange("p (h v) -> p h v", h=4),
                  gnorm[:, l * 128:(l + 1) * 128].unsqueeze(1).to_broadcast([64, 4, 128]), op=ALU.mult,
                  r=(kgwB, "gnorm"), w=(kgwB,))
                yield
                pkt, kpkt = poolBp.get()
                pktb = pkt[0:64, :].bitcast(BF16)
                for h in range(4):
                    tp(pktb[:, h * 128:(h + 1) * 128], kB[:, h, csl], identb[:], r=K_KB + ["identb"], w=(kpkt,))
                A("dve", "tensor_tensor", kbg.rearrange("p (h n) -> p h n", h=4),
                  pktb[:, 0:512].rearrange("p (h n) -> p h n", h=4), bc128("bg"), op=ALU.mult,
                  r=(kpkt, "sm_bg"), w=(kkbg,))
                A("dve", "tensor_tensor", kst.rearrange("p (h n) -> p h n", h=4),
                  pktb[:, 0:512].rearrange("p (h n) -> p h n", h=4), bc128("est"), op=ALU.mult,
                  r=(kpkt, "sm_est"), w=(kkst,))
                pvt, kpvt = poolBp.get()
                pvtb = pvt[0:64, :].bitcast(BF16)
                for h in range(4):
                    tp(pvtb[:, h * 128:(h + 1) * 128], vsb[:, h, csl], identb[:], r=K_VSB + ["identb"], w=(kpvt,))
                A("dve", "tensor_tensor", bv.rearrange("p (h n) -> p h n", h=4),
                  pvtb[:, 0:512].rearrange("p (h n) -> p h n", h=4), bc128("beta"), op=ALU.mult,
                  r=(kpvt, "sm_beta"), w=(kbv,))
                yield
                pkk, kpkk = poolBp.get()
                for h in range(4):
                    mm(pkk[0:64, h * 64:(h + 1) * 64], kB[:, h, csl], kB[:, h, csl], True, True, r=K_KB, w=(kpkk,))
                for h in range(4):
                    mm(pkk[0:64, 256 + h * 64:256 + (h + 1) * 64], qB[:, h, csl], kB[:, h, csl], True, True,
                       r=K_QB + K_KB, w=(kpkk,))
                A("dve", "tensor_tensor", gt.rearrange("p (h n) -> p h n", h=4),
                  tri.unsqueeze(1).to_broadcast([64, 4, 64]),
                  s3c("g", ch).unsqueeze(2).to_broadcast([64, 4, 64]), op=ALU.mult, r=("c64", "sm_g"), w=(kgt,))
                pdf, kpdf = poolBp.get()
                for h in range(4):
                    mm(pdf[0:64, h * 64:(h + 1) * 64], gt[:, h * 64:(h + 1) * 64], su, True, True,
                       r=(kgt, "c64"), w=(kpdf,))
                act(E_[:, 0:256], pdf[0:64, 0:256], AF.Exp, r=(kpdf,), w=(kE,))
                yield
                A("dve", "tensor_tensor", E_[:, 256:512], E_[:, 0:256], c64[:, O_INCL:O_INCL + 256], op=ALU.mult,
                  r=(kE, "c64"), w=(kE,))
                A("dve", "tensor_tensor", E_[:, 0:256], E_[:, 0:256], c64[:, O_STRICT:O_STRICT + 256], op=ALU.mult,
                  r=(kE, "c64"), w=(kE,))
                Q0 = MQ[0][0][:, 256:512]
                for h in range(4):
                    A("dve", "scalar_tensor_tensor", out=Q0[:, h * 64:(h + 1) * 64], in0=pkk[0:64, h * 64:(h + 1) * 64],
                      scalar=s3c("beta", ch)[:, h:h + 1], in1=E_[:, h * 64:(h + 1) * 64], op0=ALU.mult, op1=ALU.mult,
                      r=(kpkk, "sm_beta", kE), w=(MQ[0][1],))
                A("dve", "tensor_tensor", At_, pkk[0:64, 256:512], E_[:, 256:512], op=ALU.mult,
                  r=(kpkk, kE), w=(kAt,))
                yield
                ptr, kptr = poolBp.get()
                for h in range(4):
                    tp(ptr[0:64, h * 64:(h + 1) * 64], Q0[:, h * 64:(h + 1) * 64], ident64, r=(MQ[0][1], "c128"), w=(kptr,))
                for h in range(4):
                    tp(ptr[0:64, 256 + h * 64:256 + (h + 1) * 64], At_[:, h * 64:(h + 1) * 64], ident64,
                       r=(kAt, "c128"), w=(kptr,))
                act(MQ[0][0][:, 0:256], ptr[0:64, 0:256], AF.Copy, r=(kptr,), w=(MQ[0][1],))
                act(attnT, ptr[0:64, 256:512], AF.Copy, r=(kptr,), w=(kaT,))
                A("dve", "tensor_tensor", dg.rearrange("p (h n) -> p h n", h=4),
                  ident64.unsqueeze(1).to_broadcast([64, 4, 64]),
                  s3c("egc", ch).unsqueeze(2).to_broadcast([64, 4, 64]), op=ALU.mult, r=("c128", "sm_egc"), w=(kdg,))
                pqe, kpqe = poolBp.get()
                mm(pqe[:, 0:256], onesf, dg, True, True, r=("c64", kdg), w=(kpqe,))
                A("dve", "tensor_tensor", qinB[:, :, csl], qB[:, :, csl],
                  pqe[:, 0:256].rearrange("p (h n) -> p h n", h=4), op=ALU.mult,
                  r=K_QB + [kpqe], w=[("qinB", ch)])
                yield
                A("dve", "tensor_tensor", Rb[0][0].rearrange("p (h n) -> p h n", h=4),
                  ident64.unsqueeze(1).to_broadcast([64, 4, 64]),
                  MQ[0][0][:, 0:256].rearrange("p (h n) -> p h n", h=4), op=ALU.subtract,
                  r=("c128", MQ[0][1]), w=(Rb[0][1],))
                for k in range(5):
                    cur, kcur = MQ[k % 2]
                    nxt, knxt = MQ[(k + 1) % 2]
                    psq, kpsq = poolBp.get()
                    if k < 4:
                        for h in range(4):
                            hs = slice(h * 64, (h + 1) * 64)
                            hq = slice(256 + h * 64, 256 + (h + 1) * 64)
                            mm(psq[0:64, hs], cur[:, hq], cur[:, hs], True, True, r=(kcur,), w=(kpsq,))
                    for h in range(4):
                        hs = slice(h * 64, (h + 1) * 64)
                        hq = slice(256 + h * 64, 256 + (h + 1) * 64)
                        mm(psq[0:64, hq], cur[:, hs], cur[:, hq], True, True, r=(kcur,), w=(kpsq,))
                    if k < 4:
                        act(nxt, psq[0:64, :], AF.Copy, r=(kpsq,), w=(knxt,))
                    else:
                        act(nxt[:, 256:512], psq[0:64, 256:512], AF.Copy, r=(kpsq,), w=(knxt,))
                    yield
                    rc, krc = Rb[k % 2]
                    rn, krn = Rb[(k + 1) % 2]
                    pru, kpru = poolBp.get()
                    for h in range(4):
                        hs = slice(h * 64, (h + 1) * 64)
                        hq = slice(256 + h * 64, 256 + (h + 1) * 64)
                        mm(pru[0:64, hs], nxt[:, hq], rc[:, hs], True, True, r=(knxt, krc), w=(kpru,))
                    if k < 4:
                        A("dve", "tensor_tensor", rn, rc, pru[0:64, 0:256], op=ALU.add, r=(krc, kpru), w=(krn,))
                    else:
                        A("dve", "tensor_tensor", TmT, rc, pru[0:64, 0:256], op=ALU.add, r=(krc, kpru), w=(kTm,))
                    yield
                pu_, kpu_ = poolBp.get()
                for h in range(4):
                    mm(pu_[0:64, h * 128:(h + 1) * 128], TmT[:, h * 64:(h + 1) * 64], bv[:, h * 128:(h + 1) * 128],
                       True, True, r=(kTm, kbv), w=(kpu_,))
                act(uB, pu_[0:64, :], AF.Copy, r=(kpu_,), w=(kuB,))
                pw, kpw = poolBp.get()
                for h in range(4):
                    mm(pw[:, h * 64:(h + 1) * 64], kbg[:, h * 128:(h + 1) * 128], TmT[:, h * 64:(h + 1) * 64],
                       True, True, r=(kTm, kkbg), w=(kpw,))
                act(wTb, pw[:, 0:256], AF.Copy, r=(kpw,), w=(kwT,))

            def chBser(ch):
                csl = slice(ch * C, (ch + 1) * C)
                gwB, kgwB, uB, kuB, kst, kkst, attnT, kaT, wTb, kwT = hand(ch)
                vnew, kvn = subt(21, 0, 256, BF16, tag=0)
                osq, kosq = subt(22, 0, 512)
                omb, komix = subt(23, 0, 256, BF16)
                pws, kpws = poolBs.get()
                for h in range(4):
                    mm(pws[0:64, h * 128:(h + 1) * 128], wTb[:, h * 64:(h + 1) * 64], SBb[l][:, h * 128:(h + 1) * 128],
                       True, True, r=(kwT, ("SBb", l)), w=(kpws,))
                A("dve", "tensor_tensor", vnew, uB, pws[0:64, :], op=ALU.subtract, r=(kuB, kpws), w=(kvn,))
                yield
                po, kpo = poolBs.get()
                for h in range(4):
                    mm(po[0:64, h * 128:(h + 1) * 128], qinB[:, h, csl], SBb[l][:, h * 128:(h + 1) * 128], True, False,
                       r=[("qinB", ch), ("SBb", l)], w=(kpo,))
                    mm(po[0:64, h * 128:(h + 1) * 128], attnT[:, h * 64:(h + 1) * 64], vnew[:, h * 128:(h + 1) * 128],
                       False, True, r=(kaT, kvn), w=(kpo,))
                psu, kpsu = poolBs.get()
                for h in range(4):
                    mm(psu[:, h * 128:(h + 1) * 128], kst[:, h * 128:(h + 1) * 128], vnew[:, h * 128:(h + 1) * 128],
                       True, True, r=(kkst, kvn), w=(kpsu,))
                yield
                for h in range(4):
                    A("dve", "scalar_tensor_tensor", out=SB[l][:, h * 128:(h + 1) * 128],
                      in0=SB[l][:, h * 128:(h + 1) * 128], scalar=dlB[:, ch * 4 + h:ch * 4 + h + 1],
                      in1=psu[:, h * 128:(h + 1) * 128], op0=ALU.mult, op1=ALU.add,
                      r=(kpsu, ("SB", l), "dlB"), w=(("SB", l),))
                act(SBb[l][:], SB[l][:], AF.Copy, r=(("SB", l),), w=(("SBb", l),))
                yield
                yield from head_out(po, kpo, gwB, kgwB, 4, 128, 2, ch, osq, kosq, omb, komix, poolBs)

            units = []
            for ch in range(NCH):
                if "B" in enable:
                    units.append(Unit("p%d" % ch, chBpre(ch), (["p%d" % (ch - 1)] if ch >= 1 else []) +
                                      (["s%d" % (ch - 2)] if ch >= 2 else [])))

                def ac(ch=ch):
                    if "A" in enable:
                        yield from chA(ch)
                    if "C" in enable:
                        yield from chC(ch)
                units.append(Unit("ac%d" % ch, ac(), ["ac%d" % (ch - 1)] if ch >= 1 else []))
                if "B" in enable:
                    units.append(Unit("s%d" % ch, chBser(ch), ["p%d" % ch] + (["s%d" % (ch - 1)] if ch >= 1 else [])))
            run_units(units, 3)
            bridge(ALLKEYS)

            for blk in range(4):
                wo, kwo = load_w("w_out", l, 0, NKC, blk * 256, 256)
                for sub in range(2):
                    dc = blk * 2 + sub
                    pb, pk = psbank()
                    for c in range(NKC):
                        mm(pb[:], wo[:, c, sub * 128:(sub + 1) * 128], mixedT[:, c, :], c == 0, c == NKC - 1,
                           r=K_MIXT + [kwo], w=(pk,))
                    A("dve", "tensor_tensor", xT[:, dc, :], pb[:], xT[:, dc, :], op=ALU.add, r=(pk, "xT"), w=("xT",))

        for t in range(NT):
            load_tile(t)
            for l in range(DEPTH):
                ffn(l, 1)
                if enable:
                    mixer(l)
                ffn(l, 2)
            store_tile(t)
        sc.add("sp", None, extra=outs)
        SBUF_LEFT[0] = nc.sbuf_bytes_remaining
        sc.plan()
        sc.emit(nc, engsem, dmasem, block)
    return nc


def host_consts(inputs, DEPTH):
    f32 = np.float32
    nw = np.zeros((128, DEPTH * 3 * NKC + NKC), f32)
    for l in range(DEPTH):
        for wi, nm in enumerate(("ffn1_norm", "mix_norm", "ffn2_norm")):
            nw[:, (l * 3 + wi) * NKC:(l * 3 + wi + 1) * NKC] = np.asarray(inputs[nm][l]).reshape(NKC, 128).T
    nw[:, DEPTH * 3 * NKC:] = np.asarray(inputs["final_norm"]).reshape(NKC, 128).T
    c128 = np.zeros((128, 640), f32)
    c128[:, 0:128] = np.eye(128, dtype=f32)
    sm_ = np.ones(T, f32)
    sm_[::C] = 0.0
    c128[:, 128:640] = sm_[None, :]
    i = np.arange(C)
    c64 = np.zeros((64, C64W), f32)
    tri = (i[:, None] <= i[None, :]).astype(f32)
    c64[:, O_TRI:O_TRI + 64] = tri
    c64[:, O_SU:O_SU + 64] = (i[:, None] > i[None, :]).astype(f32)
    strict = (i[:, None] > i[None, :]).astype(f32)
    incl = (i[:, None] >= i[None, :]).astype(f32)
    c64[:, O_STRICT:O_STRICT + 256] = np.tile(strict, (1, 4))
    c64[:, O_INCL:O_INCL + 256] = np.tile(incl, (1, 4))
    lg = np.log1p(-np.exp2(-5.0 - np.arange(4, dtype=f32))).astype(f32)
    for h in range(4):
        rel = (i[None, :] - i[:, None]).astype(f32)
        dm = np.where(rel >= 0, np.exp(lg[h] * np.where(rel >= 0, rel, 0.0)), 0.0).astype(f32)
        c64[:, O_DMAT + h * 64:O_DMAT + (h + 1) * 64] = dm
        c64[:, O_GDEC + h * 64:O_GDEC + (h + 1) * 64] = np.exp(lg[h] * (C - 1 - i)).astype(f32)[:, None]
        c64[:, O_GQ + h * 64:O_GQ + (h + 1) * 64] = np.exp(lg[h] * (i + 1.0)).astype(f32)[None, :]
    pm = np.zeros((64, 64), f32)
    for d in range(32):
        pm[d + 32, d] = -1.0
        pm[d, d + 32] = 1.0
    c64[:, O_PERM:O_PERM + 64] = pm
    inv = (np.float32(10000.0) ** (-np.arange(32, dtype=f32) / np.float32(32))).astype(f32)
    c64[:, O_INVF] = np.concatenate([inv, inv])
    c64[:, O_ONES:O_ONES + 128] = 1.0
    lbp = np.asarray(inputs["hgrn_lower_bounds"]).reshape(4, 4, 128).transpose(2, 1, 0).reshape(128, 16)
    cwh = np.asarray(inputs["gdn_conv"])[:DEPTH].reshape(DEPTH, 4, 12, 128).transpose(3, 0, 2, 1).reshape(128, DEPTH * 48)
    return {"normw": nw, "c128": c128, "c64": c64,
            "lbp": np.ascontiguousarray(lbp, dtype=f32), "cw": np.ascontiguousarray(cwh, dtype=f32),
            "hnorm": np.ascontiguousarray(np.asarray(inputs["hgrn_norm"])[:DEPTH].reshape(1, -1), dtype=f32),
            "gnorm": np.ascontiguousarray(np.asarray(inputs["gdn_norm"])[:DEPTH].reshape(1, -1), dtype=f32),
            "alog": np.ascontiguousarray(np.asarray(inputs["gdn_a_log"])[:DEPTH].reshape(1, -1), dtype=f32),
            "dtb": np.ascontiguousarray(np.asarray(inputs["gdn_dt_bias"])[:DEPTH].reshape(1, -1), dtype=f32)}


def run(inputs, NT, DEPTH, seqs, ncores, enable=("A", "B", "C")):
    nc = build_program(NT, DEPTH, tuple(enable))
    consts = host_consts(inputs, DEPTH)
    in_maps = []
    S = NT * T
    zero_map = None
    for ci in range(ncores):
        b = seqs[ci]
        if b is None:
            if zero_map is None:
                zero_map = {"x": np.zeros((S, D), np.float32), "pos": np.zeros((1, S), np.int32)}
                for wn in WNAMES:
                    zero_map[wn] = np.zeros((DEPTH,) + WSHAPES[wn], np.float32)
                for kc, vc in consts.items():
                    zero_map[kc] = vc if kc in ("c128", "c64") else np.zeros_like(vc)
            in_maps.append(zero_map)
            continue
        m = {"x": np.ascontiguousarray(inputs["x"][b, :S]),
             "pos": np.ascontiguousarray(inputs["positions"][b, :S].reshape(1, S)).astype(np.int32)}
        for wn in WNAMES:
            m[wn] = np.ascontiguousarray(inputs[wn][:DEPTH])
        m.update(consts)
        in_maps.append(m)
    res = run_bass_kernel_spmd(nc, in_maps, core_ids=list(range(ncores)))
    return [r["y"] for r in res.results]


def kernel(**inputs):
    inputs = {k: np.asarray(v) for k, v in inputs.items()}
    outs = run(inputs, NT=16, DEPTH=4, seqs=[0, None, 1, None, 2, None, 3, None], ncores=8)
    return np.stack([outs[0], outs[2], outs[4], outs[6]], axis=0).astype(np.float32)
```

```python
import math
import numpy as np
import concourse.bass as bass
import concourse.mybir as mybir
from concourse.bass_utils import run_bass_kernel_spmd

F32 = mybir.dt.float32
BF16 = mybir.dt.bfloat16
I32 = mybir.dt.int32
AF = mybir.ActivationFunctionType
ALU = mybir.AluOpType
AX = mybir.AxisListType

D = 1024
DFF = 2816
DIN = 4616
T = 512
C = 64
NCH = T // C
EPS = 1e-6
NKC = D // 128
NFC = DFF // 128


class Op:
    __slots__ = ("eng", "fn", "deps", "pos", "signal", "sigval", "dma", "dsem", "dval", "waits", "gi")

    def __init__(self, eng, fn, dma):
        self.eng = eng
        self.fn = fn
        self.dma = dma
        self.deps = []
        self.signal = False
        self.sigval = 0
        self.dsem = None
        self.dval = 0
        self.waits = []


ENGS = ("pe", "act", "dve", "pool", "sp")
NDMASEM = 12


class Sched:
    def __init__(self):
        self.ops = []
        self.lastw = {}
        self.readers = {}
        self.per_eng = {e: [] for e in ENGS}
        self.dma_hist = {e: [] for e in ENGS}

    def add(self, eng, fn, reads=(), writes=(), dma=False, extra=()):
        op = Op(eng, fn, dma)
        deps = set(extra)
        for k in reads:
            w = self.lastw.get(k)
            if w is not None:
                deps.add(w)
        for k in writes:
            w = self.lastw.get(k)
            if w is not None:
                deps.add(w)
            for r in self.readers.get(k, ()):
                deps.add(r)
        for k in reads:
            self.readers.setdefault(k, []).append(op)
        for k in writes:
            self.lastw[k] = op
            self.readers[k] = []
        if dma:
            h = self.dma_hist[eng]
            n = len(h)
            op.dsem = (eng, n % NDMASEM)
            op.dval = 16 * (n // NDMASEM + 1)
            if n >= NDMASEM:
                deps.add(h[n - NDMASEM])
            h.append(op)
        deps.discard(op)
        op.deps = list(deps)
        op.pos = len(self.per_eng[eng])
        op.gi = len(self.ops)
        self.per_eng[eng].append(op)
        self.ops.append(op)
        return op

    def plan(self):
        maxpos = {e: {e2: -1 for e2 in ENGS} for e in ENGS}
        maxd = {e: {} for e in ENGS}
        for op in self.ops:
            e = op.eng
            for p in sorted(op.deps, key=lambda o: o.gi):
                if p.dma:
                    cur = maxd[e].get(p.dsem, 0)
                    if p.dval > cur:
                        maxd[e][p.dsem] = p.dval
                        op.waits.append(("d", p))
                else:
                    if p.eng == "pe" and e == "pe":
                        continue
                    if p.pos > maxpos[e][p.eng]:
                        maxpos[e][p.eng] = p.pos
                        p.signal = True
                        op.waits.append(("c", p))
        for e in ENGS:
            n = 0
            for op in self.per_eng[e]:
                if op.signal and not op.dma:
                    n += 1
                    op.sigval = n

    def emit(self, nc, engsem, dmasem, block):
        sched = self

        def run(engname, eng):
            for op in sched.per_eng[engname]:
                for kind, p in op.waits:
                    if kind == "d":
                        eng.wait_ge(dmasem[p.dsem], p.dval)
                    else:
                        eng.wait_ge(engsem[p.eng], p.sigval)
                if op.fn is None:
                    continue
                ins = op.fn(eng)
                if op.dma:
                    ins.then_inc(dmasem[op.dsem], 16)
                elif op.signal:
                    ins.then_inc(engsem[engname], 1)

        @block.tensor
        def _(pe):
            run("pe", pe)

        @block.scalar
        def _(act):
            run("act", act)

        @block.vector
        def _(dve):
            run("dve", dve)

        @block.gpsimd
        def _(pool):
            run("pool", pool)

        @block.sync
        def _(sp):
            run("sp", sp)


WNAMES = ("ffn1_w_gate", "ffn1_w_up", "ffn1_w_down", "w_in", "w_out",
          "ffn2_w_gate", "ffn2_w_up", "ffn2_w_down")
WSHAPES = {"ffn1_w_gate": (D, DFF), "ffn1_w_up": (D, DFF), "ffn1_w_down": (DFF, D),
           "w_in": (D, DIN), "w_out": (D, D),
           "ffn2_w_gate": (D, DFF), "ffn2_w_up": (D, DFF), "ffn2_w_down": (DFF, D)}


AH, ADK, ADV = 4, 128, 64
BH, BDK, BDV = 4, 128, 128
CH_, CDK, CDV = 4, 64, 64
O_TRI, O_SU, O_STRICT, O_INCL, O_DMAT, O_GDEC, O_GQ, O_PERM, O_INVF, O_ONES = 0, 64, 128, 384, 640, 896, 1152, 1408, 1472, 1473
C64W = O_ONES + 128
C1_RR = 6.28125
C2_RR = 2.0 * math.pi - 6.28125


SBUF_LEFT = [0]


def build_program(NT, DEPTH, enable=("A", "B", "C")):
    import contextlib
    S = NT * T
    nc = bass.Bass("TRN2", target_bir_lowering=False)
    dram = {}

    def din(name, shape, dt=F32):
        dram[name] = nc.dram_tensor(name, list(shape), dt, kind="ExternalInput").ap()
        return dram[name]

    x_d = din("x", (S, D))
    pos_d = din("pos", (1, S), I32)
    for wn in WNAMES:
        din(wn, (DEPTH,) + WSHAPES[wn])
    NNW = DEPTH * 3 * NKC + NKC
    din("normw", (128, NNW))
    din("c128", (128, 640))
    din("c64", (64, C64W))
    din("lbp", (128, 16))
    din("cw", (128, DEPTH * 48))
    din("hnorm", (1, DEPTH * 64))
    din("gnorm", (1, DEPTH * 128))
    din("alog", (1, DEPTH * 4))
    din("dtb", (1, DEPTH * 4))
    y_d = nc.dram_tensor("y", [S, D], F32, kind="ExternalOutput").ap()
    wbf = {wn: nc.dram_tensor("bf_" + wn, [DEPTH] + list(WSHAPES[wn]), BF16, kind="Internal").ap()
           for wn in WNAMES}

    sc = Sched()
    stack = contextlib.ExitStack()

    def salloc(name, shape, dt):
        return stack.enter_context(nc.sbuf_tensor("sb_" + name, list(shape), dt))

    g64 = [float(v) for v in np.exp(np.log1p(-np.exp2(-5.0 - np.arange(4, dtype=np.float32))).astype(np.float32) * np.float32(C)).astype(np.float32)]

    with stack:
        NCELL = 55
        arena = salloc("arena", [128, NCELL * 512], F32)
        xT = salloc("xT", [128, NKC, T], F32)
        hT = salloc("hT", [128, NKC, T], BF16)
        rstd = salloc("rstd", [128, T], F32)
        rtmp = salloc("rtmp", [128, T], F32)
        normw = salloc("normw", [128, NNW], F32)
        c128 = salloc("c128", [128, 640], F32)
        c64 = salloc("c64", [64, C64W], F32)
        identb = salloc("identb", [128, 128], BF16)
        onesb = salloc("onesb", [128, 128], BF16)
        cst = salloc("cst", [128, 4], F32)
        lbp = salloc("lbp", [128, 16], F32)
        lbs = salloc("lbs", [128, 16], F32)
        lbw = salloc("lbw", [128, 16], F32)
        oml = salloc("oml", [128, 16], F32)
        noml = salloc("noml", [128, 16], F32)
        lbm = salloc("lbm", [128, 8], F32)
        cw = salloc("cw", [128, DEPTH * 48], F32)
        hnorm = salloc("hnorm", [64, DEPTH * 64], F32)
        gnorm = salloc("gnorm", [64, DEPTH * 128], F32)
        negA = salloc("negA", [64, DEPTH * 4], F32)
        dtb = salloc("dtb", [64, DEPTH * 4], F32)
        dlA = salloc("dlA", [128, 32], F32)
        dlB = salloc("dlB", [128, 32], F32)
        sm = {n: salloc("sm_" + n, [64, 32], F32) for n in
              ("beta", "xa", "ea", "sp", "g", "gc", "egc", "bg", "tmpd", "est")}
        rs4 = salloc("rs4", [64, 64], F32)
        SA = [salloc("SA%d" % l, [128, 256], F32) for l in range(DEPTH)]
        SB = [salloc("SB%d" % l, [128, 512], F32) for l in range(DEPTH)]
        RC = [salloc("RC%d" % l, [64, 256], F32) for l in range(DEPTH)]
        SAb = [salloc("SAb%d" % l, [128, 256], BF16) for l in range(DEPTH)]
        SBb = [salloc("SBb%d" % l, [128, 512], BF16) for l in range(DEPTH)]
        RCb = [salloc("RCb%d" % l, [64, 256], BF16) for l in range(DEPTH)]
        halo = [salloc("halo%d" % l, [128, 36], F32) for l in range(DEPTH)]
        NWS = 7
        wslot = [salloc("wslot%d" % i, [128, 2048], BF16) for i in range(NWS)]
        ps = [stack.enter_context(nc.psum_tensor("ps%d" % i, [128, 512], F32)) for i in range(8)]
        engsem = {e: stack.enter_context(nc.semaphore("s_" + e)) for e in ENGS}
        dmasem = {}
        for e in ("sp", "pool"):
            for i in range(NDMASEM):
                dmasem[(e, i)] = stack.enter_context(nc.semaphore("d_%s%d" % (e, i)))
        block = stack.enter_context(nc.Block())

        ident = c128[:, 0:128]
        ident64 = c128[0:64, 0:64]
        scanmask = c128[:, 128:640]
        tri = c64[:, O_TRI:O_TRI + 64]
        su = c64[:, O_SU:O_SU + 64]
        onesf = c64[:, O_ONES:O_ONES + 128]

        def c64v(off):
            return c64[:, off:off + 256].rearrange("p (h n) -> p h n", h=4)

        def A(eng, method, *args, r=(), w=(), **kw):
            return sc.add(eng, lambda e: getattr(e, method)(*args, **kw), reads=r, writes=w)

        def cells(c0, n):
            return [("a", c) for c in range(c0, c0 + n)]

        def av(c0, n, dt=F32, parts=128):
            ap = arena[0:parts, c0 * 512:(c0 + n) * 512]
            if dt == BF16:
                ap = ap.bitcast(BF16)
            return ap

        def dma(q, out, in_, reads=(), writes=()):
            return sc.add(q, lambda eng: eng.dma_start(out=out, in_=in_), reads, writes, dma=True)

        wctr = [0]

        def load_w(wn, l, k0, nk, c0, ncol):
            i = wctr[0] % NWS
            wctr[0] += 1
            view = wslot[i][:, 0:nk * ncol].rearrange("p (k n) -> p k n", k=nk)
            src = wbf[wn][l, k0 * 128:(k0 + nk) * 128, c0:c0 + ncol].rearrange("(k p) n -> p k n", p=128)
            rk = [("bf", wn, l, r0) for r0 in range((k0 * 128) // 256 * 256, (k0 + nk) * 128, 256)]
            dma("sp", view, src, reads=rk, writes=("wslot%d" % i,))
            return view, "wslot%d" % i

        psctr = [0]

        def psbank(lo=0, hi=8):
            i = lo + psctr[0] % (hi - lo)
            psctr[0] += 1
            return ps[i], "ps%d" % i

        def mm(out, lhsT, rhs, start, stop, r, w):
            return A("pe", "matmul", out, lhsT=lhsT, rhs=rhs, start=start, stop=stop, r=r, w=w)

        def tp(out, in_, idn, r, w):
            return A("pe", "transpose", out, in_, idn, r=r, w=w)

        def act(out, in_, func, r, w, **kw):
            return A("act", "activation", out=out, in_=in_, func=func, r=r, w=w, **kw)

        def sigm(out, in_, r, w, parts=128):
            act(out, in_, AF.Exp, r=r, w=w, scale=-1.0)
            act(out, out, AF.Ln, r=list(w) + ["cst"], w=w, bias=cst[0:parts, 1:2])
            act(out, out, AF.Exp, r=w, w=w, scale=-1.0)

        def silu(out, in_, r, w, parts=128):
            sigm(out, in_, r, w, parts)
            A("dve", "tensor_tensor", out, out, in_, op=ALU.mult, r=list(w) + list(r), w=w)

        hid = av(0, 11, BF16).rearrange("p (f t) -> p f t", f=NFC)
        sqv = av(11, 4, BF16).rearrange("p (c t) -> p c t", c=NKC)
        sgv = [av(15, 1), av(16, 1)]
        xin = [av(17, 2), av(19, 2)]
        yout = [av(21, 2), av(23, 2)]

        def fm4(c0, parts=128):
            return av(c0, 2, BF16, parts).rearrange("p (h t) -> p h t", h=4)
        qinA, ktA, qB, kB, vsb, qinB = fm4(0), fm4(2), fm4(4), fm4(6), fm4(8), fm4(10)
        qrb, krb, qinC = fm4(12, 64), fm4(14, 64), fm4(16, 64)
        cosT, sinT = av(18, 1, F32, 64), av(19, 1, F32, 64)
        mixedT = av(20, 4, BF16).rearrange("p (c t) -> p c t", c=NKC)
        K_MIXT = cells(20, 4)

        def tmp(i, parts=128):
            return av(24 + i, 1, F32, parts), ("a", 24 + i)

        dma("sp", normw[:], dram["normw"][:, :], writes=("normw",))
        dma("sp", c128[:], dram["c128"][:, :], writes=("c128",))
        dma("sp", c64[:], dram["c64"][:, :], writes=("c64",))
        dma("sp", lbp[:], dram["lbp"][:, :], writes=("lbp",))
        dma("sp", cw[:], dram["cw"][:, :], writes=("cw",))
        dma("sp", hnorm[:], dram["hnorm"][0:1, :].partition_broadcast(64), writes=("hnorm",))
        dma("sp", gnorm[:], dram["gnorm"][0:1, :].partition_broadcast(64), writes=("gnorm",))
        dma("sp", negA[:], dram["alog"][0:1, :].partition_broadcast(64), writes=("negA",))
        dma("sp", dtb[:], dram["dtb"][0:1, :].partition_broadcast(64), writes=("dtb",))
        A("dve", "tensor_copy", identb[:], ident, r=("c128",), w=("identb",))
        A("pool", "memset", onesb[:], 1.0, w=("onesb",))
        A("pool", "memset", cst[:, 0:1], EPS, w=("cst",))
        A("pool", "memset", cst[:, 1:2], 1.0, w=("cst",))
        A("pool", "memset", cst[:, 2:3], math.pi / 2, w=("cst",))
        A("pool", "memset", cst[:, 3:4], 0.0, w=("cst",))
        for l in range(DEPTH):
            A("pool", "memset", SA[l][:], 0.0, w=(("SA", l),))
            A("pool", "memset", SB[l][:], 0.0, w=(("SB", l),))
            A("pool", "memset", RC[l][:], 0.0, w=(("RC", l),))
            A("pool", "memset", SAb[l][:], 0.0, w=(("SAb", l),))
            A("pool", "memset", SBb[l][:], 0.0, w=(("SBb", l),))
            A("pool", "memset", RCb[l][:], 0.0, w=(("RCb", l),))
            A("pool", "memset", halo[l][:], 0.0, w=(("halo", l),))
        act(negA[:], negA[:], AF.Exp, r=("negA",), w=("negA",))
        A("dve", "tensor_scalar", negA[:], negA[:], -1.0, None, op0=ALU.mult, r=("negA",), w=("negA",))
        lb3 = lbp[:].rearrange("p (h l) -> p h l", h=4)
        A("dve", "tensor_reduce", lbm[:, 0:4], lb3, axis=AX.X, op=ALU.max, r=("lbp",), w=("lbm",))
        A("dve", "tensor_tensor", lbw[:].rearrange("p (h l) -> p h l", h=4), lb3,
          lbm[:, 0:4].unsqueeze(2).to_broadcast([128, 4, 4]), op=ALU.subtract, r=("lbp", "lbm"), w=("lbw",))
        act(lbw[:], lbw[:], AF.Exp, r=("lbw",), w=("lbw",))
        A("dve", "tensor_reduce", lbm[:, 4:8], lbw[:].rearrange("p (h l) -> p h l", h=4), axis=AX.X, op=ALU.add,
          r=("lbw",), w=("lbm",))
        A("dve", "reciprocal", lbm[:, 4:8], lbm[:, 4:8], r=("lbm",), w=("lbm",))
        A("dve", "tensor_tensor", lbw[:].rearrange("p (h l) -> p h l", h=4), lbw[:].rearrange("p (h l) -> p h l", h=4),
          lbm[:, 4:8].unsqueeze(2).to_broadcast([128, 4, 4]), op=ALU.mult, r=("lbw", "lbm"), w=("lbw",))
        lbs3 = lbs[:].rearrange("p (h l) -> p h l", h=4)
        lbw3 = lbw[:].rearrange("p (h l) -> p h l", h=4)
        A("pool", "memset", lbs[:], 0.0, w=("lbs",))
        for l in range(1, 4):
            A("dve", "tensor_tensor", lbs3[:, :, l], lbs3[:, :, l - 1], lbw3[:, :, l], op=ALU.add,
              r=("lbs", "lbw"), w=("lbs",))
        A("dve", "tensor_scalar", oml[:], lbs[:], -1.0, 1.0, op0=ALU.mult, op1=ALU.add, r=("lbs",), w=("oml",))
        A("dve", "tensor_scalar", noml[:], lbs[:], -1.0, None, op0=ALU.add, r=("lbs",), w=("noml",))
        for l in range(DEPTH):
            for wn in WNAMES:
                K_, N_ = WSHAPES[wn]
                for r0 in range(0, K_, 256):
                    r1 = min(K_, r0 + 256)
                    dma("pool", wbf[wn][l, r0:r1, :], dram[wn][l, r0:r1, :], writes=(("bf", wn, l, r0),))

        def rmsnorm():
            act(sqv.rearrange("p c t -> p (c t)"), xT[:].rearrange("p c t -> p (c t)"), AF.Square,
                r=("xT",), w=cells(11, 4))
            pb, pk = psbank()
            for c in range(NKC):
                mm(pb[:], onesb[:], sqv[:, c, :], c == 0, c == NKC - 1, r=cells(11, 4) + ["onesb"], w=(pk,))
            act(rtmp[:], pb[:], AF.Ln, r=(pk, "cst"), w=("rtmp",), bias=cst[:, 0:1], scale=1.0 / D)
            act(rstd[:], rtmp[:], AF.Exp, r=("rtmp",), w=("rstd",), scale=-0.5)

        def norm_apply(widx, out_tile, out_keys):
            for c in range(NKC):
                A("dve", "scalar_tensor_tensor", out=out_tile[:, c, :], in0=xT[:, c, :],
                  scalar=normw[:, widx + c:widx + c + 1], in1=rstd[:], op0=ALU.mult, op1=ALU.mult,
                  r=("xT", "rstd", "normw"), w=out_keys)

        def ffn(l, which):
            pre = "ffn%d_w_" % which
            widx = (l * 3 + (0 if which == 1 else 2)) * NKC
            rmsnorm()
            norm_apply(widx, hT, ("hT",))
            for blk in range(NFC // 2):
                wg, kg = load_w(pre + "gate", l, 0, NKC, blk * 256, 256)
                wu, ku = load_w(pre + "up", l, 0, NKC, blk * 256, 256)
                for sub in range(2):
                    f = blk * 2 + sub
                    pg, kpg = psbank(0, 4)
                    pu, kpu = psbank(0, 4)
                    for c in range(NKC):
                        mm(pg[:], wg[:, c, sub * 128:(sub + 1) * 128], hT[:, c, :], c == 0, c == NKC - 1,
                           r=("hT", kg), w=(kpg,))
                    for c in range(NKC):
                        mm(pu[:], wu[:, c, sub * 128:(sub + 1) * 128], hT[:, c, :], c == 0, c == NKC - 1,
                           r=("hT", ku), w=(kpu,))
                    sgt = sgv[f % 2]
                    ksg = ("a", 15 + f % 2)
                    act(sgt, pg[:], AF.Silu, r=(kpg,), w=(ksg,))
                    A("dve", "tensor_tensor", hid[:, f, :], pu[:], sgt, op=ALU.mult, r=(kpu, ksg),
                      w=(("a", f // 2),))
            for half in range(2):
                banks = [(ps[4 + i], "ps%d" % (4 + i)) for i in range(4)]
                for f0 in range(0, NFC, 4):
                    nf = min(4, NFC - f0)
                    wd, kd = load_w(pre + "down", l, f0, nf, half * 512, 512)
                    for fi in range(nf):
                        f = f0 + fi
                        for dci in range(4):
                            pb, pk = banks[dci]
                            mm(pb[:], wd[:, fi, dci * 128:(dci + 1) * 128], hid[:, f, :], f == 0, f == NFC - 1,
                               r=(("a", f // 2), kd), w=(pk,))
                for dci in range(4):
                    dc = half * 4 + dci
                    pb, pk = banks[dci]
                    A("dve", "scalar_tensor_tensor", out=xT[:, dc, :], in0=pb[:], scalar=0.5, in1=xT[:, dc, :],
                      op0=ALU.mult, op1=ALU.add, r=(pk, "xT"), w=("xT",))

        def load_tile(t):
            for b in range(T // 128):
                xi = xin[b % 2]
                xk = cells(17 + 2 * (b % 2), 2)
                r0 = t * T + b * 128
                dma("sp", xi, x_d[r0:r0 + 128, :], writes=xk)
                for g in range(2):
                    pb, pk = psbank()
                    for j in range(4):
                        c = g * 4 + j
                        tp(pb[:, j * 128:(j + 1) * 128], xi[:, c * 128:(c + 1) * 128], ident, r=xk + ["c128"], w=(pk,))
                    act(xT[:, g * 4:(g + 1) * 4, b * 128:(b + 1) * 128], pb[:].rearrange("p (j n) -> p j n", j=4),
                        AF.Copy, r=(pk,), w=("xT",))
            posi, kpi = tmp(0, 64)
            posf, kpf = tmp(1, 64)
            kq, kkq = tmp(2, 64)
            ang, kang = tmp(3, 64)
            s1, ks1 = tmp(4, 64)
            c1, kc1 = tmp(5, 64)
            invf = c64[:, O_INVF:O_INVF + 1]
            dma("sp", posi.bitcast(I32), pos_d[0:1, t * T:(t + 1) * T].partition_broadcast(64), writes=(kpi,))
            A("dve", "tensor_copy", posf, posi.bitcast(I32), r=(kpi,), w=(kpf,))
            A("dve", "tensor_scalar", kq, posf, invf, 1.0 / (2 * math.pi), op0=ALU.mult, op1=ALU.mult,
              r=(kpf, "c64"), w=(kkq,))
            A("dve", "tensor_copy", posi.bitcast(I32), kq, r=(kkq,), w=(kpi,))
            A("dve", "tensor_copy", kq, posi.bitcast(I32), r=(kpi,), w=(kkq,))
            A("dve", "tensor_scalar", ang, posf, invf, None, op0=ALU.mult, r=(kpf, "c64"), w=(kang,))
            A("dve", "scalar_tensor_tensor", out=ang, in0=kq, scalar=-C1_RR, in1=ang, op0=ALU.mult, op1=ALU.add,
              r=(kkq, kang), w=(kang,))
            A("dve", "scalar_tensor_tensor", out=ang, in0=kq, scalar=-C2_RR, in1=ang, op0=ALU.mult, op1=ALU.add,
              r=(kkq, kang), w=(kang,))
            A("dve", "tensor_scalar", ang, ang, 0.25, None, op0=ALU.mult, r=(kang,), w=(kang,))
            act(s1, ang, AF.Sin, r=(kang,), w=(ks1,))
            act(c1, ang, AF.Sin, r=(kang, "cst"), w=(kc1,), bias=cst[0:64, 2:3])
            s2, c2 = posf, kq
            A("dve", "scalar_tensor_tensor", out=s2, in0=s1, scalar=2.0, in1=c1, op0=ALU.mult, op1=ALU.mult,
              r=(ks1, kc1), w=(kpf,))
            A("dve", "tensor_tensor", c2, s1, s1, op=ALU.mult, r=(ks1,), w=(kkq,))
            A("dve", "tensor_scalar", c2, c2, -2.0, 1.0, op0=ALU.mult, op1=ALU.add, r=(kkq,), w=(kkq,))
            A("dve", "scalar_tensor_tensor", out=sinT, in0=s2, scalar=2.0, in1=c2, op0=ALU.mult, op1=ALU.mult,
              r=(kpf, kkq), w=(("a", 19),))
            A("dve", "tensor_tensor", cosT, s2, s2, op=ALU.mult, r=(kpf,), w=(("a", 18),))
            A("dve", "tensor_scalar", cosT, cosT, -2.0, 1.0, op0=ALU.mult, op1=ALU.add, r=(("a", 18),), w=(("a", 18),))

        outs = []

        def store_tile(t):
            widx = DEPTH * 3 * NKC
            rmsnorm()
            norm_apply(widx, xT, ("xT",))
            for b in range(T // 128):
                yo = yout[b % 2]
                yk = cells(21 + 2 * (b % 2), 2)
                for g in range(2):
                    pb, pk = psbank()
                    for j in range(4):
                        c = g * 4 + j
                        tp(pb[:, j * 128:(j + 1) * 128], xT[:, c, b * 128:(b + 1) * 128], ident, r=("xT", "c128"), w=(pk,))
                    act(yo[:, g * 512:(g + 1) * 512], pb[:], AF.Copy, r=(pk,), w=yk)
                r0 = t * T + b * 128
                outs.append(dma("sp", y_d[r0:r0 + 128, :], yo, reads=yk))

        def subt(cell, off, n, dt=F32, parts=64, tag=0):
            ap = arena[0:parts, (24 + cell) * 512 + off:(24 + cell) * 512 + off + n]
            if dt == BF16:
                ap = ap.bitcast(BF16)
            return ap, ("t", cell, tag)

        class Unit:
            def __init__(self, name, gen, pre=()):
                self.name, self.gen, self.pre = name, gen, set(pre)

        def run_units(units, width):
            done = set()
            pending = list(units)
            active = []
            while pending or active:
                i = 0
                while len(active) < width and i < len(pending):
                    if pending[i].pre <= done:
                        active.append(pending.pop(i))
                    else:
                        i += 1
                assert active, "unit prerequisites can never be met"
                for u in list(active):
                    try:
                        next(u.gen)
                    except StopIteration:
                        active.remove(u)
                        done.add(u.name)

        class PsPool:
            def __init__(self, lo, hi):
                self.lo, self.hi, self.n = lo, hi, 0

            def get(self):
                i = self.lo + self.n % (self.hi - self.lo)
                self.n += 1
                return ps[i], "ps%d" % i

        bridge_t = salloc("bridge", [128, 2], F32)

        def bridge(keys):
            A("pool", "memset", bridge_t[:, 0:1], 0.0, r=(), w=list(keys) + ["bridge_t"])

        def head_out(po, pk, gw, kgw, H, dv, c0mix, ch, osq, kosq, omb, komix, pool):
            W = H * dv
            act(osq[:, 0:W], po[0:64, 0:W], AF.Square, r=(pk,), w=(kosq,))
            rs = rs4[:, c0mix * 8:c0mix * 8 + 16]
            krs = ("rs4", c0mix)
            A("dve", "tensor_reduce", rs[:, 0:4], osq[:, 0:W].rearrange("p (h v) -> p h v", h=H), axis=AX.X,
              op=ALU.add, r=(kosq,), w=(krs,))
            act(rs[:, 4:8], rs[:, 0:4], AF.Ln, r=(krs, "cst"), w=(krs,), bias=cst[0:64, 0:1], scale=1.0 / dv)
            act(rs[:, 8:12], rs[:, 4:8], AF.Exp, r=(krs,), w=(krs,), scale=-0.5)
            yield
            for h in range(H):
                A("dve", "scalar_tensor_tensor", out=omb[:, h * dv:(h + 1) * dv], in0=po[0:64, h * dv:(h + 1) * dv],
                  scalar=rs[:, 8 + h:9 + h], in1=gw[:, h * dv:(h + 1) * dv], op0=ALU.mult, op1=ALU.mult,
                  r=(pk, krs, kgw), w=(komix,))
            pb, pkb = pool.get()
            pbb = pb[:].bitcast(BF16)
            n = W // 128
            for j in range(n):
                tp(pbb[:, j * 64:(j + 1) * 64], omb[:, j * 128:(j + 1) * 128], identb[0:64, 0:64],
                   r=(komix, "identb"), w=(pkb,))
            act(mixedT[:, c0mix:c0mix + n, ch * C:(ch + 1) * C],
                pbb[:, 0:n * 64].rearrange("p (j n) -> p j n", j=n), AF.Copy, r=(pkb,), w=[("mixT", c0mix, ch)])

        def mixer(l):
            widx = (l * 3 + 1) * NKC
            rmsnorm()
            norm_apply(widx, hT, ("hT",))
            if len(enable) < 3:
                A("pool", "memset", mixedT.rearrange("p c t -> p (c t)"), 0.0, w=K_MIXT)

            poolFA, poolFB, poolFC, poolFS = PsPool(0, 2), PsPool(2, 5), PsPool(5, 7), PsPool(7, 8)

            def fmA(h, slot):
                wq, kwq = load_w("w_in", l, 0, NKC, h * 128, 128)
                wf, kwf = load_w("w_in", l, 0, NKC, 512 + h * 128, 128)
                pq, kpq = poolFA.get()
                pf, kpf_ = poolFA.get()
                for c in range(NKC):
                    mm(pq[:], wq[:, c, :], hT[:, c, :], c == 0, c == NKC - 1, r=("hT", kwq), w=(kpq,))
                for c in range(NKC):
                    mm(pf[:], wf[:, c, :], hT[:, c, :], c == 0, c == NKC - 1, r=("hT", kwf), w=(kpf_,))
                b0 = slot * 6
                (tq, ktq), (ts, kts), (tk, ktk), (tf, ktf), (teb, kteb), (tenb, ktenb) = [tmp(b0 + i) for i in range(6)]
                li = h * 4 + l
                silu(tq, pq[:], r=(kpq,), w=(ktq,))
                sigm(ts, pf[:], r=(kpf_,), w=(kts,))
                yield
                A("dve", "tensor_scalar", tf, ts, oml[:, li:li + 1], lbs[:, li:li + 1], op0=ALU.mult, op1=ALU.add,
                  r=(kts, "oml", "lbs"), w=(ktf,))
                A("dve", "tensor_scalar", tf, tf, 1e-20, None, op0=ALU.max, r=(ktf,), w=(ktf,))
                act(tf, tf, AF.Ln, r=(ktf,), w=(ktf,))
                A("dve", "tensor_scalar", tk, ts, noml[:, li:li + 1], oml[:, li:li + 1], op0=ALU.mult, op1=ALU.add,
                  r=(kts, "oml", "noml"), w=(ktk,))
                yield
                A("dve", "tensor_tensor_scan", ts, scanmask, tf, 0.0, op0=ALU.mult, op1=ALU.add,
                  r=(ktf, "c128"), w=(kts,))
                act(teb, ts, AF.Exp, r=(kts,), w=(kteb,))
                act(tenb, ts, AF.Exp, r=(kts,), w=(ktenb,), scale=-1.0)
                yield
                A("dve", "tensor_tensor", qinA[:, h, :], tq, teb, op=ALU.mult, r=(ktq, kteb), w=[("qinA", h)])
                A("dve", "tensor_tensor", ktA[:, h, :], tk, tenb, op=ALU.mult, r=(ktk, ktenb), w=[("ktA", h)])
                A("dve", "tensor_copy", dlA[:, h * 8:(h + 1) * 8],
                  teb.rearrange("p (c j) -> p c j", j=C)[:, :, C - 1], r=(kteb,), w=[("dlA", h)])

            def fmB(cidx, slot):
                wv_, kwv = load_w("w_in", l, 0, NKC, 1536 + cidx * 128, 128)
                pc, kpc = poolFB.get()
                for c in range(NKC):
                    mm(pc[:], wv_[:, c, :], hT[:, c, :], c == 0, c == NKC - 1, r=("hT", kwv), w=(kpc,))
                b0 = 12 + slot * 4
                cb = av(24 + b0, 2)
                kcb = cells(24 + b0, 2)
                acc, kacc = tmp(b0 + 2)
                tm_, ktm = tmp(b0 + 3)
                hk = ("halo", l, cidx)
                A("pool", "tensor_copy", cb[:, 0:3], halo[l][:, cidx * 3:cidx * 3 + 3], r=(("halo", l), hk), w=kcb)
                act(cb[:, 3:3 + T], pc[:], AF.Copy, r=(kpc,), w=kcb)
                A("pool", "tensor_copy", halo[l][:, cidx * 3:cidx * 3 + 3], cb[:, T:T + 3], r=kcb, w=(hk,))
                yield
                cwb = l * 48 + cidx * 4
                A("dve", "tensor_scalar", acc, cb[:, 0:T], cw[:, cwb:cwb + 1], None, op0=ALU.mult,
                  r=kcb + ["cw"], w=(kacc,))
                for w_ in range(1, 4):
                    A("dve", "scalar_tensor_tensor", out=acc, in0=cb[:, w_:w_ + T], scalar=cw[:, cwb + w_:cwb + w_ + 1],
                      in1=acc, op0=ALU.mult, op1=ALU.add, r=kcb + ["cw", kacc], w=(kacc,))
                silu(tm_, acc, r=(kacc,), w=(ktm,))
                yield
                h = cidx % 4
                if cidx < 8:
                    sqb = acc.bitcast(BF16)[:, 0:T]
                    act(sqb, tm_, AF.Square, r=(ktm,), w=(kacc,))
                    pn, kpn = poolFB.get()
                    mm(pn[:], onesb[:], sqb, True, True, r=(kacc, "onesb"), w=(kpn,))
                    yield
                    r1 = cb[:, 0:T]
                    r2 = cb[:, T:2 * T]
                    act(r1, pn[:], AF.Ln, r=(kpn, "cst"), w=kcb, bias=cst[:, 0:1])
                    act(r2, r1, AF.Exp, r=kcb, w=kcb, scale=-0.5)
                    yield
                    dst, kd_ = (qB, ("qB", h)) if cidx < 4 else (kB, ("kB", h))
                    A("dve", "scalar_tensor_tensor", out=dst[:, h, :], in0=tm_,
                      scalar=(BDK ** -0.5 if cidx < 4 else 1.0), in1=r2, op0=ALU.mult, op1=ALU.mult,
                      r=[ktm] + kcb, w=[kd_])
                else:
                    A("dve", "tensor_copy", vsb[:, h, :], tm_, r=(ktm,), w=[("vsb", h)])

            def s3(n):
                return sm[n][:].rearrange("p (c h) -> p c h", h=4)

            def fmBsmall():
                wt, kwt = load_w("w_in", l, 0, NKC, 3584, 8)
                pba, kpba = poolFS.get()
                for ch in range(NCH):
                    for c in range(NKC):
                        mm(pba[0:64, ch * 8:(ch + 1) * 8], hT[:, c, ch * C:(ch + 1) * C], wt[:, c, :], c == 0, c == NKC - 1,
                           r=("hT", kwt), w=(kpba,))
                pba3 = pba[0:64, 0:64].rearrange("p (c n) -> p c n", n=8)
                act(s3("beta"), pba3[:, :, 0:4], AF.Sigmoid, r=(kpba,), w=("sm_beta",))
                A("dve", "tensor_tensor", s3("xa"), pba3[:, :, 4:8],
                  dtb[:, l * 4:(l + 1) * 4].unsqueeze(1).to_broadcast([64, NCH, 4]), op=ALU.add,
                  r=(kpba, "dtb"), w=("sm_xa",))
                act(sm["ea"][:], sm["xa"][:], AF.Exp, r=("sm_xa",), w=("sm_ea",))
                act(sm["sp"][:], sm["ea"][:], AF.Ln, r=("sm_ea", "cst"), w=("sm_sp",), bias=cst[0:64, 1:2])
                A("dve", "tensor_tensor", s3("g"), s3("sp"),
                  negA[:, l * 4:(l + 1) * 4].unsqueeze(1).to_broadcast([64, NCH, 4]), op=ALU.mult,
                  r=("sm_sp", "negA"), w=("sm_g",))
                yield
                pg1, kpg1 = poolFS.get()
                mm(pg1[0:64, 0:32], tri, sm["g"][:], True, True, r=("c64", "sm_g"), w=(kpg1,))
                mm(pg1[0:64, 32:64], onesf[:, 0:64], sm["g"][:], True, True, r=("c64", "sm_g"), w=(kpg1,))
                pg2, kpg2 = pg1, kpg1
                mm(pg2[:, 64:96], onesf, sm["g"][:], True, True, r=("c64", "sm_g"), w=(kpg2,))
                yield
                act(sm["gc"][:], pg1[0:64, 0:32], AF.Copy, r=(kpg1,), w=("sm_gc",))
                act(sm["egc"][:], pg1[0:64, 0:32], AF.Exp, r=(kpg1,), w=("sm_egc",))
                A("dve", "tensor_tensor", sm["bg"][:], sm["beta"][:], sm["egc"][:], op=ALU.mult,
                  r=("sm_beta", "sm_egc"), w=("sm_bg",))
                A("dve", "tensor_tensor", sm["tmpd"][:], pg1[0:64, 32:64], sm["gc"][:], op=ALU.subtract,
                  r=(kpg1, "sm_gc"), w=("sm_tmpd",))
                act(sm["est"][:], sm["tmpd"][:], AF.Exp, r=("sm_tmpd",), w=("sm_est",))
                act(dlB[:], pg2[:, 64:96], AF.Exp, r=(kpg2,), w=("dlB",))

            def fmC(qk, h, slot):
                perm = c64[:, O_PERM:O_PERM + 64]
                wc_, kwc = load_w("w_in", l, 0, NKC, 3592 + qk * 256 + h * 64, 64)
                px, kpx = poolFC.get()
                for c in range(NKC):
                    mm(px[0:64, :], wc_[:, c, :], hT[:, c, :], c == 0, c == NKC - 1, r=("hT", kwc), w=(kpx,))
                b0 = 24 + slot * 3
                (xf, kxf), (t1, kt1), (t2, kt2) = [tmp(b0 + i, 64) for i in range(3)]
                act(xf, px[0:64, :], AF.Copy, r=(kpx,), w=(kxf,), scale=(1.0 if qk == 0 else CDK ** -0.5))
                yield
                pr, kpr = poolFC.get()
                mm(pr[0:64, :], perm, xf, True, True, r=("c64", kxf), w=(kpr,))
                A("dve", "tensor_tensor", t1, xf, cosT, op=ALU.mult, r=(kxf, ("a", 18)), w=(kt1,))
                yield
                A("dve", "tensor_tensor", t2, pr[0:64, :], sinT, op=ALU.mult, r=(kpr, ("a", 19)), w=(kt2,))
                A("dve", "tensor_tensor", t1, t1, t2, op=ALU.add, r=(kt1, kt2), w=(kt1,))
                yield
                if qk == 0:
                    act(qrb[:, h, :], t1, AF.Copy, r=(kt1,), w=[("qrb", h)])
                    A("dve", "tensor_tensor", qinC[:, h, :].rearrange("p (c i) -> p c i", i=C),
                      t1.rearrange("p (c i) -> p c i", i=C),
                      c64[:, O_GQ + h * 64:O_GQ + (h + 1) * 64].unsqueeze(1).to_broadcast([64, NCH, C]),
                      op=ALU.mult, r=(kt1, "c64"), w=[("qinC", h)])
                else:
                    act(krb[:, h, :], t1, AF.Copy, r=(kt1,), w=[("krb", h)])

            ALLFINE = ([(n_, h) for n_ in ("qinA", "ktA", "qB", "kB", "vsb", "qrb", "krb", "qinC", "dlA") for h in range(4)] +
                       [("qinB", ch) for ch in range(NCH)] + [("mixT", c0, ch) for c0 in (0, 2, 6) for ch in range(NCH)] +
                       [("t", c_, tg) for c_ in range(NCELL - 24) for tg in range(3)])
            ALLKEYS = cells(0, NCELL) + ALLFINE
            bridge(ALLKEYS)

            uA, uB_, uC = [], [], []
            if "A" in enable:
                uA = [Unit("A%d" % h, fmA(h, h % 2), ["A%d" % (h - 2)] if h >= 2 else []) for h in range(4)]
            if "B" in enable:
                uB_ = [Unit("B%d" % ci, fmB(ci, ci % 3), ["B%d" % (ci - 3)] if ci >= 3 else []) for ci in range(12)]
                uB_.append(Unit("Bs", fmBsmall()))
            if "C" in enable:
                uC = [Unit("C%d" % i, fmC(i // 4, i % 4, i % 2), ["C%d" % (i - 2)] if i >= 2 else []) for i in range(8)]
            units = []
            while uA or uB_ or uC:
                for lst, n_ in ((uB_, 2), (uA, 1), (uC, 1)):
                    for _ in range(n_):
                        if lst:
                            units.append(lst.pop(0))
            run_units(units, 6)
            bridge(ALLKEYS)

            wTM = {}
            for which, c0 in (("A", 1024), ("B", 3072), ("C", 4104)):
                if which in enable:
                    wTM[which] = [load_w("w_in", l, 0, NKC, c0 + i * 256, 256) for i in range(2)]

            poolBp, poolBs, poolAC = [PsPool(0, 2), PsPool(2, 4)], PsPool(4, 6), PsPool(6, 8)

            def tm_proj(which, ch, pool):
                pb, pk = pool.get()
                for i in range(2):
                    wv_, kw_ = wTM[which][i]
                    for c in range(NKC):
                        mm(pb[0:64, i * 256:(i + 1) * 256], hT[:, c, ch * C:(ch + 1) * C], wv_[:, c, :],
                           c == 0, c == NKC - 1, r=("hT", kw_), w=(pk,))
                return pb, pk

            K_QINA = [("qinA", h) for h in range(4)]
            K_KTA = [("ktA", h) for h in range(4)]
            K_QB = [("qB", h) for h in range(4)]
            K_KB = [("kB", h) for h in range(4)]
            K_VSB = [("vsb", h) for h in range(4)]
            K_QRB = [("qrb", h) for h in range(4)]
            K_KRB = [("krb", h) for h in range(4)]
            K_QINC = [("qinC", h) for h in range(4)]
            K_DLA = [("dlA", h) for h in range(4)]

            def chA(ch):
                csl = slice(ch * C, (ch + 1) * C)
                pat, kpat = tm_proj("A", ch, poolAC)
                gwA, kgwA = subt(0, 0, 256, tag=0)
                vA, kvA = subt(0, 256, 128, BF16, tag=1)
                ktTM, kkt = subt(1, 0, 256, BF16, tag=0)
                PT, kpt = subt(1, 256, 128, BF16, tag=1)
                tS, ktS = subt(2, 0, 256, parts=128)
                osq, kosq = subt(3, 0, 256, tag=0)
                omb, komix = subt(3, 256, 128, BF16, tag=1)
                act(vA, pat[0:64, 0:256], AF.Copy, r=(kpat,), w=(kvA,))
                silu(gwA, pat[0:64, 256:512], r=(kpat,), w=(kgwA,), parts=64)
                A("dve", "tensor_tensor", gwA.rearrange("p (h v) -> p h v", h=4),
                  gwA.rearrange("p (h v) -> p h v", h=4),
                  hnorm[:, l * 64:(l + 1) * 64].unsqueeze(1).to_broadcast([64, 4, 64]), op=ALU.mult,
                  r=(kgwA, "hnorm"), w=(kgwA,))
                yield
                pkt, kpkt = poolAC.get()
                pktb = pkt[0:64, :].bitcast(BF16)
                for h in range(4):
                    tp(pktb[:, h * 128:(h + 1) * 128], ktA[:, h, csl], identb[:], r=K_KTA + ["identb"], w=(kpkt,))
                act(ktTM, pktb[:, 0:512], AF.Copy, r=(kpkt,), w=(kkt,))
                pa, kpa = poolAC.get()
                for h in range(4):
                    mm(pa[0:64, h * 64:(h + 1) * 64], ktA[:, h, csl], qinA[:, h, csl], True, True,
                       r=K_KTA + K_QINA, w=(kpa,))
                A("dve", "tensor_tensor", PT.rearrange("p (h n) -> p h n", h=4),
                  pa[0:64, 0:256].rearrange("p (h n) -> p h n", h=4),
                  tri.unsqueeze(1).to_broadcast([64, 4, 64]), op=ALU.mult, r=(kpa, "c64"), w=(kpt,))
                yield
                pu_, kpu_ = poolAC.get()
                for h in range(4):
                    mm(pu_[:, h * 64:(h + 1) * 64], ktTM[:, h * 128:(h + 1) * 128], vA[:, h * 64:(h + 1) * 64],
                       True, True, r=(kkt, kvA), w=(kpu_,))
                po, kpo = poolAC.get()
                for h in range(4):
                    mm(po[0:64, h * 64:(h + 1) * 64], qinA[:, h, csl], SAb[l][:, h * 64:(h + 1) * 64], True, False,
                       r=K_QINA + [("SAb", l)], w=(kpo,))
                    mm(po[0:64, h * 64:(h + 1) * 64], PT[:, h * 64:(h + 1) * 64], vA[:, h * 64:(h + 1) * 64], False, True,
                       r=(kpt, kvA), w=(kpo,))
                yield
                A("dve", "tensor_tensor", tS, pu_[:, 0:256], SA[l][:], op=ALU.add, r=(kpu_, ("SA", l)), w=(ktS,))
                A("dve", "tensor_tensor", SA[l][:].rearrange("p (h v) -> p h v", h=4),
                  tS.rearrange("p (h v) -> p h v", h=4),
                  dlA[:].rearrange("p (h c) -> p h c", h=4)[:, :, ch:ch + 1].to_broadcast([128, 4, 64]),
                  op=ALU.mult, r=[ktS] + K_DLA, w=(("SA", l),))
                act(SAb[l][:], SA[l][:], AF.Copy, r=(("SA", l),), w=(("SAb", l),))
                yield
                yield from head_out(po, kpo, gwA, kgwA, 4, 64, 0, ch, osq, kosq, omb, komix, poolAC)

            def chC(ch):
                csl = slice(ch * C, (ch + 1) * C)
                pct, kpct = tm_proj("C", ch, poolAC)
                gwC, kgwC = subt(4, 0, 256, tag=0)
                vC, kvC = subt(4, 256, 128, BF16, tag=1)
                kst, kkt = subt(5, 0, 128, BF16, tag=0)
                PT, kpt = subt(5, 128, 128, BF16, tag=1)
                osq, kosq = subt(6, 0, 256, tag=0)
                omb, komix = subt(6, 256, 128, BF16, tag=1)
                act(vC, pct[0:64, 0:256], AF.Copy, r=(kpct,), w=(kvC,))
                silu(gwC, pct[0:64, 256:512], r=(kpct,), w=(kgwC,), parts=64)
                yield
                pkt, kpkt = poolAC.get()
                pktb = pkt[0:64, :].bitcast(BF16)
                for h in range(4):
                    tp(pktb[:, h * 64:(h + 1) * 64], krb[:, h, csl], identb[0:64, 0:64], r=K_KRB + ["identb"], w=(kpkt,))
                A("dve", "tensor_tensor", kst, pktb[:, 0:256], c64[:, O_GDEC:O_GDEC + 256], op=ALU.mult,
                  r=(kpkt, "c64"), w=(kkt,))
                pa, kpa = poolAC.get()
                for h in range(4):
                    mm(pa[0:64, h * 64:(h + 1) * 64], krb[:, h, csl], qrb[:, h, csl], True, True,
                       r=K_KRB + K_QRB, w=(kpa,))
                A("dve", "tensor_tensor", PT, pa[0:64, 0:256], c64[:, O_DMAT:O_DMAT + 256], op=ALU.mult,
                  r=(kpa, "c64"), w=(kpt,))
                yield
                pu_, kpu_ = poolAC.get()
                for h in range(4):
                    mm(pu_[0:64, h * 64:(h + 1) * 64], kst[:, h * 64:(h + 1) * 64], vC[:, h * 64:(h + 1) * 64],
                       True, True, r=(kkt, kvC), w=(kpu_,))
                po, kpo = poolAC.get()
                for h in range(4):
                    mm(po[0:64, h * 64:(h + 1) * 64], qinC[:, h, csl], RCb[l][:, h * 64:(h + 1) * 64], True, False,
                       r=K_QINC + [("RCb", l)], w=(kpo,))
                    mm(po[0:64, h * 64:(h + 1) * 64], PT[:, h * 64:(h + 1) * 64], vC[:, h * 64:(h + 1) * 64], False, True,
                       r=(kpt, kvC), w=(kpo,))
                yield
                for h in range(4):
                    A("dve", "scalar_tensor_tensor", out=RC[l][:, h * 64:(h + 1) * 64], in0=RC[l][:, h * 64:(h + 1) * 64],
                      scalar=g64[h], in1=pu_[0:64, h * 64:(h + 1) * 64], op0=ALU.mult, op1=ALU.add,
                      r=(kpu_, ("RC", l)), w=(("RC", l),))
                act(RCb[l][:], RC[l][:], AF.Copy, r=(("RC", l),), w=(("RCb", l),))
                yield
                yield from head_out(po, kpo, gwC, kgwC, 4, 64, 6, ch, osq, kosq, omb, komix, poolAC)

            def hand(ch):
                p = ch % 3
                uB, kuB = subt(7 + p * 2, 0, 512, tag=0)
                kst, kkst = subt(8 + p * 2, 0, 256, BF16, tag=0)
                attnT, kaT = subt(8 + p * 2, 256, 128, BF16, tag=1)
                wTb, kwT = subt(8 + p * 2, 384, 128, BF16, parts=128, tag=2)
                return uB, kuB, kst, kkst, attnT, kaT, wTb, kwT

            def s3c(n, ch):
                return sm[n][:].rearrange("p (c h) -> p c h", h=4)[:, ch, :]

            def chBpre(ch):
                csl = slice(ch * C, (ch + 1) * C)
                uB, kuB, kst, kkst, attnT, kaT, wTb, kwT = hand(ch)
                pool = poolBp[ch % 2]
                c0_ = 13 + 7 * (ch % 2)
                kbg, kkbg = subt(c0_ + 0, 0, 256, BF16, tag=0)
                bv, kbv = subt(c0_ + 0, 256, 256, BF16, tag=1)
                gt, kgt = subt(c0_ + 1, 0, 256, tag=0)
                At_, kAt = subt(c0_ + 1, 256, 256, tag=1)
                E_, kE = subt(c0_ + 2, 0, 512)
                MQ = [subt(c0_ + 3, 0, 512), subt(c0_ + 4, 0, 512)]
                dg, kdg = subt(c0_ + 5, 0, 256, tag=0)
                TmT, kTm = subt(c0_ + 5, 256, 128, BF16, tag=1)
                Rb = [subt(c0_ + 6, 0, 256, tag=0), subt(c0_ + 6, 256, 256, tag=1)]

                def bc128(n):
                    return s3c(n, ch).unsqueeze(2).to_broadcast([64, 4, 128])
                pkt, kpkt = pool.get()
                pktb = pkt[0:64, :].bitcast(BF16)
                for h in range(4):
                    tp(pktb[:, h * 128:(h + 1) * 128], kB[:, h, csl], identb[:], r=K_KB + ["identb"], w=(kpkt,))
                A("dve", "tensor_tensor", kbg.rearrange("p (h n) -> p h n", h=4),
                  pktb[:, 0:512].rearrange("p (h n) -> p h n", h=4), bc128("bg"), op=ALU.mult,
                  r=(kpkt, "sm_bg"), w=(kkbg,))
                A("dve", "tensor_tensor", kst.rearrange("p (h n) -> p h n", h=4),
                  pktb[:, 0:512].rearrange("p (h n) -> p h n", h=4), bc128("est"), op=ALU.mult,
                  r=(kpkt, "sm_est"), w=(kkst,))
                pvt, kpvt = pool.get()
                pvtb = pvt[0:64, :].bitcast(BF16)
                for h in range(4):
                    tp(pvtb[:, h * 128:(h + 1) * 128], vsb[:, h, csl], identb[:], r=K_VSB + ["identb"], w=(kpvt,))
                A("dve", "tensor_tensor", bv.rearrange("p (h n) -> p h n", h=4),
                  pvtb[:, 0:512].rearrange("p (h n) -> p h n", h=4), bc128("beta"), op=ALU.mult,
                  r=(kpvt, "sm_beta"), w=(kbv,))
                yield
                pkk, kpkk = pool.get()
                for h in range(4):
                    mm(pkk[0:64, h * 64:(h + 1) * 64], kB[:, h, csl], kB[:, h, csl], True, True, r=K_KB, w=(kpkk,))
                for h in range(4):
                    mm(pkk[0:64, 256 + h * 64:256 + (h + 1) * 64], qB[:, h, csl], kB[:, h, csl], True, True,
                       r=K_QB + K_KB, w=(kpkk,))
                A("dve", "tensor_tensor", gt.rearrange("p (h n) -> p h n", h=4),
                  tri.unsqueeze(1).to_broadcast([64, 4, 64]),
                  s3c("g", ch).unsqueeze(2).to_broadcast([64, 4, 64]), op=ALU.mult, r=("c64", "sm_g"), w=(kgt,))
                pdf, kpdf = pool.get()
                for h in range(4):
                    mm(pdf[0:64, h * 64:(h + 1) * 64], gt[:, h * 64:(h + 1) * 64], su, True, True,
                       r=(kgt, "c64"), w=(kpdf,))
                act(E_[:, 0:256], pdf[0:64, 0:256], AF.Exp, r=(kpdf,), w=(kE,))
                yield
                A("dve", "tensor_tensor", E_[:, 256:512], E_[:, 0:256], c64[:, O_INCL:O_INCL + 256], op=ALU.mult,
                  r=(kE, "c64"), w=(kE,))
                A("dve", "tensor_tensor", E_[:, 0:256], E_[:, 0:256], c64[:, O_STRICT:O_STRICT + 256], op=ALU.mult,
                  r=(kE, "c64"), w=(kE,))
                Q0 = MQ[0][0][:, 256:512]
                for h in range(4):
                    A("dve", "scalar_tensor_tensor", out=Q0[:, h * 64:(h + 1) * 64], in0=pkk[0:64, h * 64:(h + 1) * 64],
                      scalar=s3c("beta", ch)[:, h:h + 1], in1=E_[:, h * 64:(h + 1) * 64], op0=ALU.mult, op1=ALU.mult,
                      r=(kpkk, "sm_beta", kE), w=(MQ[0][1],))
                A("dve", "tensor_tensor", At_, pkk[0:64, 256:512], E_[:, 256:512], op=ALU.mult,
                  r=(kpkk, kE), w=(kAt,))
                yield
                ptr, kptr = pool.get()
                for h in range(4):
                    tp(ptr[0:64, h * 64:(h + 1) * 64], Q0[:, h * 64:(h + 1) * 64], ident64, r=(MQ[0][1], "c128"), w=(kptr,))
                for h in range(4):
                    tp(ptr[0:64, 256 + h * 64:256 + (h + 1) * 64], At_[:, h * 64:(h + 1) * 64], ident64,
                       r=(kAt, "c128"), w=(kptr,))
                act(MQ[0][0][:, 0:256], ptr[0:64, 0:256], AF.Copy, r=(kptr,), w=(MQ[0][1],))
                act(attnT, ptr[0:64, 256:512], AF.Copy, r=(kptr,), w=(kaT,))
                A("dve", "tensor_tensor", dg.rearrange("p (h n) -> p h n", h=4),
                  ident64.unsqueeze(1).to_broadcast([64, 4, 64]),
                  s3c("egc", ch).unsqueeze(2).to_broadcast([64, 4, 64]), op=ALU.mult, r=("c128", "sm_egc"), w=(kdg,))
                pqe, kpqe = pool.get()
                mm(pqe[:, 0:256], onesf, dg, True, True, r=("c64", kdg), w=(kpqe,))
                A("dve", "tensor_tensor", qinB[:, :, csl], qB[:, :, csl],
                  pqe[:, 0:256].rearrange("p (h n) -> p h n", h=4), op=ALU.mult,
                  r=K_QB + [kpqe], w=[("qinB", ch)])
                yield
                A("dve", "tensor_tensor", Rb[0][0].rearrange("p (h n) -> p h n", h=4),
                  ident64.unsqueeze(1).to_broadcast([64, 4, 64]),
                  MQ[0][0][:, 0:256].rearrange("p (h n) -> p h n", h=4), op=ALU.subtract,
                  r=("c128", MQ[0][1]), w=(Rb[0][1],))
                for k in range(5):
                    cur, kcur = MQ[k % 2]
                    nxt, knxt = MQ[(k + 1) % 2]
                    psq, kpsq = pool.get()
                    if k < 4:
                        for h in range(4):
                            hs = slice(h * 64, (h + 1) * 64)
                            hq = slice(256 + h * 64, 256 + (h + 1) * 64)
                            mm(psq[0:64, hs], cur[:, hq], cur[:, hs], True, True, r=(kcur,), w=(kpsq,))
                    for h in range(4):
                        hs = slice(h * 64, (h + 1) * 64)
                        hq = slice(256 + h * 64, 256 + (h + 1) * 64)
                        mm(psq[0:64, hq], cur[:, hs], cur[:, hq], True, True, r=(kcur,), w=(kpsq,))
                    if k < 4:
                        act(nxt, psq[0:64, :], AF.Copy, r=(kpsq,), w=(knxt,))
                    else:
                        act(nxt[:, 256:512], psq[0:64, 256:512], AF.Copy, r=(kpsq,), w=(knxt,))
                    yield
                    rc, krc = Rb[k % 2]
                    rn, krn = Rb[(k + 1) % 2]
                    pru, kpru = pool.get()
                    for h in range(4):
                        hs = slice(h * 64, (h + 1) * 64)
                        hq = slice(256 + h * 64, 256 + (h + 1) * 64)
                        mm(pru[0:64, hs], nxt[:, hq], rc[:, hs], True, True, r=(knxt, krc), w=(kpru,))
                    if k < 4:
                        A("dve", "tensor_tensor", rn, rc, pru[0:64, 0:256], op=ALU.add, r=(krc, kpru), w=(krn,))
                    else:
                        A("dve", "tensor_tensor", TmT, rc, pru[0:64, 0:256], op=ALU.add, r=(krc, kpru), w=(kTm,))
                    yield
                pu_, kpu_ = pool.get()
                for h in range(4):
                    mm(pu_[0:64, h * 128:(h + 1) * 128], TmT[:, h * 64:(h + 1) * 64], bv[:, h * 128:(h + 1) * 128],
                       True, True, r=(kTm, kbv), w=(kpu_,))
                act(uB, pu_[0:64, :], AF.Copy, r=(kpu_,), w=(kuB,))
                pw, kpw = pool.get()
                for h in range(4):
                    mm(pw[:, h * 64:(h + 1) * 64], kbg[:, h * 128:(h + 1) * 128], TmT[:, h * 64:(h + 1) * 64],
                       True, True, r=(kTm, kkbg), w=(kpw,))
                act(wTb, pw[:, 0:256], AF.Copy, r=(kpw,), w=(kwT,))

            def chBser(ch):
                csl = slice(ch * C, (ch + 1) * C)
                uB, kuB, kst, kkst, attnT, kaT, wTb, kwT = hand(ch)
                vnew, kvn = subt(27, 0, 256, BF16, tag=0)
                osq, kosq = subt(28, 0, 512)
                omb, komix = subt(29, 0, 256, BF16)
                gwB, kgwB = subt(30, 0, 512)
                pbt, kpbt = tm_proj("B", ch, poolBs)
                silu(gwB, pbt[0:64, :], r=(kpbt,), w=(kgwB,), parts=64)
                A("dve", "tensor_tensor", gwB.rearrange("p (h v) -> p h v", h=4),
                  gwB.rearrange("p (h v) -> p h v", h=4),
                  gnorm[:, l * 128:(l + 1) * 128].unsqueeze(1).to_broadcast([64, 4, 128]), op=ALU.mult,
                  r=(kgwB, "gnorm"), w=(kgwB,))
                pws, kpws = poolBs.get()
                for h in range(4):
                    mm(pws[0:64, h * 128:(h + 1) * 128], wTb[:, h * 64:(h + 1) * 64], SBb[l][:, h * 128:(h + 1) * 128],
                       True, True, r=(kwT, ("SBb", l)), w=(kpws,))
                A("dve", "tensor_tensor", vnew, uB, pws[0:64, :], op=ALU.subtract, r=(kuB, kpws), w=(kvn,))
                yield
                po, kpo = poolBs.get()
                for h in range(4):
                    mm(po[0:64, h * 128:(h + 1) * 128], qinB[:, h, csl], SBb[l][:, h * 128:(h + 1) * 128], True, False,
                       r=[("qinB", ch), ("SBb", l)], w=(kpo,))
                    mm(po[0:64, h * 128:(h + 1) * 128], attnT[:, h * 64:(h + 1) * 64], vnew[:, h * 128:(h + 1) * 128],
                       False, True, r=(kaT, kvn), w=(kpo,))
                psu, kpsu = poolBs.get()
                for h in range(4):
                    mm(psu[:, h * 128:(h + 1) * 128], kst[:, h * 128:(h + 1) * 128], vnew[:, h * 128:(h + 1) * 128],
                       True, True, r=(kkst, kvn), w=(kpsu,))
                yield
                for h in range(4):
                    A("dve", "scalar_tensor_tensor", out=SB[l][:, h * 128:(h + 1) * 128],
                      in0=SB[l][:, h * 128:(h + 1) * 128], scalar=dlB[:, ch * 4 + h:ch * 4 + h + 1],
                      in1=psu[:, h * 128:(h + 1) * 128], op0=ALU.mult, op1=ALU.add,
                      r=(kpsu, ("SB", l), "dlB"), w=(("SB", l),))
                act(SBb[l][:], SB[l][:], AF.Copy, r=(("SB", l),), w=(("SBb", l),))
                yield
                yield from head_out(po, kpo, gwB, kgwB, 4, 128, 2, ch, osq, kosq, omb, komix, poolBs)

            units = []
            for ch in range(NCH):
                if "B" in enable:
                    units.append(Unit("p%d" % ch, chBpre(ch), (["p%d" % (ch - 2)] if ch >= 2 else []) +
                                      (["s%d" % (ch - 3)] if ch >= 3 else [])))

                def ac(ch=ch):
                    if "A" in enable:
                        yield from chA(ch)
                    if "C" in enable:
                        yield from chC(ch)
                units.append(Unit("ac%d" % ch, ac(), ["ac%d" % (ch - 1)] if ch >= 1 else []))
                if "B" in enable:
                    units.append(Unit("s%d" % ch, chBser(ch), ["p%d" % ch] + (["s%d" % (ch - 1)] if ch >= 1 else [])))
            run_units(units, 4)
            bridge(ALLKEYS)

            for blk in range(4):
                wo, kwo = load_w("w_out", l, 0, NKC, blk * 256, 256)
                for sub in range(2):
                    dc = blk * 2 + sub
                    pb, pk = psbank()
                    for c in range(NKC):
                        mm(pb[:], wo[:, c, sub * 128:(sub + 1) * 128], mixedT[:, c, :], c == 0, c == NKC - 1,
                           r=K_MIXT + [kwo], w=(pk,))
                    A("dve", "tensor_tensor", xT[:, dc, :], pb[:], xT[:, dc, :], op=ALU.add, r=(pk, "xT"), w=("xT",))

        for t in range(NT):
            load_tile(t)
            for l in range(DEPTH):
                ffn(l, 1)
                if enable:
                    mixer(l)
                ffn(l, 2)
            store_tile(t)
        sc.add("sp", None, extra=outs)
        SBUF_LEFT[0] = nc.sbuf_bytes_remaining
        sc.plan()
        sc.emit(nc, engsem, dmasem, block)
    return nc


def host_consts(inputs, DEPTH):
    f32 = np.float32
    nw = np.zeros((128, DEPTH * 3 * NKC + NKC), f32)
    for l in range(DEPTH):
        for wi, nm in enumerate(("ffn1_norm", "mix_norm", "ffn2_norm")):
            nw[:, (l * 3 + wi) * NKC:(l * 3 + wi + 1) * NKC] = np.asarray(inputs[nm][l]).reshape(NKC, 128).T
    nw[:, DEPTH * 3 * NKC:] = np.asarray(inputs["final_norm"]).reshape(NKC, 128).T
    c128 = np.zeros((128, 640), f32)
    c128[:, 0:128] = np.eye(128, dtype=f32)
    sm_ = np.ones(T, f32)
    sm_[::C] = 0.0
    c128[:, 128:640] = sm_[None, :]
    i = np.arange(C)
    c64 = np.zeros((64, C64W), f32)
    tri = (i[:, None] <= i[None, :]).astype(f32)
    c64[:, O_TRI:O_TRI + 64] = tri
    c64[:, O_SU:O_SU + 64] = (i[:, None] > i[None, :]).astype(f32)
    strict = (i[:, None] > i[None, :]).astype(f32)
    incl = (i[:, None] >= i[None, :]).astype(f32)
    c64[:, O_STRICT:O_STRICT + 256] = np.tile(strict, (1, 4))
    c64[:, O_INCL:O_INCL + 256] = np.tile(incl, (1, 4))
    lg = np.log1p(-np.exp2(-5.0 - np.arange(4, dtype=f32))).astype(f32)
    for h in range(4):
        rel = (i[None, :] - i[:, None]).astype(f32)
        dm = np.where(rel >= 0, np.exp(lg[h] * np.where(rel >= 0, rel, 0.0)), 0.0).astype(f32)
        c64[:, O_DMAT + h * 64:O_DMAT + (h + 1) * 64] = dm
        c64[:, O_GDEC + h * 64:O_GDEC + (h + 1) * 64] = np.exp(lg[h] * (C - 1 - i)).astype(f32)[:, None]
        c64[:, O_GQ + h * 64:O_GQ + (h + 1) * 64] = np.exp(lg[h] * (i + 1.0)).astype(f32)[None, :]
    pm = np.zeros((64, 64), f32)
    for d in range(32):
        pm[d + 32, d] = -1.0
        pm[d, d + 32] = 1.0
    c64[:, O_PERM:O_PERM + 64] = pm
    inv = (np.float32(10000.0) ** (-np.arange(32, dtype=f32) / np.float32(32))).astype(f32)
    c64[:, O_INVF] = np.concatenate([inv, inv])
    c64[:, O_ONES:O_ONES + 128] = 1.0
    lbp = np.asarray(inputs["hgrn_lower_bounds"]).reshape(4, 4, 128).transpose(2, 1, 0).reshape(128, 16)
    cwh = np.asarray(inputs["gdn_conv"])[:DEPTH].reshape(DEPTH, 4, 12, 128).transpose(3, 0, 2, 1).reshape(128, DEPTH * 48)
    return {"normw": nw, "c128": c128, "c64": c64,
            "lbp": np.ascontiguousarray(lbp, dtype=f32), "cw": np.ascontiguousarray(cwh, dtype=f32),
            "hnorm": np.ascontiguousarray(np.asarray(inputs["hgrn_norm"])[:DEPTH].reshape(1, -1), dtype=f32),
            "gnorm": np.ascontiguousarray(np.asarray(inputs["gdn_norm"])[:DEPTH].reshape(1, -1), dtype=f32),
            "alog": np.ascontiguousarray(np.asarray(inputs["gdn_a_log"])[:DEPTH].reshape(1, -1), dtype=f32),
            "dtb": np.ascontiguousarray(np.asarray(inputs["gdn_dt_bias"])[:DEPTH].reshape(1, -1), dtype=f32)}


def run(inputs, NT, DEPTH, seqs, ncores, enable=("A", "B", "C")):
    nc = build_program(NT, DEPTH, tuple(enable))
    consts = host_consts(inputs, DEPTH)
    in_maps = []
    S = NT * T
    zero_map = None
    for ci in range(ncores):
        b = seqs[ci]
        if b is None:
            if zero_map is None:
                zero_map = {"x": np.zeros((S, D), np.float32), "pos": np.zeros((1, S), np.int32)}
                for wn in WNAMES:
                    zero_map[wn] = np.zeros((DEPTH,) + WSHAPES[wn], np.float32)
                for kc, vc in consts.items():
                    zero_map[kc] = vc if kc in ("c128", "c64") else np.zeros_like(vc)
            in_maps.append(zero_map)
            continue
        m = {"x": np.ascontiguousarray(inputs["x"][b, :S]),
             "pos": np.ascontiguousarray(inputs["positions"][b, :S].reshape(1, S)).astype(np.int32)}
        for wn in WNAMES:
            m[wn] = np.ascontiguousarray(inputs[wn][:DEPTH])
        m.update(consts)
        in_maps.append(m)
    res = run_bass_kernel_spmd(nc, in_maps, core_ids=list(range(ncores)))
    return [r["y"] for r in res.results]


def kernel(**inputs):
    inputs = {k: np.asarray(v) for k, v in inputs.items()}
    outs = run(inputs, NT=16, DEPTH=4, seqs=[0, None, 1, None, 2, None, 3, None], ncores=8)
    return np.stack([outs[0], outs[2], outs[4], outs[6]], axis=0).astype(np.float32)
```
